# Optimizing a Trainium2 kernel written in Bass

```python
import jax, jax.numpy as jnp
from jax import lax
import numpy as np

D_MODEL = 1024
BATCH = 2
SEQ = 8192
DEPTH = 2

CONV_WIDTH = D_MODEL
CONV_K = 3
ATTN_GROUPS = ((128, 1), (512, 4), (2048, 16))
HEADS_PER_GROUP = 8
HEAD_DIM = 64
ATTN_BLK = 128
ATTN_OUT = HEADS_PER_GROUP * HEAD_DIM
ROPE_THETA = 10000.0
GMLP_WIDTH = D_MODEL
GMLP_GROUPS = 8
GMLP_GROUP_DIM = GMLP_WIDTH // GMLP_GROUPS
GMLP_CHUNK = 128
D_FF = ((-(-8 * D_MODEL // 3)) + 255) // 256 * 256
ALPHA = (2 * DEPTH) ** 0.25
BETA = (8 * DEPTH) ** -0.25
LN_EPS = 1e-5

GROUP_COL = HEADS_PER_GROUP * HEAD_DIM
SPLIT_SIZES = ([D_MODEL] * 3 + [CONV_WIDTH] * 3 + [GROUP_COL] * (3 * len(ATTN_GROUPS)) + [GMLP_WIDTH] * 2)
N_IN = sum(SPLIT_SIZES)

kernel_name = 'hybrid_conv_dilattn_gmlp_deepnorm'


def layer_norm(x, g, b):
    x32 = x.astype(jnp.float32)
    mu = jnp.mean(x32, axis=-1, keepdims=True)
    var = jnp.mean(jnp.square(x32 - mu), axis=-1, keepdims=True)
    y = (x32 - mu) * lax.rsqrt(var + LN_EPS) * g.astype(jnp.float32) + b.astype(jnp.float32)
    return y.astype(x.dtype)


def rope(t, positions):
    half = HEAD_DIM // 2
    inv_freq = ROPE_THETA ** (-jnp.arange(half, dtype=jnp.float32) / half)
    ang = positions.astype(jnp.float32)[..., None] * inv_freq
    cos = jnp.cos(ang)[:, :, None, :]
    sin = jnp.sin(ang)[:, :, None, :]
    t32 = t.astype(jnp.float32)
    t1, t2 = t32[..., :half], t32[..., half:]
    return jnp.concatenate([t1 * cos - t2 * sin, t2 * cos + t1 * sin], axis=-1).astype(t.dtype)


def short_conv_mixer(b_gate, c_gate, h, conv_w):
    z = c_gate * h
    conv = lax.conv_general_dilated(
        z, conv_w[:, None, :].astype(z.dtype), window_strides=(1,), padding=[(CONV_K - 1, 0)],
        dimension_numbers=('NWC', 'WIO', 'NWC'), feature_group_count=CONV_WIDTH)
    return b_gate * conv


def dilated_window_attention(q, k, v, window, dil):
    B, S, H, Dh = q.shape
    steps = window // dil
    span = dil * ATTN_BLK
    s_pad = -(-S // span) * span
    L = s_pad // dil
    nb = L // ATTN_BLK
    pad = ((0, 0), (0, s_pad - S), (0, 0), (0, 0))

    def fold(a):
        a = jnp.pad(a, pad).reshape(B, L, dil, H, Dh).transpose(0, 2, 1, 3, 4)
        return a.reshape(B, dil, nb, ATTN_BLK, H, Dh)

    qb, kb, vb = fold(q), fold(k), fold(v)
    blk_pad = ((0, 0), (0, 0), (1, 0), (0, 0), (0, 0), (0, 0))
    kw = jnp.concatenate([jnp.pad(kb, blk_pad)[:, :, :-1], kb], axis=3)
    vw = jnp.concatenate([jnp.pad(vb, blk_pad)[:, :, :-1], vb], axis=3)
    s = jnp.einsum('brnqhd,brnkhd->brnhqk', qb, kw).astype(jnp.float32) * (HEAD_DIM ** -0.5)
    qi = jnp.arange(ATTN_BLK)[:, None] + ATTN_BLK
    ki = jnp.arange(2 * ATTN_BLK)[None, :]
    dist = qi - ki
    band = (dist >= 0) & (dist <= steps)
    first = (jnp.arange(nb)[:, None, None] == 0) & (ki[None] < ATTN_BLK)
    valid = band[None] & jnp.logical_not(first)
    s = jnp.where(valid[None, None, :, None], s, -jnp.inf)
    lse = jax.nn.logsumexp(s, axis=-1)
    p = jnp.exp(s - lse[..., None]).astype(v.dtype)
    o = jnp.einsum('brnhqk,brnkhd->brnqhd', p, vw)
    o = o.reshape(B, dil, L, H, Dh).transpose(0, 2, 1, 3, 4).reshape(B, s_pad, H, Dh)[:, :S]
    lse = lse.transpose(0, 1, 2, 4, 3).reshape(B, dil, L, H).transpose(0, 2, 1, 3).reshape(B, s_pad, H)[:, :S]
    return o, lse


def dilated_attention_mixer(qkv_parts, positions):
    B, S, _ = qkv_parts[0].shape
    outs, lses = [], []
    for g, (window, dil) in enumerate(ATTN_GROUPS):
        q = rope(qkv_parts[3 * g].reshape(B, S, HEADS_PER_GROUP, HEAD_DIM), positions)
        k = rope(qkv_parts[3 * g + 1].reshape(B, S, HEADS_PER_GROUP, HEAD_DIM), positions)
        v = qkv_parts[3 * g + 2].reshape(B, S, HEADS_PER_GROUP, HEAD_DIM)
        o, lse = dilated_window_attention(q, k, v, window, dil)
        outs.append(o)
        lses.append(lse)
    wts = jax.nn.softmax(jnp.stack(lses, axis=0), axis=0)
    o = sum(wts[g][..., None].astype(outs[g].dtype) * outs[g] for g in range(len(ATTN_GROUPS)))
    return o.reshape(B, S, ATTN_OUT)


def chunked_spatial_gating(u_pre, v_pre, ln_g, ln_b, w_s, b_s):
    B, S, _ = u_pre.shape
    u = jax.nn.gelu(u_pre, approximate=False)
    v = layer_norm(jax.nn.gelu(v_pre, approximate=False), ln_g, ln_b)
    n = S // GMLP_CHUNK
    vc = v.reshape(B, n, GMLP_CHUNK, GMLP_GROUPS, GMLP_GROUP_DIM)
    tril = jnp.tril(jnp.ones((GMLP_CHUNK, GMLP_CHUNK), dtype=w_s.dtype))
    w_causal = w_s * tril[None]
    sp = jnp.einsum('gij,bnjgc->bnigc', w_causal, vc) + b_s.T[None, None, :, :, None]
    return u * sp.reshape(B, S, GMLP_WIDTH)


def setup_inputs(seed: int = 0) -> dict:
    key = jax.random.key(seed)
    ks = jax.random.split(key, 24)
    f32 = jnp.float32

    def nrm(k, shape, fan_in, scale=1.0):
        return jax.random.normal(k, shape, f32) * (scale * fan_in ** -0.5)

    col_scale = np.ones((N_IN,), np.float32)
    off = 3 * D_MODEL + 3 * CONV_WIDTH
    for g in range(len(ATTN_GROUPS)):
        vs = off + (3 * g + 2) * GROUP_COL
        col_scale[vs:vs + GROUP_COL] = BETA
    x = jax.random.normal(ks[0], (BATCH, SEQ, D_MODEL), f32)
    offset = jax.random.randint(ks[1], (BATCH, 1), 0, 4096, dtype=jnp.int32)
    positions = (offset + jnp.arange(SEQ, dtype=jnp.int32)[None, :]).astype(jnp.int32)
    return {
        'x': x,
        'positions': positions,
        'w_in': nrm(ks[2], (DEPTH, D_MODEL, N_IN), D_MODEL) * jnp.asarray(col_scale),
        'conv_w': nrm(ks[3], (DEPTH, CONV_K, CONV_WIDTH), CONV_K),
        'gmlp_ln_g': 1.0 + 0.02 * jax.random.normal(ks[4], (DEPTH, GMLP_WIDTH), f32),
        'gmlp_ln_b': 0.02 * jax.random.normal(ks[5], (DEPTH, GMLP_WIDTH), f32),
        'w_s': nrm(ks[6], (DEPTH, GMLP_GROUPS, GMLP_CHUNK, GMLP_CHUNK), GMLP_CHUNK),
        'b_s': 1.0 + 0.1 * jax.random.normal(ks[7], (DEPTH, GMLP_GROUPS, GMLP_CHUNK), f32),
        'p_a': nrm(ks[8], (DEPTH, CONV_WIDTH, D_MODEL), CONV_WIDTH, BETA),
        'p_b': nrm(ks[9], (DEPTH, ATTN_OUT, D_MODEL), ATTN_OUT, BETA),
        'p_c': nrm(ks[10], (DEPTH, GMLP_WIDTH, D_MODEL), GMLP_WIDTH, BETA),
        'w_o': nrm(ks[11], (DEPTH, D_MODEL, D_MODEL), D_MODEL, BETA),
        'ln1_g': 1.0 + 0.02 * jax.random.normal(ks[12], (DEPTH, D_MODEL), f32),
        'ln1_b': 0.02 * jax.random.normal(ks[13], (DEPTH, D_MODEL), f32),
        'w_gate': nrm(ks[14], (DEPTH, D_MODEL, D_FF), D_MODEL),
        'w_up': nrm(ks[15], (DEPTH, D_MODEL, D_FF), D_MODEL, BETA),
        'w_down': nrm(ks[16], (DEPTH, D_FF, D_MODEL), D_FF, BETA),
        'ln2_g': 1.0 + 0.02 * jax.random.normal(ks[17], (DEPTH, D_MODEL), f32),
        'ln2_b': 0.02 * jax.random.normal(ks[18], (DEPTH, D_MODEL), f32),
    }


def reference(x, positions, w_in, conv_w, gmlp_ln_g, gmlp_ln_b, w_s, b_s, p_a, p_b, p_c, w_o,
              ln1_g, ln1_b, w_gate, w_up, w_down, ln2_g, ln2_b):
    split_points = np.cumsum(np.array(SPLIT_SIZES))[:-1].tolist()
    n_attn = 3 * len(ATTN_GROUPS)
    for l in range(DEPTH):
        proj = x @ w_in[l].astype(x.dtype)
        parts = jnp.split(proj, split_points, axis=-1)
        g_a, g_b, g_c = (jax.nn.sigmoid(p) for p in parts[0:3])
        y_a = short_conv_mixer(parts[3], parts[4], parts[5], conv_w[l])
        y_b = dilated_attention_mixer(parts[6:6 + n_attn], positions)
        y_c = chunked_spatial_gating(parts[6 + n_attn], parts[7 + n_attn], gmlp_ln_g[l], gmlp_ln_b[l],
                                     w_s[l], b_s[l])
        m = g_a * (y_a @ p_a[l]) + g_b * (y_b @ p_b[l]) + g_c * (y_c @ p_c[l])
        x = layer_norm(ALPHA * x + m @ w_o[l], ln1_g[l], ln1_b[l])
        h = jax.nn.silu(x @ w_gate[l]) * (x @ w_up[l])
        x = layer_norm(ALPHA * x + h @ w_down[l], ln2_g[l], ln2_b[l])
    return x
```

```python
import math
from contextlib import ExitStack

import numpy as np
import concourse.bass as bass
import concourse.mybir as mybir
from concourse.bass_utils import run_bass_kernel_spmd

F32 = mybir.dt.float32
BF16 = mybir.dt.bfloat16
I32 = mybir.dt.int32
AF = mybir.ActivationFunctionType
ALU = mybir.AluOpType

D = 1024
NT = 8
T = 2048
SP = 512
NS = T // SP
DFF = 2816
NF = DFF // 128
DEPTH = 2
ALPHA = (2 * DEPTH) ** 0.25
EPS = 1e-5
N_IN = 12800
GA, GB, GC, CB, CC, CH = 0, 1024, 2048, 3072, 4096, 5120
ATT0 = 6144
UO, VO = 10752, 11776
DILS = (1, 4, 16)
MAGIC = 12582912.0
TWO_PI = 2.0 * math.pi
C1 = 6.28125
C2 = TWO_PI - C1


class Res:
    def __init__(self, name=""):
        self.name = name
        self.last_w = None
        self.reads = []


class Sched:
    ENG = ["pe", "act", "dve", "pool", "sp"]

    def __init__(self, nc, ndma=16):
        self.nc = nc
        self.ops = {e: [] for e in self.ENG}
        self.cnt = {e: 0 for e in self.ENG}
        self.known = {e: {} for e in self.ENG}
        self.ndma = ndma
        self.dma_cnt = [0] * ndma
        self.n_sp = 8
        self.rr_sp = 0
        self.rr_pool = 0
        self.opclock = 0
        self.on_op = None
        self.last_use = None

    def _need(self, eng, tok, waits):
        if tok is None:
            return
        kind, key, val = tok
        if kind == 'e' and key == eng and eng == 'pe':
            return
        k = (kind, key)
        if self.known[eng].get(k, 0) >= val:
            return
        self.known[eng][k] = val
        waits[k] = max(waits.get(k, 0), val)

    def _deps(self, eng, reads, writes, waits):
        for r in reads:
            self._need(eng, r.last_w, waits)
        for w in writes:
            self._need(eng, w.last_w, waits)
            for t in w.reads:
                self._need(eng, t, waits)

    def _commit(self, tok, reads, writes):
        for r in reads:
            r.reads.append(tok)
            if len(r.reads) > 64:
                r.reads = r.reads[-48:]
        for w in writes:
            w.last_w = tok
            w.reads = []

    def op(self, eng, fn, reads=(), writes=()):
        self.opclock += 1
        if self.last_use is not None:
            for r in reads:
                w = getattr(r, "widx", None)
                if w is not None:
                    self.last_use[w] = self.opclock
        if self.on_op is not None:
            self.on_op()
        waits = {}
        self._deps(eng, reads, writes, waits)
        self.cnt[eng] += 1
        tok = ('e', eng, self.cnt[eng])
        self.ops[eng].append((list(waits.items()), fn, ('e', eng, 1)))
        self._commit(tok, reads, writes)
        return tok

    def dma(self, fn, reads=(), writes=(), eng="sp"):
        if eng == "sp":
            ch = self.rr_sp
            self.rr_sp = (self.rr_sp + 1) % self.n_sp
        else:
            ch = self.n_sp + self.rr_pool
            self.rr_pool = (self.rr_pool + 1) % (self.ndma - self.n_sp)
        waits = {}
        if self.dma_cnt[ch] > 0:
            self._need(eng, ('d', ch, self.dma_cnt[ch] * 16), waits)
        self._deps(eng, reads, writes, waits)
        self.dma_cnt[ch] += 1
        tok = ('d', ch, self.dma_cnt[ch] * 16)
        self.ops[eng].append((list(waits.items()), fn, ('d', ch, 16)))
        self._commit(tok, reads, writes)
        return tok

    def barrier(self):
        toks = [('e', e2, self.cnt[e2]) for e2 in self.ENG if self.cnt[e2] > 0]
        toks += [('d', i, self.dma_cnt[i] * 16) for i in range(self.ndma) if self.dma_cnt[i] > 0]
        for eng in self.ENG:
            waits = {}
            for t in toks:
                if t[0] == 'e' and t[1] == eng:
                    continue
                self._need(eng, t, waits)
            self.ops[eng].append((list(waits.items()), None, None))

    def finish(self, eng="sp"):
        waits = {}
        for i in range(self.ndma):
            if self.dma_cnt[i] > 0:
                self._need(eng, ('d', i, self.dma_cnt[i] * 16), waits)
        self.ops[eng].append((list(waits.items()), None, None))

    def emit(self, stack):
        nc = self.nc
        esem = {e: stack.enter_context(nc.semaphore("s_" + e)) for e in self.ENG}
        dsem = [stack.enter_context(nc.semaphore("d_%d" % i)) for i in range(self.ndma)]

        def semof(k):
            return esem[k[1]] if k[0] == 'e' else dsem[k[1]]

        block = stack.enter_context(nc.Block())

        def run(engname):
            def body(e):
                for waits, fn, inc in self.ops[engname]:
                    for k, v in waits:
                        e.wait_ge(semof(k), v)
                    if fn is not None:
                        ins = fn(e)
                        ins.then_inc(semof(inc), inc[2])
            return body
        block.tensor(run("pe"))
        block.scalar(run("act"))
        block.vector(run("dve"))
        block.gpsimd(run("pool"))
        block.sync(run("sp"))


class Tile:
    def __init__(self, h, name):
        self.h = h
        self.r = Res(name)

    def __getitem__(self, k):
        return self.h[k]


class Ring:
    def __init__(self, tiles):
        self.tiles = tiles
        self.i = 0

    def next(self):
        t = self.tiles[self.i % len(self.tiles)]
        self.i += 1
        return t


def build_program(n_layers, kstop=99, plan=None):
    nc = bass.Bass("TRN2", target_bir_lowering=False)

    def din(name, shape, dt=F32):
        return nc.dram_tensor(name, list(shape), dt, kind="ExternalInput").ap()

    xT_d = din("xT", [D, T])
    xhT_d = din("xhT", [D, T])
    xh2T_d = din("xh2T", [D, T])
    pos_d = din("pos", [1, 3 * T], I32)
    flag_d = din("flag", [128, 1])
    flag2_d = din("flag2", [128, 1])
    ident_d = din("ident", [128, 128])
    cmask_d = din("cmask", [128, 1024])
    esel_d = din("esel", [128, 16])
    selb_d = din("selb", [4, 256])
    ones_d = din("ones", [128, 128])
    tril_d = din("tril", [128, 128])
    invf_d = din("invf", [128, 1])
    W = []
    for l in range(n_layers):
        W.append(dict(
            w_in=din("w_in%d" % l, [D, N_IN]),
            convw=din("convw%d" % l, [128, NT * 3]),
            glng=din("glng%d" % l, [1, D]),
            glnb=din("glnb%d" % l, [1, D]),
            wsT=din("wsT%d" % l, [8, 128, 128]),
            bs=din("bs%d" % l, [1, D]),
            p_a=din("p_a%d" % l, [D, D]),
            p_b=din("p_b%d" % l, [512, D]),
            p_c=din("p_c%d" % l, [D, D]),
            w_o=din("w_o%d" % l, [D, D]),
            ln1g=din("ln1g%d" % l, [128, NT]),
            ln1b=din("ln1b%d" % l, [128, NT]),
            w_gate=din("w_gate%d" % l, [D, DFF]),
            w_up=din("w_up%d" % l, [D, DFF]),
            w_down=din("w_down%d" % l, [DFF, D]),
            ln2g=din("ln2g%d" % l, [128, NT]),
            ln2b=din("ln2b%d" % l, [128, NT]),
        ))
    out_d = nc.dram_tensor("out", [D, T], F32, kind="ExternalOutput").ap()
    x1_d = nc.dram_tensor("x1_scratch", [D, T], F32).ap()
    x1h_d = nc.dram_tensor("x1h_scratch", [D, T], F32).ap()

    S = Sched(nc, ndma=24)

    with ExitStack() as top:
        def sb(st, name, shape, dt):
            return Tile(st.enter_context(nc.sbuf_tensor("sb_" + name, list(shape), dt)), name)

        def psum(st, name, shape, dt=F32):
            return Tile(st.enter_context(nc.psum_tensor("ps_" + name, list(shape), dt)), name)

        y_b = sb(top, "y_b", [128, 4, T], BF16)
        ident = sb(top, "ident", [128, 128], BF16)
        cmask = sb(top, "cmask", [128, 1024], BF16)
        esel = sb(top, "esel", [128, 16], BF16)
        eselF = sb(top, "eselF", [128, 16], BF16)
        selb = sb(top, "selb", [4, 256], F32)
        ones = sb(top, "ones", [128, 128], BF16)
        tril = sb(top, "tril", [128, 128], F32)
        invf = sb(top, "invf", [128, 1], F32)
        flag = sb(top, "flag", [128, 1], F32)
        flag2 = sb(top, "flag2", [128, 1], F32)
        eselF2 = sb(top, "eselF2", [128, 16], BF16)
        zhist = sb(top, "zhist", [128, NT, 2], F32)
        wring = Ring([sb(top, "wbuf%d" % i, [128, 4096], BF16) for i in range(5)])
        tmpr = Ring([sb(top, "tmp%d" % i, [128, 512], F32) for i in range(8)])

        def dmac(out, in_, writes, reads=()):
            return S.dma(lambda e, o=out, i=in_: e.dma_start(out=o, in_=i), reads=reads, writes=writes, eng="pool")

        def dmas(out, in_, writes=(), reads=()):
            return S.dma(lambda e, o=out, i=in_: e.dma_start(out=o, in_=i), reads=reads, writes=writes, eng="sp")

        NBUF = len(wring.tiles)
        PF = 2
        wst = {"idx": 0, "next": 0}
        WAP = {}
        for Wl_ in W:
            for ap_ in Wl_.values():
                WAP[ap_.tensor.name] = ap_
        if plan is None:
            S.last_use = {}
            dry_seq = []

        wcache = {}
        wcount = {}
        if plan is not None:
            for key_ in plan["seq"]:
                wcount[key_] = wcount.get(key_, 0) + 1

        def _issue(k):
            key = plan["seq"][k]
            (nm, c0, width, K) = key
            kt = K // 128
            t = wring.tiles[k % NBUF]
            flat = t.h[:, 0:kt * width]
            view = flat.rearrange("p (k c) -> p k c", k=kt)
            if wcount[key] < 3:
                dmac(view, WAP[nm][:, c0:c0 + width].rearrange("(k p) c -> p k c", p=128), writes=[t.r])
            elif key not in wcache:
                dmac(view, WAP[nm][:, c0:c0 + width].rearrange("(k p) c -> p k c", p=128), writes=[t.r])
                sc = nc.dram_tensor("wcache_%d" % len(wcache), [128, kt * width], BF16).ap()
                r = Res("wc")
                wcache[key] = (sc, r)
                dmas(sc, flat, writes=[r], reads=[t.r])
            else:
                sc, r = wcache[key]
                dmas(flat, sc, writes=[t.r], reads=[r])

        def _issue_upto(limit):
            limit = min(limit, len(plan["seq"]) - 1)
            while wst["next"] <= limit:
                k = wst["next"]
                if k >= NBUF and S.opclock < plan["last_use"].get(k - NBUF, 0):
                    break
                _issue(k)
                wst["next"] += 1

        if plan is not None:
            S.on_op = lambda: _issue_upto(wst["idx"] - 1 + PF)

        def wload(w_ap, c0, width, K):
            kt = K // 128
            key = (w_ap.tensor.name, c0, width, K)
            idx = wst["idx"]
            wst["idx"] += 1
            if plan is None:
                dry_seq.append(key)
                t = wring.next()
                view = t.h[:, 0:kt * width].rearrange("p (k c) -> p k c", k=kt)
                r = Res("w%d" % idx)
                r.widx = idx
                return view, r
            assert plan["seq"][idx] == key, (idx, key, plan["seq"][idx])
            _issue_upto(idx + PF)
            assert wst["next"] > idx, "weight ring too small: block %d not issuable" % idx
            t = wring.tiles[idx % NBUF]
            view = t.h[:, 0:kt * width].rearrange("p (k c) -> p k c", k=kt)
            return view, t.r

        def mmgroup(out_ap, pairs, reads, writes, tp=None):
            n = len(pairs)

            def fn(e, out_ap=out_ap, pairs=pairs, tp=tp):
                ins = None
                for i, (l, r) in enumerate(pairs):
                    if tp is None:
                        ins = e.matmul(out_ap, lhsT=l, rhs=r, start=(i == 0), stop=(i == n - 1))
                    else:
                        ins = e.matmul(out_ap, lhsT=l, rhs=r, start=(i == 0), stop=(i == n - 1), tile_position=tp)
                return ins
            return S.op("pe", fn, reads=reads, writes=writes)

        def tt(eng, out, in0, in1, op, reads, writes):
            return S.op(eng, lambda e: e.tensor_tensor(out=out, in0=in0, in1=in1, op=op), reads=reads, writes=writes)

        def tsc(eng, out, in0, s1, s2, op0, op1, reads, writes):
            if op1 is None:
                return S.op(eng, lambda e: e.tensor_scalar(out=out, in0=in0, scalar1=s1, scalar2=None, op0=op0), reads=reads, writes=writes)
            return S.op(eng, lambda e: e.tensor_scalar(out=out, in0=in0, scalar1=s1, scalar2=s2, op0=op0, op1=op1), reads=reads, writes=writes)

        def stt(eng, out, in0, scalar, in1, op0, op1, reads, writes):
            return S.op(eng, lambda e: e.scalar_tensor_tensor(out=out, in0=in0, scalar=scalar, in1=in1, op0=op0, op1=op1), reads=reads, writes=writes)

        def cpy(eng, out, in_, reads, writes):
            return S.op(eng, lambda e: e.tensor_copy(out=out, in_=in_), reads=reads, writes=writes)

        def rcp(out, in_, reads, writes):
            return S.op("dve", lambda e: e.reciprocal(out=out, in_=in_), reads=reads, writes=writes)

        def act(out, in_, func, reads, writes, **kw):
            return S.op("act", lambda e: e.activation(out=out, in_=in_, func=func, **kw), reads=reads, writes=writes)

        dmac(ident[:], ident_d, [ident.r])
        dmac(cmask[:], cmask_d, [cmask.r])
        dmac(esel[:], esel_d, [esel.r])
        dmac(ones[:], ones_d, [ones.r])
        dmas(selb[:], selb_d, [selb.r])
        dmas(tril[:], tril_d, [tril.r])
        dmas(invf[:], invf_d, [invf.r])
        dmas(flag[:], flag_d, [flag.r])
        dmas(flag2[:], flag2_d, [flag2.r])
        S.op("dve", lambda e: e.tensor_scalar(out=eselF2[:], in0=esel[:], scalar1=flag2[:, 0:1], scalar2=None, op0=ALU.mult),
             reads=[esel.r, flag2.r], writes=[eselF2.r])
        S.op("dve", lambda e: e.tensor_scalar(out=eselF[:], in0=esel[:], scalar1=flag[:, 0:1], scalar2=None, op0=ALU.mult),
             reads=[esel.r, flag.r], writes=[eselF.r])

        def rope_tables(p0, n, cos_ap, sin_ap, cos_r, sin_r, posi):
            posf, ta, tb = tmpr.next(), tmpr.next(), tmpr.next()
            dmas(posi[:, 0:n], pos_d[0:1, p0:p0 + n].partition_broadcast(128), writes=[posi.r])
            S.op("dve", lambda e: e.tensor_copy(out=posf[:, 0:n], in_=posi[:, 0:n]), reads=[posi.r], writes=[posf.r])
            S.op("dve", lambda e: e.tensor_scalar(out=posf[:, 0:n], in0=posf[:, 0:n], scalar1=invf[:, 0:1], scalar2=None, op0=ALU.mult),
                 reads=[posf.r, invf.r], writes=[posf.r])
            S.op("dve", lambda e: e.tensor_scalar(out=ta[:, 0:n], in0=posf[:, 0:n], scalar1=1.0 / TWO_PI, scalar2=MAGIC, op0=ALU.mult, op1=ALU.add),
                 reads=[posf.r], writes=[ta.r])
            S.op("dve", lambda e: e.tensor_scalar(out=ta[:, 0:n], in0=ta[:, 0:n], scalar1=MAGIC, scalar2=None, op0=ALU.subtract),
                 reads=[ta.r], writes=[ta.r])
            S.op("dve", lambda e: e.scalar_tensor_tensor(out=posf[:, 0:n], in0=ta[:, 0:n], scalar=-C1, in1=posf[:, 0:n], op0=ALU.mult, op1=ALU.add),
                 reads=[ta.r, posf.r], writes=[posf.r])
            S.op("dve", lambda e: e.scalar_tensor_tensor(out=posf[:, 0:n], in0=ta[:, 0:n], scalar=-C2, in1=posf[:, 0:n], op0=ALU.mult, op1=ALU.add),
                 reads=[ta.r, posf.r], writes=[posf.r])
            S.op("dve", lambda e: e.tensor_scalar(out=posf[:, 0:n], in0=posf[:, 0:n], scalar1=-3.1415925, scalar2=3.1415925, op0=ALU.max, op1=ALU.min),
                 reads=[posf.r], writes=[posf.r])
            act(sin_ap, posf[:, 0:n], AF.Sin, [posf.r], [sin_r])
            S.op("dve", lambda e: e.scalar_tensor_tensor(out=tb[:, 0:n], in0=posf[:, 0:n], scalar=-1.0, in1=posf[:, 0:n], op0=ALU.mult, op1=ALU.max),
                 reads=[posf.r], writes=[tb.r])
            S.op("dve", lambda e: e.tensor_scalar(out=tb[:, 0:n], in0=tb[:, 0:n], scalar1=-1.0, scalar2=math.pi / 2, op0=ALU.mult, op1=ALU.add),
                 reads=[tb.r], writes=[tb.r])
            act(cos_ap, tb[:, 0:n], AF.Sin, [tb.r], [cos_r])

        xin_r = Res("xin")
        x1_r = Res("x1")
        x1h_r = Res("x1h")
        if n_layers == 1:
            passes = [(0, xT_d, xhT_d, T, flag, eselF, out_d, xin_r, xin_r, None)]
        else:
            passes = [
                (0, xhT_d, xh2T_d, 0, flag2, eselF2, x1h_d, xin_r, xin_r, x1h_r),
                (0, xT_d, xhT_d, T, flag, eselF, x1_d, xin_r, xin_r, x1_r),
                (1, x1_d, x1h_d, T, flag, eselF, out_d, x1_r, x1h_r, None),
            ]
        for pi, (l, x_src, xh_src, pos0, flag_t, eselF_t, out_ap, xsrc_r, xhsrc_r, out_r) in enumerate(passes):
            Wl = W[l]
            w_in = Wl["w_in"]
            with ExitStack() as pa:
                bring = Ring([psum(pa, "bankA%d_%d" % (i, pi), [128, 512]) for i in range(7)])
                bank_t = psum(pa, "bank_t_%d" % pi, [128, 1024], BF16)
                cos_o = sb(pa, "cos_o" + "_%d" % pi, [128, T], F32)
                sin_o = sb(pa, "sin_o" + "_%d" % pi, [128, T], F32)
                cos_h = sb(pa, "cos_h" + "_%d" % pi, [128, SP], F32)
                sin_h = sb(pa, "sin_h" + "_%d" % pi, [128, SP], F32)
                posi = sb(pa, "posi" + "_%d" % pi, [128, SP], I32)
                acc = sb(pa, "acc" + "_%d" % pi, [128, 2, T], F32)
                accd = sb(pa, "accd" + "_%d" % pi, [4, T], F32)
                KT = sb(pa, "KT" + "_%d" % pi, [128, 2, 2 * T], BF16)
                VT = sb(pa, "VT" + "_%d" % pi, [128, 2, 2 * T], BF16)
                QT = sb(pa, "QT" + "_%d" % pi, [128, 2, T], BF16)
                vbr = Ring([sb(pa, "vb%d_%d" % (i, pi), [128, 256], BF16) for i in range(6)])
                P_t = Ring([sb(pa, "P%d_%d" % (i, pi), [128, 1024], BF16) for i in range(3)])
                xcr = Ring([sb(pa, "xca%d_%d" % (i, pi), [128, NT, SP], BF16) for i in range(2)])
                for s in range(NS):
                    rope_tables(pos0 + T + s * SP, SP, cos_o[:, s * SP:(s + 1) * SP], sin_o[:, s * SP:(s + 1) * SP], cos_o.r, sin_o.r, posi)

                for qd in range(2):
                    if kstop <= 1:
                        break
                    S.op("pool", lambda e, acc=acc: e.memset(acc[:], 0.0), writes=[acc.r])
                    S.op("pool", lambda e, accd=accd: e.memset(accd[:], 0.0), writes=[accd.r])
                    for g, dil in enumerate(DILS):
                        HL = 128 * dil
                        base = ATT0 + g * 1536
                        wq_v, wq_r = wload(w_in, base + qd * 256, 256, D)
                        wk_v, wk_r = wload(w_in, base + 512 + qd * 256, 256, D)
                        wv_v, wv_r = wload(w_in, base + 1024 + qd * 256, 256, D)
                        chunks = []
                        hs0 = T - HL
                        n_h = min(HL, SP)
                        for a in range(hs0, T, n_h):
                            chunks.append((True, a, n_h, a - hs0))
                        for s in range(NS):
                            chunks.append((False, s * SP, SP, HL + s * SP))
                        loaded = {}

                        def issue_chunk(ci, chunks=chunks, loaded=loaded):
                            (is_h_, t0_, n_, c0_) = chunks[ci]
                            xc_ = xcr.next()
                            src_ = xh_src if is_h_ else x_src
                            dmac(xc_[:, :, 0:n_], src_[:, t0_:t0_ + n_].rearrange("(k p) t -> p k t", p=128), [xc_.r], reads=[xhsrc_r if is_h_ else xsrc_r])
                            loaded[ci] = xc_
                        issue_chunk(0)
                        for ci, (is_h, t0, n, c0) in enumerate(chunks):
                            if ci + 1 < len(chunks):
                                issue_chunk(ci + 1)
                            xc = loaded[ci]
                            if is_h:
                                rope_tables(pos0 + t0, n, cos_h[:, 0:n], sin_h[:, 0:n], cos_h.r, sin_h.r, posi)
                                cs_ap, sn_ap, cs_r, sn_r = cos_h[:, 0:n], sin_h[:, 0:n], cos_h.r, sin_h.r
                            else:
                                cs_ap, sn_ap, cs_r, sn_r = cos_o[:, t0:t0 + n], sin_o[:, t0:t0 + n], cos_o.r, sin_o.r
                            todo = [(wk_v, wk_r, KT, c0)]
                            if not is_h:
                                todo.append((wq_v, wq_r, QT, t0))
                            for (wt, wr, dst, dc0) in todo:
                                pA = bring.next()
                                pB = bring.next()
                                mmgroup(pA[:, 0:n], [(wt[:, k, 0:128], xc[:, k, 0:n]) for k in range(NT)], [wr, xc.r], [pA.r])
                                mmgroup(pB[:, 0:n], [(wt[:, k, 128:256], xc[:, k, 0:n]) for k in range(NT)], [wr, xc.r], [pB.r])
                                t1, t2, t3, t4 = tmpr.next(), tmpr.next(), tmpr.next(), tmpr.next()
                                tt("dve", t1[:, 0:n], pA[:, 0:n], cs_ap, ALU.mult, [pA.r, cs_r], [t1.r])
                                tt("dve", t2[:, 0:n], pB[:, 0:n], sn_ap, ALU.mult, [pB.r, sn_r], [t2.r])
                                tt("pool", dst[:, 0, dc0:dc0 + n], t1[:, 0:n], t2[:, 0:n], ALU.subtract, [t1.r, t2.r], [dst.r])
                                tt("dve", t3[:, 0:n], pB[:, 0:n], cs_ap, ALU.mult, [pB.r, cs_r], [t3.r])
                                tt("dve", t4[:, 0:n], pA[:, 0:n], sn_ap, ALU.mult, [pA.r, sn_r], [t4.r])
                                tt("pool", dst[:, 1, dc0:dc0 + n], t3[:, 0:n], t4[:, 0:n], ALU.add, [t3.r, t4.r], [dst.r])
                            for vt in range(2):
                                pV = bring.next()
                                mmgroup(pV[:, 0:n], [(wv_v[:, k, vt * 128:(vt + 1) * 128], xc[:, k, 0:n]) for k in range(NT)], [wv_r, xc.r], [pV.r])
                                act(VT[:, vt, c0:c0 + n], pV[:, 0:n], AF.Copy, [pV.r], [VT.r])
                        if kstop <= 2:
                            break
                        nb = T // (128 * dil)
                        Wd = HL + T

                        def kview(tl, lo, hi, ab, m, r, dil=dil, Wd=Wd):
                            return tl.h[lo:hi, ab, 0:Wd].rearrange("p (m i r) -> p m r i", i=128, r=dil)[:, m, r, :]

                        def qview(tl, lo, hi, ab, m, r, dil=dil):
                            return tl.h[lo:hi, ab, 0:T].rearrange("p (m i r) -> p m r i", i=128, r=dil)[:, m, r, :]

                        pending = [None]
                        for r in range(dil):
                            vprev = None
                            for m in range(nb + 1):
                                vb = vbr.next()

                                v0_ap = kview(VT, 0, 128, 0, m, r)
                                v1_ap = kview(VT, 0, 128, 1, m, r)

                                def tr2(e, v0_ap=v0_ap, v1_ap=v1_ap, o0=bank_t[:, 0:128], o1=bank_t[:, 128:256], idn=ident[:]):
                                    e.transpose(out=o0, in_=v0_ap, identity=idn)
                                    return e.transpose(out=o1, in_=v1_ap, identity=idn)
                                S.op("pe", tr2, reads=[VT.r, ident.r], writes=[bank_t.r])
                                if m == 0:
                                    act(vb[:], bank_t[:, 0:256], AF.Copy, [bank_t.r, flag_t.r], [vb.r], scale=flag_t[:, 0:1])
                                    vprev = vb
                                    continue
                                act(vb[:], bank_t[:, 0:256], AF.Copy, [bank_t.r], [vb.r])
                                n_q = m - 1
                                sbks = [bring.next() for _ in range(4)]
                                for hs in range(4):
                                    sbk = sbks[hs]
                                    lo, hi = 32 * hs, 32 * hs + 32
                                    for kt in range(2):
                                        o0 = kt * 128
                                        mmgroup(sbk[:, o0:o0 + 128],
                                                [(kview(KT, lo, hi, 0, n_q + kt, r), qview(QT, lo, hi, 0, n_q, r)),
                                                 (kview(KT, lo, hi, 1, n_q + kt, r), qview(QT, lo, hi, 1, n_q, r))],
                                                [KT.r, QT.r], [sbk.r], tp=(32 * hs, 0))
                                P = P_t.next()
                                for hs in range(4):
                                    act(P[:, hs * 256:(hs + 1) * 256], sbks[hs][:, 0:256], AF.Exp, [sbks[hs].r], [P.r], scale=0.125)
                                tt("pool", P[:], P[:], cmask[:], ALU.mult, [P.r, cmask.r], [P.r])
                                def pv_stage(vprev=vprev, vb=vb, P=P, m=m, n_q=n_q, r=r, dil=dil, acc=acc, accd=accd):
                                    nd = bring.next()
                                    vbs = (vprev, vb)
                                    for pr in range(2):
                                        for hh in range(2):
                                            hs = 2 * pr + hh
                                            mmgroup(nd[64 * hh:64 * hh + 64, pr * 128:(pr + 1) * 128],
                                                    [(vbs[kt][:, hs * 64:(hs + 1) * 64], P[:, hs * 256 + kt * 128: hs * 256 + kt * 128 + 128]) for kt in range(2)],
                                                    [vprev.r, vb.r, P.r], [nd.r], tp=(0, 64 * hh))
                                    es_prev = eselF_t if m == 1 else esel
                                    pairs = []
                                    for hs in range(4):
                                        pairs.append((es_prev[:, hs * 4:(hs + 1) * 4], P[:, hs * 256: hs * 256 + 128]))
                                        pairs.append((esel[:, hs * 4:(hs + 1) * 4], P[:, hs * 256 + 128: hs * 256 + 256]))
                                    mmgroup(nd[0:4, 256:384], pairs, [P.r, esel.r, eselF_t.r], [nd.r])
                                    accv = acc.h[:, :, :].rearrange("p a (m i r) -> p a m r i", i=128, r=dil)[:, :, n_q, r, :]
                                    tt("dve", accv, accv, nd[:, 0:256].rearrange("p (a i) -> p a i", a=2), ALU.add, [nd.r, acc.r], [acc.r])
                                    adv = accd.h[:, :].rearrange("p (m i r) -> p m r i", i=128, r=dil)[:, n_q, r, :]
                                    tt("dve", adv, adv, nd[0:4, 256:384], ALU.add, [nd.r, accd.r], [accd.r])
                                if pending[0] is not None:
                                    pending[0]()
                                pending[0] = pv_stage
                                vprev = vb
                        if pending[0] is not None:
                            pending[0]()
                            pending[0] = None
                    if kstop <= 3:
                        break
                    rcp(accd[:], accd[:], [accd.r], [accd.r])
                    if kstop <= 4:
                        break
                    for pr in range(2):
                        for s in range(NS):
                            bc = bring.next()
                            mmgroup(bc[:, :], [(selb[:, pr * 128:(pr + 1) * 128], accd[:, s * SP:(s + 1) * SP])], [selb.r, accd.r], [bc.r])
                            tt("dve", y_b[:, 2 * qd + pr, s * SP:(s + 1) * SP], acc[:, pr, s * SP:(s + 1) * SP], bc[:, :], ALU.mult,
                               [bc.r, acc.r], [y_b.r])

                S.barrier()
            if kstop <= 5:
                break
            with ExitStack() as pb:
                bring = Ring([psum(pb, "bankB%d_%d" % (i, pi), [128, 512]) for i in range(6)])
                stat_banks = [psum(pb, "bankS%d_%d" % (i, pi), [128, 512]) for i in range(2)]
                xs = sb(pb, "xs" + "_%d" % pi, [128, NT, SP], F32)
                xc = sb(pb, "xcb" + "_%d" % pi, [128, NT, SP], BF16)
                vfm = sb(pb, "vfm" + "_%d" % pi, [128, 4 * D], F32)
                u_sb = sb(pb, "u_sb" + "_%d" % pi, [128, NT, SP], BF16)
                y_a = sb(pb, "y_a" + "_%d" % pi, [128, NT, SP], BF16)
                y_c = sb(pb, "y_c" + "_%d" % pi, [128, NT, SP], BF16)
                vbf = [sb(pb, "vbf%d_%d" % (i, pi), [128, D], BF16) for i in range(4)]
                m_sb = sb(pb, "m_sb" + "_%d" % pi, [128, NT, SP], BF16)
                h_sb = sb(pb, "h_sb" + "_%d" % pi, [128, NF, SP], BF16)
                zbuf = sb(pb, "zbuf" + "_%d" % pi, [128, SP + 2], F32)
                convw = sb(pb, "convw" + "_%d" % pi, [128, NT * 3], F32)
                glng = sb(pb, "glng" + "_%d" % pi, [128, D], F32)
                glnb = sb(pb, "glnb" + "_%d" % pi, [128, D], F32)
                bsb = sb(pb, "bsb" + "_%d" % pi, [128, D], F32)
                wsf = sb(pb, "wsf" + "_%d" % pi, [128, 8, 128], F32)
                wsm = sb(pb, "wsm" + "_%d" % pi, [128, 8, 128], BF16)
                ln1g = sb(pb, "ln1g" + "_%d" % pi, [128, NT], F32)
                ln1b = sb(pb, "ln1b" + "_%d" % pi, [128, NT], F32)
                ln2g = sb(pb, "ln2g" + "_%d" % pi, [128, NT], F32)
                ln2b = sb(pb, "ln2b" + "_%d" % pi, [128, NT], F32)
                st6 = sb(pb, "st6" + "_%d" % pi, [128, 2, 6], F32)
                mv = sb(pb, "mv" + "_%d" % pi, [128, 2], F32)
                rstd1 = sb(pb, "rstd1" + "_%d" % pi, [128, 1], F32)
                xh16 = sb(pb, "xh16" + "_%d" % pi, [128, NT, 16], BF16)
                mean_t = sb(pb, "mean_t" + "_%d" % pi, [128, SP], F32)
                rstd_t = sb(pb, "rstd_t" + "_%d" % pi, [128, SP], F32)

                dmas(convw[:], Wl["convw"], [convw.r])
                dmas(glng[:], Wl["glng"].partition_broadcast(128), [glng.r])
                dmas(glnb[:], Wl["glnb"].partition_broadcast(128), [glnb.r])
                dmas(bsb[:], Wl["bs"].partition_broadcast(128), [bsb.r])
                dmas(wsf[:], Wl["wsT"].rearrange("g j i -> j g i"), [wsf.r])
                dmas(ln1g[:], Wl["ln1g"], [ln1g.r])
                dmas(ln1b[:], Wl["ln1b"], [ln1b.r])
                dmas(ln2g[:], Wl["ln2g"], [ln2g.r])
                dmas(ln2b[:], Wl["ln2b"], [ln2b.r])
                tt("dve", wsm[:], wsf[:], tril[:].unsqueeze(1).to_broadcast([128, 8, 128]), ALU.mult, [wsf.r, tril.r], [wsm.r])
                dmac(xh16[:], xh_src[:, T - 16:T].rearrange("(k p) t -> p k t", p=128), [xh16.r], reads=[xhsrc_r])
                for jb in range(2):
                    wC, wCr = wload(w_in, CC + jb * 512, 512, D)
                    wH, wHr = wload(w_in, CH + jb * 512, 512, D)
                    for jj in range(4):
                        j = jb * 4 + jj
                        pC = bring.next()
                        pH = bring.next()
                        mmgroup(pC[:, 0:2], [(wC[:, k, jj * 128:(jj + 1) * 128], xh16[:, k, 14:16]) for k in range(NT)], [wCr, xh16.r], [pC.r])
                        mmgroup(pH[:, 0:2], [(wH[:, k, jj * 128:(jj + 1) * 128], xh16[:, k, 14:16]) for k in range(NT)], [wHr, xh16.r], [pH.r])
                        tz = tmpr.next()
                        act(tz[:, 0:2], pC[:, 0:2], AF.Copy, [pC.r, flag_t.r], [tz.r], scale=flag_t[:, 0:1])
                        tt("dve", zhist[:, j, :], tz[:, 0:2], pH[:, 0:2], ALU.mult, [tz.r, pH.r], [zhist.r])

                class LNStats:
                    def __init__(self):
                        self.rb, self.rsq = y_a, y_c
                        self.pm = stat_banks[0]
                        self.pq = stat_banks[1]
                        self.lag = None

                    def _mm(self, j):
                        rb, rsq, pm, pq = self.rb, self.rsq, self.pm, self.pq
                        S.op("pe", lambda e: e.matmul(pm[:, :], lhsT=ones[:], rhs=rb[:, j, :], start=(j == 0), stop=(j == NT - 1)),
                             reads=[ones.r, rb.r], writes=[pm.r])
                        S.op("pe", lambda e: e.matmul(pq[:, :], lhsT=ones[:], rhs=rsq[:, j, :], start=(j == 0), stop=(j == NT - 1)),
                             reads=[ones.r, rsq.r], writes=[pq.r])

                    def tile_done(self, j):
                        act(self.rb[:, j, :], xs[:, j, :], AF.Copy, [xs.r], [self.rb.r])
                        act(self.rsq[:, j, :], xs[:, j, :], AF.Square, [xs.r], [self.rsq.r])
                        if self.lag is not None:
                            self._mm(self.lag)
                        self.lag = j

                    def finish(self, g_t, b_t):
                        self._mm(self.lag)
                        pm, pq = self.pm, self.pq
                        msq = tmpr.next()
                        nmr = tmpr.next()
                        act(mean_t[:], pm[:, :], AF.Copy, [pm.r], [mean_t.r], scale=1.0 / D)
                        tt("dve", msq[:], mean_t[:], mean_t[:], ALU.mult, [mean_t.r], [msq.r])
                        stt("dve", msq[:], pq[:, :], 1.0 / D, msq[:], ALU.mult, ALU.subtract, [pq.r, msq.r], [msq.r])
                        tsc("dve", msq[:], msq[:], EPS, None, ALU.add, None, [msq.r], [msq.r])
                        act(msq[:], msq[:], AF.Sqrt, [msq.r], [msq.r])
                        rcp(rstd_t[:], msq[:], [msq.r], [rstd_t.r])
                        for j in range(NT):
                            t = tmpr.next()
                            tt("dve", t[:], xs[:, j, :], mean_t[:], ALU.subtract, [xs.r, mean_t.r], [t.r])
                            tt("dve", t[:], t[:], rstd_t[:], ALU.mult, [t.r, rstd_t.r], [t.r])
                            act(xs[:, j, :], t[:], AF.Identity, [t.r, g_t.r, b_t.r], [xs.r], scale=g_t[:, j:j + 1], bias=b_t[:, j:j + 1])
                            act(xc[:, j, :], t[:], AF.Identity, [t.r, g_t.r, b_t.r], [xc.r], scale=g_t[:, j:j + 1], bias=b_t[:, j:j + 1])

                mf = lambda j: vfm[:, j * SP:(j + 1) * SP]
                vf = lambda t_: vfm[:, t_ * D:(t_ + 1) * D]

                for s in range(NS):
                    c0 = s * SP
                    dmac(xs[:, :, :], x_src[:, c0:c0 + SP].rearrange("(k p) t -> p k t", p=128), [xs.r], reads=[xsrc_r])
                    act(xc[:, :, :], xs[:, :, :], AF.Copy, [xs.r], [xc.r])
                    for jb in range(2):
                        wB, wBr = wload(w_in, CB + jb * 512, 512, D)
                        wC, wCr = wload(w_in, CC + jb * 512, 512, D)
                        wH, wHr = wload(w_in, CH + jb * 512, 512, D)
                        for jj in range(4):
                            j = jb * 4 + jj
                            pB_, pC, pH = bring.next(), bring.next(), bring.next()
                            mmgroup(pC[:, :], [(wC[:, k, jj * 128:(jj + 1) * 128], xc[:, k, :]) for k in range(NT)], [wCr, xc.r], [pC.r])
                            mmgroup(pH[:, :], [(wH[:, k, jj * 128:(jj + 1) * 128], xc[:, k, :]) for k in range(NT)], [wHr, xc.r], [pH.r])
                            mmgroup(pB_[:, :], [(wB[:, k, jj * 128:(jj + 1) * 128], xc[:, k, :]) for k in range(NT)], [wBr, xc.r], [pB_.r])
                            tc_ = tmpr.next()
                            ta_ = tmpr.next()
                            act(tc_[:], pC[:, :], AF.Copy, [pC.r], [tc_.r])
                            act(zbuf[:, 0:2], zhist[:, j, :], AF.Copy, [zhist.r], [zbuf.r])
                            tt("dve", zbuf[:, 2:SP + 2], tc_[:], pH[:, :], ALU.mult, [tc_.r, pH.r], [zbuf.r])
                            act(zhist[:, j, :], zbuf[:, SP:SP + 2], AF.Copy, [zbuf.r], [zhist.r])
                            tsc("dve", ta_[:], zbuf[:, 0:SP], convw[:, 3 * j:3 * j + 1], None, ALU.mult, None, [zbuf.r, convw.r], [ta_.r])
                            stt("dve", ta_[:], zbuf[:, 1:SP + 1], convw[:, 3 * j + 1:3 * j + 2], ta_[:], ALU.mult, ALU.add, [zbuf.r, convw.r, ta_.r], [ta_.r])
                            stt("dve", ta_[:], zbuf[:, 2:SP + 2], convw[:, 3 * j + 2:3 * j + 3], ta_[:], ALU.mult, ALU.add, [zbuf.r, convw.r, ta_.r], [ta_.r])
                            tt("dve", y_a[:, j, :], ta_[:], pB_[:, :], ALU.mult, [ta_.r, pB_.r], [y_a.r])
                    for jb in range(2):
                        wU, wUr = wload(w_in, UO + jb * 512, 512, D)
                        for jj in range(4):
                            j = jb * 4 + jj
                            pU = bring.next()
                            mmgroup(pU[:, :], [(wU[:, k, jj * 128:(jj + 1) * 128], xc[:, k, :]) for k in range(NT)], [wUr, xc.r], [pU.r])
                            act(u_sb[:, j, :], pU[:, :], AF.Gelu, [pU.r], [u_sb.r])
                    for half in range(2):
                        wV, wVr = wload(w_in, VO + half * 512, 512, D)
                        for t_ in range(4):
                            pV = bring.next()
                            mmgroup(pV[:, :], [(xc[:, k, t_ * 128:(t_ + 1) * 128], wV[:, k, :]) for k in range(NT)], [wVr, xc.r], [pV.r])
                            act(vfm[:, t_ * D + half * 512: t_ * D + (half + 1) * 512], pV[:, :], AF.Gelu, [pV.r], [vfm.r])
                    for t_ in range(4):
                        v = vf(t_)
                        S.op("dve", lambda e, o_=st6[:, 0, :], i_=v[:, 0:512]: e.bn_stats(out=o_, in_=i_), reads=[vfm.r], writes=[st6.r])
                        S.op("dve", lambda e, o_=st6[:, 1, :], i_=v[:, 512:1024]: e.bn_stats(out=o_, in_=i_), reads=[vfm.r, st6.r], writes=[st6.r])
                        S.op("dve", lambda e, o_=mv[:], i_=st6[:].rearrange("p a b -> p (a b)"): e.bn_aggr(out=o_, in_=i_), reads=[st6.r], writes=[mv.r])
                        tsc("dve", rstd1[:], mv[:, 1:2], EPS, None, ALU.add, None, [mv.r], [rstd1.r])
                        act(rstd1[:], rstd1[:], AF.Sqrt, [rstd1.r], [rstd1.r])
                        rcp(rstd1[:], rstd1[:], [rstd1.r], [rstd1.r])
                        tsc("dve", v, v, mv[:, 0:1], rstd1[:, 0:1], ALU.subtract, ALU.mult, [vfm.r, mv.r, rstd1.r], [vfm.r])
                        tt("dve", v, v, glng[:], ALU.mult, [vfm.r, glng.r], [vfm.r])
                        tt("dve", vbf[t_][:], v, glnb[:], ALU.add, [vfm.r, glnb.r], [vbf[t_].r])
                    def spatial_stage():
                        for gg in range(8):
                            pS = bring.next()
                            for t_ in range(4):
                                mmgroup(pS[:, t_ * 128:(t_ + 1) * 128], [(vbf[t_][:, gg * 128:(gg + 1) * 128], wsm[:, gg, :])], [vbf[t_].r, wsm.r], [pS.r])
                            tq = tmpr.next()
                            tt("dve", tq[:].rearrange("p (c i) -> p c i", c=4), pS[:, :].rearrange("p (c i) -> p c i", c=4),
                               bsb[:, gg * 128:(gg + 1) * 128].unsqueeze(1).to_broadcast([128, 4, 128]), ALU.add, [pS.r, bsb.r], [tq.r])
                            tt("dve", y_c[:, gg, :], tq[:], u_sb[:, gg, :], ALU.mult, [tq.r, u_sb.r], [y_c.r])
                    for bi, (gcol, p_ap, K, y_t) in enumerate(((GA, Wl["p_a"], D, y_a), (GB, Wl["p_b"], 512, y_b), (GC, Wl["p_c"], D, y_c))):
                        kt = K // 128
                        if bi == 2:
                            spatial_stage()
                        for jb in range(2):
                            wG_, wGr_ = wload(w_in, gcol + jb * 512, 512, D)
                            wP_, wPr_ = wload(p_ap, jb * 512, 512, K)
                            for jj in range(4):
                                j = jb * 4 + jj
                                pg, py = bring.next(), bring.next()
                                mmgroup(pg[:, :], [(wG_[:, k, jj * 128:(jj + 1) * 128], xc[:, k, :]) for k in range(NT)], [wGr_, xc.r], [pg.r])
                                if bi == 1:
                                    prs = [(wP_[:, k, jj * 128:(jj + 1) * 128], y_b[:, k, c0:c0 + SP]) for k in range(kt)]
                                else:
                                    prs = [(wP_[:, k, jj * 128:(jj + 1) * 128], y_t[:, k, :]) for k in range(kt)]
                                mmgroup(py[:, :], prs, [wPr_, y_t.r], [py.r])
                                sg = tmpr.next()
                                act(sg[:], pg[:, :], AF.Sigmoid, [pg.r], [sg.r])
                                if bi == 0:
                                    tt("dve", mf(j), sg[:], py[:, :], ALU.mult, [sg.r, py.r], [vfm.r])
                                elif bi == 1:
                                    tt("dve", sg[:], sg[:], py[:, :], ALU.mult, [sg.r, py.r], [sg.r])
                                    tt("dve", mf(j), mf(j), sg[:], ALU.add, [sg.r, vfm.r], [vfm.r])
                                else:
                                    tt("dve", sg[:], sg[:], py[:, :], ALU.mult, [sg.r, py.r], [sg.r])
                                    tt("dve", m_sb[:, j, :], mf(j), sg[:], ALU.add, [sg.r, vfm.r], [m_sb.r])
                    lns = LNStats()
                    for jb in range(2):
                        wO, wOr = wload(Wl["w_o"], jb * 512, 512, D)
                        for jj in range(4):
                            j = jb * 4 + jj
                            po = bring.next()
                            mmgroup(po[:, :], [(wO[:, k, jj * 128:(jj + 1) * 128], m_sb[:, k, :]) for k in range(NT)], [wOr, m_sb.r], [po.r])
                            stt("dve", xs[:, j, :], xs[:, j, :], ALPHA, po[:, :], ALU.mult, ALU.add, [po.r, xs.r], [xs.r])
                            lns.tile_done(j)
                    lns.finish(ln1g, ln1b)
                    for fb in range(NF // 2):
                        wG_, wGr_ = wload(Wl["w_gate"], fb * 256, 256, D)
                        wU_, wUr_ = wload(Wl["w_up"], fb * 256, 256, D)
                        for ff in range(2):
                            f = fb * 2 + ff
                            pg, pu = bring.next(), bring.next()
                            mmgroup(pg[:, :], [(wG_[:, k, ff * 128:(ff + 1) * 128], xc[:, k, :]) for k in range(NT)], [wGr_, xc.r], [pg.r])
                            mmgroup(pu[:, :], [(wU_[:, k, ff * 128:(ff + 1) * 128], xc[:, k, :]) for k in range(NT)], [wUr_, xc.r], [pu.r])
                            sg = tmpr.next()
                            act(sg[:], pg[:, :], AF.Silu, [pg.r], [sg.r])
                            tt("dve", h_sb[:, f, :], sg[:], pu[:, :], ALU.mult, [sg.r, pu.r], [h_sb.r])
                    lns = LNStats()
                    for j in range(NT):
                        wD, wDr = wload(Wl["w_down"], j * 128, 128, DFF)
                        pd = bring.next()
                        mmgroup(pd[:, :], [(wD[:, k, :], h_sb[:, k, :]) for k in range(NF)], [wDr, h_sb.r], [pd.r])
                        stt("dve", xs[:, j, :], xs[:, j, :], ALPHA, pd[:, :], ALU.mult, ALU.add, [pd.r, xs.r], [xs.r])
                        lns.tile_done(j)
                    lns.finish(ln2g, ln2b)
                    dmac(out_ap[:, c0:c0 + SP].rearrange("(k p) t -> p k t", p=128), xs[:, :, :], reads=[xs.r], writes=([out_r] if out_r is not None else []))
                S.barrier()
        if plan is None:
            return {"seq": dry_seq, "last_use": dict(S.last_use)}
        S.finish("sp")
        S.emit(top)
    return nc


def _consts():
    ident = np.eye(128, dtype=np.float32)
    k = np.arange(128)[:, None]
    q = np.arange(128)[None, :]
    prev = (k >= q).astype(np.float32)
    cur = (k <= q).astype(np.float32)
    cm = np.concatenate([prev, cur], axis=1)
    cmask = np.tile(cm, (1, 4))
    esel = np.zeros((128, 16), np.float32)
    for hs in range(4):
        esel[:, hs * 4 + hs] = 1.0
    selb = np.zeros((4, 256), np.float32)
    for pr in range(2):
        for col in range(128):
            selb[2 * pr + col // 64, pr * 128 + col] = 1.0
    ones = np.ones((128, 128), np.float32)
    tril = cur.copy()
    half = 32
    inv_freq = (np.float32(10000.0) ** (-np.arange(half, dtype=np.float32) / np.float32(half))).astype(np.float32)
    invf = np.tile(inv_freq, 4).reshape(128, 1).astype(np.float32)
    return dict(ident=ident, cmask=cmask, esel=esel, selb=selb, ones=ones, tril=tril, invf=invf)


def _qk_perm():
    idx = []
    for qd in range(2):
        for ab in range(2):
            for hs in range(4):
                h = 4 * qd + hs
                idx.extend(range(h * 64 + ab * 32, h * 64 + ab * 32 + 32))
    return np.array(idx)


def _layer_weights(l, w_in, conv_w, gmlp_ln_g, gmlp_ln_b, w_s, b_s, p_a, p_b, p_c, w_o, ln1_g, ln1_b,
                   w_gate, w_up, w_down, ln2_g, ln2_b, suffix):
    perm = _qk_perm()
    wi = np.array(w_in[l], dtype=np.float32, copy=True)
    for g in range(3):
        base = ATT0 + g * 1536
        wi[:, base:base + 512] = w_in[l][:, base + perm]
        wi[:, base + 512:base + 1024] = w_in[l][:, base + 512 + perm]

    def pj(v):
        return np.ascontiguousarray(np.asarray(v, np.float32).reshape(NT, 128).T)
    cw = np.asarray(conv_w[l], np.float32)
    convw = np.ascontiguousarray(cw.reshape(3, NT, 128).transpose(2, 1, 0).reshape(128, NT * 3))
    d = {
        "w_in": wi,
        "convw": convw,
        "glng": np.asarray(gmlp_ln_g[l], np.float32).reshape(1, D),
        "glnb": np.asarray(gmlp_ln_b[l], np.float32).reshape(1, D),
        "wsT": np.ascontiguousarray(np.asarray(w_s[l], np.float32).transpose(0, 2, 1)),
        "bs": np.asarray(b_s[l], np.float32).reshape(1, D),
        "p_a": np.asarray(p_a[l], np.float32),
        "p_b": np.asarray(p_b[l], np.float32),
        "p_c": np.asarray(p_c[l], np.float32),
        "w_o": np.asarray(w_o[l], np.float32),
        "ln1g": pj(ln1_g[l]), "ln1b": pj(ln1_b[l]),
        "w_gate": np.asarray(w_gate[l], np.float32),
        "w_up": np.asarray(w_up[l], np.float32),
        "w_down": np.asarray(w_down[l], np.float32),
        "ln2g": pj(ln2_g[l]), "ln2b": pj(ln2_b[l]),
    }
    return {k + suffix: np.ascontiguousarray(v) for k, v in d.items()}


_NC_CACHE = {}


def kernel(x, positions, w_in, conv_w, gmlp_ln_g, gmlp_ln_b, w_s, b_s, p_a, p_b, p_c, w_o,
           ln1_g, ln1_b, w_gate, w_up, w_down, ln2_g, ln2_b):
    x = np.asarray(x, np.float32)
    positions = np.asarray(positions, np.int32)
    B, Sq, _ = x.shape
    consts = _consts()
    if 2 not in _NC_CACHE:
        plan = build_program(2)
        _NC_CACHE[2] = build_program(2, plan=plan)
    nc = _NC_CACHE[2]
    lw = {}
    for l in range(DEPTH):
        lw.update(_layer_weights(l, w_in, conv_w, gmlp_ln_g, gmlp_ln_b, w_s, b_s, p_a, p_b, p_c, w_o, ln1_g, ln1_b,
                                 w_gate, w_up, w_down, ln2_g, ln2_b, str(l)))
    zx = np.zeros((D, T), np.float32)
    zp = np.zeros((T,), np.int32)

    def xt(b, q):
        return np.ascontiguousarray(x[b, q * T:(q + 1) * T, :].T) if q >= 0 else zx

    def pp(b, q):
        return positions[b, q * T:(q + 1) * T] if q >= 0 else zp

    in_maps = []
    for c in range(8):
        b, qtr = c // 4, c % 4
        pos = np.concatenate([pp(b, qtr - 2), pp(b, qtr - 1), pp(b, qtr)]).reshape(1, 3 * T).astype(np.int32)
        m = {"xT": xt(b, qtr), "xhT": xt(b, qtr - 1), "xh2T": xt(b, qtr - 2), "pos": pos,
             "flag": np.full((128, 1), 1.0 if qtr >= 1 else 0.0, np.float32),
             "flag2": np.full((128, 1), 1.0 if qtr >= 2 else 0.0, np.float32)}
        m.update(consts)
        m.update(lw)
        in_maps.append(m)
    res = run_bass_kernel_spmd(nc, in_maps, core_ids=list(range(8)))
    out = np.empty((B, Sq, D), np.float32)
    for c in range(8):
        out[c // 4, (c % 4) * T:(c % 4 + 1) * T, :] = np.asarray(res.results[c]["out"]).T
    return out
```

```python
import math
from contextlib import ExitStack

import numpy as np
import concourse.bass as bass
import concourse.mybir as mybir
from concourse.bass_utils import run_bass_kernel_spmd

F32 = mybir.dt.float32
BF16 = mybir.dt.bfloat16
I32 = mybir.dt.int32
AF = mybir.ActivationFunctionType
ALU = mybir.AluOpType

D = 1024
NT = 8
T = 2048
SP = 512
NS = T // SP
DFF = 2816
NF = DFF // 128
DEPTH = 2
ALPHA = (2 * DEPTH) ** 0.25
EPS = 1e-5
N_IN = 12800
GA, GB, GC, CB, CC, CH = 0, 1024, 2048, 3072, 4096, 5120
ATT0 = 6144
UO, VO = 10752, 11776
DILS = (1, 4, 16)
MAGIC = 12582912.0
TWO_PI = 2.0 * math.pi
C1 = 6.28125
C2 = TWO_PI - C1


class Res:
    def __init__(self, name=""):
        self.name = name
        self.last_w = None
        self.reads = []


class Sched:
    ENG = ["pe", "act", "dve", "pool", "sp"]

    def __init__(self, nc, ndma=16):
        self.nc = nc
        self.ops = {e: [] for e in self.ENG}
        self.cnt = {e: 0 for e in self.ENG}
        self.known = {e: {} for e in self.ENG}
        self.ndma = ndma
        self.dma_cnt = [0] * ndma
        self.n_sp = 8
        self.rr_sp = 0
        self.rr_pool = 0
        self.opclock = 0
        self.on_op = None
        self.last_use = None

    def _need(self, eng, tok, waits):
        if tok is None:
            return
        kind, key, val = tok
        if kind == 'e' and key == eng and eng == 'pe':
            return
        k = (kind, key)
        if self.known[eng].get(k, 0) >= val:
            return
        self.known[eng][k] = val
        waits[k] = max(waits.get(k, 0), val)

    def _deps(self, eng, reads, writes, waits):
        for r in reads:
            self._need(eng, r.last_w, waits)
        for w in writes:
            self._need(eng, w.last_w, waits)
            for t in w.reads:
                self._need(eng, t, waits)

    def _commit(self, tok, reads, writes):
        for r in reads:
            r.reads.append(tok)
            if len(r.reads) > 64:
                r.reads = r.reads[-48:]
        for w in writes:
            w.last_w = tok
            w.reads = []

    def op(self, eng, fn, reads=(), writes=()):
        self.opclock += 1
        if self.last_use is not None:
            for r in reads:
                w = getattr(r, "widx", None)
                if w is not None:
                    self.last_use[w] = self.opclock
        if self.on_op is not None:
            self.on_op()
        waits = {}
        self._deps(eng, reads, writes, waits)
        self.cnt[eng] += 1
        tok = ('e', eng, self.cnt[eng])
        self.ops[eng].append((list(waits.items()), fn, ('e', eng, 1)))
        self._commit(tok, reads, writes)
        return tok

    def dma(self, fn, reads=(), writes=(), eng="sp"):
        if eng == "sp":
            ch = self.rr_sp
            self.rr_sp = (self.rr_sp + 1) % self.n_sp
        else:
            ch = self.n_sp + self.rr_pool
            self.rr_pool = (self.rr_pool + 1) % (self.ndma - self.n_sp)
        waits = {}
        if self.dma_cnt[ch] > 0:
            self._need(eng, ('d', ch, self.dma_cnt[ch] * 16), waits)
        self._deps(eng, reads, writes, waits)
        self.dma_cnt[ch] += 1
        tok = ('d', ch, self.dma_cnt[ch] * 16)
        self.ops[eng].append((list(waits.items()), fn, ('d', ch, 16)))
        self._commit(tok, reads, writes)
        return tok

    def barrier(self):
        toks = [('e', e2, self.cnt[e2]) for e2 in self.ENG if self.cnt[e2] > 0]
        toks += [('d', i, self.dma_cnt[i] * 16) for i in range(self.ndma) if self.dma_cnt[i] > 0]
        for eng in self.ENG:
            waits = {}
            for t in toks:
                if t[0] == 'e' and t[1] == eng:
                    continue
                self._need(eng, t, waits)
            self.ops[eng].append((list(waits.items()), None, None))

    def finish(self, eng="sp"):
        waits = {}
        for i in range(self.ndma):
            if self.dma_cnt[i] > 0:
                self._need(eng, ('d', i, self.dma_cnt[i] * 16), waits)
        self.ops[eng].append((list(waits.items()), None, None))

    def emit(self, stack):
        nc = self.nc
        esem = {e: stack.enter_context(nc.semaphore("s_" + e)) for e in self.ENG}
        dsem = [stack.enter_context(nc.semaphore("d_%d" % i)) for i in range(self.ndma)]

        def semof(k):
            return esem[k[1]] if k[0] == 'e' else dsem[k[1]]

        block = stack.enter_context(nc.Block())

        def run(engname):
            def body(e):
                for waits, fn, inc in self.ops[engname]:
                    for k, v in waits:
                        e.wait_ge(semof(k), v)
                    if fn is not None:
                        ins = fn(e)
                        ins.then_inc(semof(inc), inc[2])
            return body
        block.tensor(run("pe"))
        block.scalar(run("act"))
        block.vector(run("dve"))
        block.gpsimd(run("pool"))
        block.sync(run("sp"))


class Tile:
    def __init__(self, h, name):
        self.h = h
        self.r = Res(name)

    def __getitem__(self, k):
        return self.h[k]


class Ring:
    def __init__(self, tiles):
        self.tiles = tiles
        self.i = 0

    def next(self):
        t = self.tiles[self.i % len(self.tiles)]
        self.i += 1
        return t


def build_program(n_layers, kstop=99, plan=None):
    nc = bass.Bass("TRN2", target_bir_lowering=False)

    def din(name, shape, dt=F32):
        return nc.dram_tensor(name, list(shape), dt, kind="ExternalInput").ap()

    xT_d = din("xT", [D, T])
    xhT_d = din("xhT", [D, T])
    xh2T_d = din("xh2T", [D, T])
    pos_d = din("pos", [1, 3 * T], I32)
    flag_d = din("flag", [128, 1])
    flag2_d = din("flag2", [128, 1])
    ident_d = din("ident", [128, 128])
    cmask_d = din("cmask", [128, 1024])
    esel_d = din("esel", [128, 16])
    selb_d = din("selb", [4, 256])
    ones_d = din("ones", [128, 128])
    tril_d = din("tril", [128, 128])
    invf_d = din("invf", [128, 1])
    W = []
    for l in range(n_layers):
        W.append(dict(
            w_in=din("w_in%d" % l, [D, N_IN]),
            convw=din("convw%d" % l, [128, NT * 3]),
            glng=din("glng%d" % l, [1, D]),
            glnb=din("glnb%d" % l, [1, D]),
            wsT=din("wsT%d" % l, [8, 128, 128]),
            bs=din("bs%d" % l, [1, D]),
            p_a=din("p_a%d" % l, [D, D]),
            p_b=din("p_b%d" % l, [512, D]),
            p_c=din("p_c%d" % l, [D, D]),
            w_o=din("w_o%d" % l, [D, D]),
            ln1g=din("ln1g%d" % l, [128, NT]),
            ln1b=din("ln1b%d" % l, [128, NT]),
            w_gate=din("w_gate%d" % l, [D, DFF]),
            w_up=din("w_up%d" % l, [D, DFF]),
            w_down=din("w_down%d" % l, [DFF, D]),
            ln2g=din("ln2g%d" % l, [128, NT]),
            ln2b=din("ln2b%d" % l, [128, NT]),
        ))
    out_d = nc.dram_tensor("out", [D, T], F32, kind="ExternalOutput").ap()
    x1_d = nc.dram_tensor("x1_scratch", [D, T], F32).ap()
    x1h_d = nc.dram_tensor("x1h_scratch", [D, T], F32).ap()

    S = Sched(nc, ndma=24)

    with ExitStack() as top:
        def sb(st, name, shape, dt):
            return Tile(st.enter_context(nc.sbuf_tensor("sb_" + name, list(shape), dt)), name)

        def psum(st, name, shape, dt=F32):
            return Tile(st.enter_context(nc.psum_tensor("ps_" + name, list(shape), dt)), name)

        y_b = sb(top, "y_b", [128, 4, T], BF16)
        ident = sb(top, "ident", [128, 128], BF16)
        cmask = sb(top, "cmask", [128, 1024], BF16)
        esel = sb(top, "esel", [128, 16], BF16)
        eselF = sb(top, "eselF", [128, 16], BF16)
        selb = sb(top, "selb", [4, 256], F32)
        ones = sb(top, "ones", [128, 128], BF16)
        tril = sb(top, "tril", [128, 128], F32)
        invf = sb(top, "invf", [128, 1], F32)
        flag = sb(top, "flag", [128, 1], F32)
        flag2 = sb(top, "flag2", [128, 1], F32)
        eselF2 = sb(top, "eselF2", [128, 16], BF16)
        zhist = sb(top, "zhist", [128, NT, 2], F32)
        wring = Ring([sb(top, "wbuf%d" % i, [128, 4096], BF16) for i in range(5)])
        tmpr = Ring([sb(top, "tmp%d" % i, [128, 512], F32) for i in range(8)])

        def dmac(out, in_, writes, reads=()):
            return S.dma(lambda e, o=out, i=in_: e.dma_start(out=o, in_=i), reads=reads, writes=writes, eng="pool")

        def dmas(out, in_, writes=(), reads=()):
            return S.dma(lambda e, o=out, i=in_: e.dma_start(out=o, in_=i), reads=reads, writes=writes, eng="sp")

        NBUF = len(wring.tiles)
        PF = 2
        wst = {"idx": 0, "next": 0}
        WAP = {}
        for Wl_ in W:
            for ap_ in Wl_.values():
                WAP[ap_.tensor.name] = ap_
        if plan is None:
            S.last_use = {}
            dry_seq = []

        wcache = {}
        wcount = {}
        if plan is not None:
            for key_ in plan["seq"]:
                wcount[key_] = wcount.get(key_, 0) + 1

        def _issue(k):
            key = plan["seq"][k]
            (nm, c0, width, K) = key
            kt = K // 128
            t = wring.tiles[k % NBUF]
            flat = t.h[:, 0:kt * width]
            view = flat.rearrange("p (k c) -> p k c", k=kt)
            if wcount[key] < 3:
                dmac(view, WAP[nm][:, c0:c0 + width].rearrange("(k p) c -> p k c", p=128), writes=[t.r])
            elif key not in wcache:
                dmac(view, WAP[nm][:, c0:c0 + width].rearrange("(k p) c -> p k c", p=128), writes=[t.r])
                sc = nc.dram_tensor("wcache_%d" % len(wcache), [128, kt * width], BF16).ap()
                r = Res("wc")
                wcache[key] = (sc, r)
                dmas(sc, flat, writes=[r], reads=[t.r])
            else:
                sc, r = wcache[key]
                dmas(flat, sc, writes=[t.r], reads=[r])

        def _issue_upto(limit):
            limit = min(limit, len(plan["seq"]) - 1)
            while wst["next"] <= limit:
                k = wst["next"]
                if k >= NBUF and S.opclock < plan["last_use"].get(k - NBUF, 0):
                    break
                _issue(k)
                wst["next"] += 1

        if plan is not None:
            S.on_op = lambda: _issue_upto(wst["idx"] - 1 + PF)

        def wload(w_ap, c0, width, K):
            kt = K // 128
            key = (w_ap.tensor.name, c0, width, K)
            idx = wst["idx"]
            wst["idx"] += 1
            if plan is None:
                dry_seq.append(key)
                t = wring.next()
                view = t.h[:, 0:kt * width].rearrange("p (k c) -> p k c", k=kt)
                r = Res("w%d" % idx)
                r.widx = idx
                return view, r
            assert plan["seq"][idx] == key, (idx, key, plan["seq"][idx])
            _issue_upto(idx + PF)
            assert wst["next"] > idx, "weight ring too small: block %d not issuable" % idx
            t = wring.tiles[idx % NBUF]
            view = t.h[:, 0:kt * width].rearrange("p (k c) -> p k c", k=kt)
            return view, t.r

        def mmgroup(out_ap, pairs, reads, writes, tp=None):
            n = len(pairs)

            def fn(e, out_ap=out_ap, pairs=pairs, tp=tp):
                ins = None
                for i, (l, r) in enumerate(pairs):
                    if tp is None:
                        ins = e.matmul(out_ap, lhsT=l, rhs=r, start=(i == 0), stop=(i == n - 1))
                    else:
                        ins = e.matmul(out_ap, lhsT=l, rhs=r, start=(i == 0), stop=(i == n - 1), tile_position=tp)
                return ins
            return S.op("pe", fn, reads=reads, writes=writes)

        def tt(eng, out, in0, in1, op, reads, writes):
            return S.op(eng, lambda e: e.tensor_tensor(out=out, in0=in0, in1=in1, op=op), reads=reads, writes=writes)

        def tsc(eng, out, in0, s1, s2, op0, op1, reads, writes):
            if op1 is None:
                return S.op(eng, lambda e: e.tensor_scalar(out=out, in0=in0, scalar1=s1, scalar2=None, op0=op0), reads=reads, writes=writes)
            return S.op(eng, lambda e: e.tensor_scalar(out=out, in0=in0, scalar1=s1, scalar2=s2, op0=op0, op1=op1), reads=reads, writes=writes)

        def stt(eng, out, in0, scalar, in1, op0, op1, reads, writes):
            return S.op(eng, lambda e: e.scalar_tensor_tensor(out=out, in0=in0, scalar=scalar, in1=in1, op0=op0, op1=op1), reads=reads, writes=writes)

        def cpy(eng, out, in_, reads, writes):
            return S.op(eng, lambda e: e.tensor_copy(out=out, in_=in_), reads=reads, writes=writes)

        def rcp(out, in_, reads, writes):
            return S.op("dve", lambda e: e.reciprocal(out=out, in_=in_), reads=reads, writes=writes)

        def act(out, in_, func, reads, writes, **kw):
            return S.op("act", lambda e: e.activation(out=out, in_=in_, func=func, **kw), reads=reads, writes=writes)

        dmac(ident[:], ident_d, [ident.r])
        dmac(cmask[:], cmask_d, [cmask.r])
        dmac(esel[:], esel_d, [esel.r])
        dmac(ones[:], ones_d, [ones.r])
        dmas(selb[:], selb_d, [selb.r])
        dmas(tril[:], tril_d, [tril.r])
        dmas(invf[:], invf_d, [invf.r])
        dmas(flag[:], flag_d, [flag.r])
        dmas(flag2[:], flag2_d, [flag2.r])
        S.op("dve", lambda e: e.tensor_scalar(out=eselF2[:], in0=esel[:], scalar1=flag2[:, 0:1], scalar2=None, op0=ALU.mult),
             reads=[esel.r, flag2.r], writes=[eselF2.r])
        S.op("dve", lambda e: e.tensor_scalar(out=eselF[:], in0=esel[:], scalar1=flag[:, 0:1], scalar2=None, op0=ALU.mult),
             reads=[esel.r, flag.r], writes=[eselF.r])

        def rope_tables(p0, n, cos_ap, sin_ap, cos_r, sin_r, posi):
            posf, ta, tb = tmpr.next(), tmpr.next(), tmpr.next()
            dmas(posi[:, 0:n], pos_d[0:1, p0:p0 + n].partition_broadcast(128), writes=[posi.r])
            S.op("dve", lambda e: e.tensor_copy(out=posf[:, 0:n], in_=posi[:, 0:n]), reads=[posi.r], writes=[posf.r])
            S.op("dve", lambda e: e.tensor_scalar(out=posf[:, 0:n], in0=posf[:, 0:n], scalar1=invf[:, 0:1], scalar2=None, op0=ALU.mult),
                 reads=[posf.r, invf.r], writes=[posf.r])
            S.op("dve", lambda e: e.tensor_scalar(out=ta[:, 0:n], in0=posf[:, 0:n], scalar1=1.0 / TWO_PI, scalar2=MAGIC, op0=ALU.mult, op1=ALU.add),
                 reads=[posf.r], writes=[ta.r])
            S.op("dve", lambda e: e.tensor_scalar(out=ta[:, 0:n], in0=ta[:, 0:n], scalar1=MAGIC, scalar2=None, op0=ALU.subtract),
                 reads=[ta.r], writes=[ta.r])
            S.op("dve", lambda e: e.scalar_tensor_tensor(out=posf[:, 0:n], in0=ta[:, 0:n], scalar=-C1, in1=posf[:, 0:n], op0=ALU.mult, op1=ALU.add),
                 reads=[ta.r, posf.r], writes=[posf.r])
            S.op("dve", lambda e: e.scalar_tensor_tensor(out=posf[:, 0:n], in0=ta[:, 0:n], scalar=-C2, in1=posf[:, 0:n], op0=ALU.mult, op1=ALU.add),
                 reads=[ta.r, posf.r], writes=[posf.r])
            S.op("dve", lambda e: e.tensor_scalar(out=posf[:, 0:n], in0=posf[:, 0:n], scalar1=-3.1415925, scalar2=3.1415925, op0=ALU.max, op1=ALU.min),
                 reads=[posf.r], writes=[posf.r])
            act(sin_ap, posf[:, 0:n], AF.Sin, [posf.r], [sin_r])
            S.op("dve", lambda e: e.scalar_tensor_tensor(out=tb[:, 0:n], in0=posf[:, 0:n], scalar=-1.0, in1=posf[:, 0:n], op0=ALU.mult, op1=ALU.max),
                 reads=[posf.r], writes=[tb.r])
            S.op("dve", lambda e: e.tensor_scalar(out=tb[:, 0:n], in0=tb[:, 0:n], scalar1=-1.0, scalar2=math.pi / 2, op0=ALU.mult, op1=ALU.add),
                 reads=[tb.r], writes=[tb.r])
            act(cos_ap, tb[:, 0:n], AF.Sin, [tb.r], [cos_r])

        xin_r = Res("xin")
        x1_r = Res("x1")
        x1h_r = Res("x1h")
        if n_layers == 1:
            passes = [(0, xT_d, xhT_d, T, flag, eselF, out_d, xin_r, xin_r, None)]
        else:
            passes = [
                (0, xhT_d, xh2T_d, 0, flag2, eselF2, x1h_d, xin_r, xin_r, x1h_r),
                (0, xT_d, xhT_d, T, flag, eselF, x1_d, xin_r, xin_r, x1_r),
                (1, x1_d, x1h_d, T, flag, eselF, out_d, x1_r, x1h_r, None),
            ]
        for pi, (l, x_src, xh_src, pos0, flag_t, eselF_t, out_ap, xsrc_r, xhsrc_r, out_r) in enumerate(passes):
            Wl = W[l]
            w_in = Wl["w_in"]
            with ExitStack() as pa:
                bring = Ring([psum(pa, "bankA%d_%d" % (i, pi), [128, 512]) for i in range(7)])
                bank_t = psum(pa, "bank_t_%d" % pi, [128, 1024], BF16)
                cos_o = sb(pa, "cos_o" + "_%d" % pi, [128, T], F32)
                sin_o = sb(pa, "sin_o" + "_%d" % pi, [128, T], F32)
                cos_h = sb(pa, "cos_h" + "_%d" % pi, [128, SP], F32)
                sin_h = sb(pa, "sin_h" + "_%d" % pi, [128, SP], F32)
                posi = sb(pa, "posi" + "_%d" % pi, [128, SP], I32)
                acc = sb(pa, "acc" + "_%d" % pi, [128, 2, T], F32)
                accd = sb(pa, "accd" + "_%d" % pi, [4, T], F32)
                KT = sb(pa, "KT" + "_%d" % pi, [128, 2, 2 * T], BF16)
                VT = sb(pa, "VT" + "_%d" % pi, [128, 2, 2 * T], BF16)
                QT = sb(pa, "QT" + "_%d" % pi, [128, 2, T], BF16)
                vbr = Ring([sb(pa, "vb%d_%d" % (i, pi), [128, 256], BF16) for i in range(6)])
                P_t = Ring([sb(pa, "P%d_%d" % (i, pi), [128, 1024], BF16) for i in range(3)])
                xcr = Ring([sb(pa, "xca%d_%d" % (i, pi), [128, NT, SP], BF16) for i in range(2)])
                for s in range(NS):
                    rope_tables(pos0 + T + s * SP, SP, cos_o[:, s * SP:(s + 1) * SP], sin_o[:, s * SP:(s + 1) * SP], cos_o.r, sin_o.r, posi)

                for qd in range(2):
                    if kstop <= 1:
                        break
                    S.op("pool", lambda e, acc=acc: e.memset(acc[:], 0.0), writes=[acc.r])
                    S.op("pool", lambda e, accd=accd: e.memset(accd[:], 0.0), writes=[accd.r])
                    for g, dil in enumerate(DILS):
                        HL = 128 * dil
                        base = ATT0 + g * 1536
                        wq_v, wq_r = wload(w_in, base + qd * 256, 256, D)
                        wk_v, wk_r = wload(w_in, base + 512 + qd * 256, 256, D)
                        wv_v, wv_r = wload(w_in, base + 1024 + qd * 256, 256, D)
                        chunks = []
                        hs0 = T - HL
                        n_h = min(HL, SP)
                        for a in range(hs0, T, n_h):
                            chunks.append((True, a, n_h, a - hs0))
                        for s in range(NS):
                            chunks.append((False, s * SP, SP, HL + s * SP))
                        loaded = {}

                        def issue_chunk(ci, chunks=chunks, loaded=loaded):
                            (is_h_, t0_, n_, c0_) = chunks[ci]
                            xc_ = xcr.next()
                            src_ = xh_src if is_h_ else x_src
                            dmac(xc_[:, :, 0:n_], src_[:, t0_:t0_ + n_].rearrange("(k p) t -> p k t", p=128), [xc_.r], reads=[xhsrc_r if is_h_ else xsrc_r])
                            loaded[ci] = xc_
                        issue_chunk(0)
                        for ci, (is_h, t0, n, c0) in enumerate(chunks):
                            if ci + 1 < len(chunks):
                                issue_chunk(ci + 1)
                            xc = loaded[ci]
                            if is_h:
                                rope_tables(pos0 + t0, n, cos_h[:, 0:n], sin_h[:, 0:n], cos_h.r, sin_h.r, posi)
                                cs_ap, sn_ap, cs_r, sn_r = cos_h[:, 0:n], sin_h[:, 0:n], cos_h.r, sin_h.r
                            else:
                                cs_ap, sn_ap, cs_r, sn_r = cos_o[:, t0:t0 + n], sin_o[:, t0:t0 + n], cos_o.r, sin_o.r
                            todo = [(wk_v, wk_r, KT, c0)]
                            if not is_h:
                                todo.append((wq_v, wq_r, QT, t0))
                            for (wt, wr, dst, dc0) in todo:
                                pA = bring.next()
                                pB = bring.next()
                                mmgroup(pA[:, 0:n], [(wt[:, k, 0:128], xc[:, k, 0:n]) for k in range(NT)], [wr, xc.r], [pA.r])
                                mmgroup(pB[:, 0:n], [(wt[:, k, 128:256], xc[:, k, 0:n]) for k in range(NT)], [wr, xc.r], [pB.r])
                                t1, t2, t3, t4 = tmpr.next(), tmpr.next(), tmpr.next(), tmpr.next()
                                tt("dve", t1[:, 0:n], pA[:, 0:n], cs_ap, ALU.mult, [pA.r, cs_r], [t1.r])
                                tt("dve", t2[:, 0:n], pB[:, 0:n], sn_ap, ALU.mult, [pB.r, sn_r], [t2.r])
                                tt("pool", dst[:, 0, dc0:dc0 + n], t1[:, 0:n], t2[:, 0:n], ALU.subtract, [t1.r, t2.r], [dst.r])
                                tt("dve", t3[:, 0:n], pB[:, 0:n], cs_ap, ALU.mult, [pB.r, cs_r], [t3.r])
                                tt("dve", t4[:, 0:n], pA[:, 0:n], sn_ap, ALU.mult, [pA.r, sn_r], [t4.r])
                                tt("pool", dst[:, 1, dc0:dc0 + n], t3[:, 0:n], t4[:, 0:n], ALU.add, [t3.r, t4.r], [dst.r])
                            for vt in range(2):
                                pV = bring.next()
                                mmgroup(pV[:, 0:n], [(wv_v[:, k, vt * 128:(vt + 1) * 128], xc[:, k, 0:n]) for k in range(NT)], [wv_r, xc.r], [pV.r])
                                act(VT[:, vt, c0:c0 + n], pV[:, 0:n], AF.Copy, [pV.r], [VT.r])
                        if kstop <= 2:
                            break
                        nb = T // (128 * dil)
                        Wd = HL + T

                        def kview(tl, lo, hi, ab, m, r, dil=dil, Wd=Wd):
                            return tl.h[lo:hi, ab, 0:Wd].rearrange("p (m i r) -> p m r i", i=128, r=dil)[:, m, r, :]

                        def qview(tl, lo, hi, ab, m, r, dil=dil):
                            return tl.h[lo:hi, ab, 0:T].rearrange("p (m i r) -> p m r i", i=128, r=dil)[:, m, r, :]

                        pending = [None]
                        for r in range(dil):
                            vprev = None
                            for m in range(nb + 1):
                                vb = vbr.next()

                                v0_ap = kview(VT, 0, 128, 0, m, r)
                                v1_ap = kview(VT, 0, 128, 1, m, r)

                                def tr2(e, v0_ap=v0_ap, v1_ap=v1_ap, o0=bank_t[:, 0:128], o1=bank_t[:, 128:256], idn=ident[:]):
                                    e.transpose(out=o0, in_=v0_ap, identity=idn)
                                    return e.transpose(out=o1, in_=v1_ap, identity=idn)
                                S.op("pe", tr2, reads=[VT.r, ident.r], writes=[bank_t.r])
                                if m == 0:
                                    act(vb[:], bank_t[:, 0:256], AF.Copy, [bank_t.r, flag_t.r], [vb.r], scale=flag_t[:, 0:1])
                                    vprev = vb
                                    continue
                                act(vb[:], bank_t[:, 0:256], AF.Copy, [bank_t.r], [vb.r])
                                n_q = m - 1
                                sbks = [bring.next() for _ in range(4)]
                                for hs in range(4):
                                    sbk = sbks[hs]
                                    lo, hi = 32 * hs, 32 * hs + 32
                                    for kt in range(2):
                                        o0 = kt * 128
                                        mmgroup(sbk[:, o0:o0 + 128],
                                                [(kview(KT, lo, hi, 0, n_q + kt, r), qview(QT, lo, hi, 0, n_q, r)),
                                                 (kview(KT, lo, hi, 1, n_q + kt, r), qview(QT, lo, hi, 1, n_q, r))],
                                                [KT.r, QT.r], [sbk.r], tp=(32 * hs, 0))
                                P = P_t.next()
                                for hs in range(4):
                                    act(P[:, hs * 256:(hs + 1) * 256], sbks[hs][:, 0:256], AF.Exp, [sbks[hs].r], [P.r], scale=0.125)
                                tt("pool", P[:], P[:], cmask[:], ALU.mult, [P.r, cmask.r], [P.r])
                                def pv_stage(vprev=vprev, vb=vb, P=P, m=m, n_q=n_q, r=r, dil=dil, acc=acc, accd=accd):
                                    nd = bring.next()
                                    vbs = (vprev, vb)
                                    for pr in range(2):
                                        for hh in range(2):
                                            hs = 2 * pr + hh
                                            mmgroup(nd[64 * hh:64 * hh + 64, pr * 128:(pr + 1) * 128],
                                                    [(vbs[kt][:, hs * 64:(hs + 1) * 64], P[:, hs * 256 + kt * 128: hs * 256 + kt * 128 + 128]) for kt in range(2)],
                                                    [vprev.r, vb.r, P.r], [nd.r], tp=(0, 64 * hh))
                                    es_prev = eselF_t if m == 1 else esel
                                    pairs = []
                                    for hs in range(4):
                                        pairs.append((es_prev[:, hs * 4:(hs + 1) * 4], P[:, hs * 256: hs * 256 + 128]))
                                        pairs.append((esel[:, hs * 4:(hs + 1) * 4], P[:, hs * 256 + 128: hs * 256 + 256]))
                                    mmgroup(nd[0:4, 256:384], pairs, [P.r, esel.r, eselF_t.r], [nd.r])
                                    accv = acc.h[:, :, :].rearrange("p a (m i r) -> p a m r i", i=128, r=dil)[:, :, n_q, r, :]
                                    tt("dve", accv, accv, nd[:, 0:256].rearrange("p (a i) -> p a i", a=2), ALU.add, [nd.r, acc.r], [acc.r])
                                    adv = accd.h[:, :].rearrange("p (m i r) -> p m r i", i=128, r=dil)[:, n_q, r, :]
                                    tt("dve", adv, adv, nd[0:4, 256:384], ALU.add, [nd.r, accd.r], [accd.r])
                                if pending[0] is not None:
                                    pending[0]()
                                pending[0] = pv_stage
                                vprev = vb
                        if pending[0] is not None:
                            pending[0]()
                            pending[0] = None
                    if kstop <= 3:
                        break
                    rcp(accd[:], accd[:], [accd.r], [accd.r])
                    if kstop <= 4:
                        break
                    for pr in range(2):
                        for s in range(NS):
                            bc = bring.next()
                            mmgroup(bc[:, :], [(selb[:, pr * 128:(pr + 1) * 128], accd[:, s * SP:(s + 1) * SP])], [selb.r, accd.r], [bc.r])
                            tt("dve", y_b[:, 2 * qd + pr, s * SP:(s + 1) * SP], acc[:, pr, s * SP:(s + 1) * SP], bc[:, :], ALU.mult,
                               [bc.r, acc.r], [y_b.r])

                S.barrier()
            if kstop <= 5:
                break
            with ExitStack() as pb:
                bring = Ring([psum(pb, "bankB%d_%d" % (i, pi), [128, 512]) for i in range(6)])
                stat_banks = [psum(pb, "bankS%d_%d" % (i, pi), [128, 512]) for i in range(2)]
                xs = sb(pb, "xs" + "_%d" % pi, [128, NT, SP], F32)
                xc = sb(pb, "xcb" + "_%d" % pi, [128, NT, SP], BF16)
                vfm = sb(pb, "vfm" + "_%d" % pi, [128, 4 * D], F32)
                u_sb = sb(pb, "u_sb" + "_%d" % pi, [128, NT, SP], BF16)
                y_a = sb(pb, "y_a" + "_%d" % pi, [128, NT, SP], BF16)
                y_c = sb(pb, "y_c" + "_%d" % pi, [128, NT, SP], BF16)
                vbf = [sb(pb, "vbf%d_%d" % (i, pi), [128, D], BF16) for i in range(4)]
                m_sb = sb(pb, "m_sb" + "_%d" % pi, [128, NT, SP], BF16)
                h_sb = sb(pb, "h_sb" + "_%d" % pi, [128, NF, SP], BF16)
                zbuf = sb(pb, "zbuf" + "_%d" % pi, [128, SP + 2], F32)
                convw = sb(pb, "convw" + "_%d" % pi, [128, NT * 3], F32)
                glng = sb(pb, "glng" + "_%d" % pi, [128, D], F32)
                glnb = sb(pb, "glnb" + "_%d" % pi, [128, D], F32)
                bsb = sb(pb, "bsb" + "_%d" % pi, [128, D], F32)
                wsf = sb(pb, "wsf" + "_%d" % pi, [128, 8, 128], F32)
                wsm = sb(pb, "wsm" + "_%d" % pi, [128, 8, 128], BF16)
                ln1g = sb(pb, "ln1g" + "_%d" % pi, [128, NT], F32)
                ln1b = sb(pb, "ln1b" + "_%d" % pi, [128, NT], F32)
                ln2g = sb(pb, "ln2g" + "_%d" % pi, [128, NT], F32)
                ln2b = sb(pb, "ln2b" + "_%d" % pi, [128, NT], F32)
                st6 = sb(pb, "st6" + "_%d" % pi, [128, 2, 6], F32)
                mv = sb(pb, "mv" + "_%d" % pi, [128, 2], F32)
                rstd1 = sb(pb, "rstd1" + "_%d" % pi, [128, 1], F32)
                xh16 = sb(pb, "xh16" + "_%d" % pi, [128, NT, 16], BF16)
                mean_t = sb(pb, "mean_t" + "_%d" % pi, [128, SP], F32)
                rstd_t = sb(pb, "rstd_t" + "_%d" % pi, [128, SP], F32)

                dmas(convw[:], Wl["convw"], [convw.r])
                dmas(glng[:], Wl["glng"].partition_broadcast(128), [glng.r])
                dmas(glnb[:], Wl["glnb"].partition_broadcast(128), [glnb.r])
                dmas(bsb[:], Wl["bs"].partition_broadcast(128), [bsb.r])
                dmas(wsf[:], Wl["wsT"].rearrange("g j i -> j g i"), [wsf.r])
                dmas(ln1g[:], Wl["ln1g"], [ln1g.r])
                dmas(ln1b[:], Wl["ln1b"], [ln1b.r])
                dmas(ln2g[:], Wl["ln2g"], [ln2g.r])
                dmas(ln2b[:], Wl["ln2b"], [ln2b.r])
                tt("dve", wsm[:], wsf[:], tril[:].unsqueeze(1).to_broadcast([128, 8, 128]), ALU.mult, [wsf.r, tril.r], [wsm.r])
                dmac(xh16[:], xh_src[:, T - 16:T].rearrange("(k p) t -> p k t", p=128), [xh16.r], reads=[xhsrc_r])
                for jb in range(2):
                    wC, wCr = wload(w_in, CC + jb * 512, 512, D)
                    wH, wHr = wload(w_in, CH + jb * 512, 512, D)
                    for jj in range(4):
                        j = jb * 4 + jj
                        pC = bring.next()
                        pH = bring.next()
                        mmgroup(pC[:, 0:2], [(wC[:, k, jj * 128:(jj + 1) * 128], xh16[:, k, 14:16]) for k in range(NT)], [wCr, xh16.r], [pC.r])
                        mmgroup(pH[:, 0:2], [(wH[:, k, jj * 128:(jj + 1) * 128], xh16[:, k, 14:16]) for k in range(NT)], [wHr, xh16.r], [pH.r])
                        tz = tmpr.next()
                        act(tz[:, 0:2], pC[:, 0:2], AF.Copy, [pC.r, flag_t.r], [tz.r], scale=flag_t[:, 0:1])
                        tt("dve", zhist[:, j, :], tz[:, 0:2], pH[:, 0:2], ALU.mult, [tz.r, pH.r], [zhist.r])

                class LNStats:
                    def __init__(self):
                        self.rb, self.rsq = y_a, y_c
                        self.pm = stat_banks[0]
                        self.pq = stat_banks[1]
                        self.lag = None

                    def _mm(self, j):
                        rb, rsq, pm, pq = self.rb, self.rsq, self.pm, self.pq
                        S.op("pe", lambda e: e.matmul(pm[:, :], lhsT=ones[:], rhs=rb[:, j, :], start=(j == 0), stop=(j == NT - 1)),
                             reads=[ones.r, rb.r], writes=[pm.r])
                        S.op("pe", lambda e: e.matmul(pq[:, :], lhsT=ones[:], rhs=rsq[:, j, :], start=(j == 0), stop=(j == NT - 1)),
                             reads=[ones.r, rsq.r], writes=[pq.r])

                    def tile_done(self, j):
                        act(self.rb[:, j, :], xs[:, j, :], AF.Identity, [xs.r], [self.rb.r])
                        tt("dve", self.rsq[:, j, :], xs[:, j, :], xs[:, j, :], ALU.mult, [xs.r], [self.rsq.r])
                        if self.lag is not None:
                            self._mm(self.lag)
                        self.lag = j

                    def finish(self, g_t, b_t):
                        self._mm(self.lag)
                        pm, pq = self.pm, self.pq
                        msq = tmpr.next()
                        nmr = tmpr.next()
                        tsc("dve", mean_t[:], pm[:, :], 1.0 / D, None, ALU.mult, None, [pm.r], [mean_t.r])
                        tt("dve", msq[:], mean_t[:], mean_t[:], ALU.mult, [mean_t.r], [msq.r])
                        stt("dve", msq[:], pq[:, :], 1.0 / D, msq[:], ALU.mult, ALU.subtract, [pq.r, msq.r], [msq.r])
                        tsc("dve", msq[:], msq[:], EPS, None, ALU.add, None, [msq.r], [msq.r])
                        act(msq[:], msq[:], AF.Sqrt, [msq.r], [msq.r])
                        rcp(rstd_t[:], msq[:], [msq.r], [rstd_t.r])
                        for j in range(NT):
                            t = tmpr.next()
                            tt("dve", t[:], xs[:, j, :], mean_t[:], ALU.subtract, [xs.r, mean_t.r], [t.r])
                            tt("dve", t[:], t[:], rstd_t[:], ALU.mult, [t.r, rstd_t.r], [t.r])
                            act(xs[:, j, :], t[:], AF.Identity, [t.r, g_t.r, b_t.r], [xs.r], scale=g_t[:, j:j + 1], bias=b_t[:, j:j + 1])
                            act(xc[:, j, :], t[:], AF.Identity, [t.r, g_t.r, b_t.r], [xc.r], scale=g_t[:, j:j + 1], bias=b_t[:, j:j + 1])

                mf = lambda j: vfm[:, j * SP:(j + 1) * SP]
                vf = lambda t_: vfm[:, t_ * D:(t_ + 1) * D]

                for s in range(NS):
                    c0 = s * SP
                    dmac(xs[:, :, :], x_src[:, c0:c0 + SP].rearrange("(k p) t -> p k t", p=128), [xs.r], reads=[xsrc_r])
                    act(xc[:, :, :], xs[:, :, :], AF.Copy, [xs.r], [xc.r])
                    for jb in range(2):
                        wB, wBr = wload(w_in, CB + jb * 512, 512, D)
                        wC, wCr = wload(w_in, CC + jb * 512, 512, D)
                        wH, wHr = wload(w_in, CH + jb * 512, 512, D)
                        for jj in range(4):
                            j = jb * 4 + jj
                            pB_, pC, pH = bring.next(), bring.next(), bring.next()
                            mmgroup(pC[:, :], [(wC[:, k, jj * 128:(jj + 1) * 128], xc[:, k, :]) for k in range(NT)], [wCr, xc.r], [pC.r])
                            mmgroup(pH[:, :], [(wH[:, k, jj * 128:(jj + 1) * 128], xc[:, k, :]) for k in range(NT)], [wHr, xc.r], [pH.r])
                            mmgroup(pB_[:, :], [(wB[:, k, jj * 128:(jj + 1) * 128], xc[:, k, :]) for k in range(NT)], [wBr, xc.r], [pB_.r])
                            tc_ = tmpr.next()
                            ta_ = tmpr.next()
                            act(tc_[:], pC[:, :], AF.Copy, [pC.r], [tc_.r])
                            act(zbuf[:, 0:2], zhist[:, j, :], AF.Copy, [zhist.r], [zbuf.r])
                            tt("dve", zbuf[:, 2:SP + 2], tc_[:], pH[:, :], ALU.mult, [tc_.r, pH.r], [zbuf.r])
                            act(zhist[:, j, :], zbuf[:, SP:SP + 2], AF.Copy, [zbuf.r], [zhist.r])
                            tsc("dve", ta_[:], zbuf[:, 0:SP], convw[:, 3 * j:3 * j + 1], None, ALU.mult, None, [zbuf.r, convw.r], [ta_.r])
                            stt("dve", ta_[:], zbuf[:, 1:SP + 1], convw[:, 3 * j + 1:3 * j + 2], ta_[:], ALU.mult, ALU.add, [zbuf.r, convw.r, ta_.r], [ta_.r])
                            stt("dve", ta_[:], zbuf[:, 2:SP + 2], convw[:, 3 * j + 2:3 * j + 3], ta_[:], ALU.mult, ALU.add, [zbuf.r, convw.r, ta_.r], [ta_.r])
                            tt("dve", y_a[:, j, :], ta_[:], pB_[:, :], ALU.mult, [ta_.r, pB_.r], [y_a.r])
                    for jb in range(2):
                        wU, wUr = wload(w_in, UO + jb * 512, 512, D)
                        for jj in range(4):
                            j = jb * 4 + jj
                            pU = bring.next()
                            mmgroup(pU[:, :], [(wU[:, k, jj * 128:(jj + 1) * 128], xc[:, k, :]) for k in range(NT)], [wUr, xc.r], [pU.r])
                            act(u_sb[:, j, :], pU[:, :], AF.Gelu, [pU.r], [u_sb.r])
                    for half in range(2):
                        wV, wVr = wload(w_in, VO + half * 512, 512, D)
                        for t_ in range(4):
                            pV = bring.next()
                            mmgroup(pV[:, :], [(xc[:, k, t_ * 128:(t_ + 1) * 128], wV[:, k, :]) for k in range(NT)], [wVr, xc.r], [pV.r])
                            act(vfm[:, t_ * D + half * 512: t_ * D + (half + 1) * 512], pV[:, :], AF.Gelu, [pV.r], [vfm.r])
                    for t_ in range(4):
                        v = vf(t_)
                        S.op("dve", lambda e, o_=st6[:, 0, :], i_=v[:, 0:512]: e.bn_stats(out=o_, in_=i_), reads=[vfm.r], writes=[st6.r])
                        S.op("dve", lambda e, o_=st6[:, 1, :], i_=v[:, 512:1024]: e.bn_stats(out=o_, in_=i_), reads=[vfm.r, st6.r], writes=[st6.r])
                        S.op("dve", lambda e, o_=mv[:], i_=st6[:].rearrange("p a b -> p (a b)"): e.bn_aggr(out=o_, in_=i_), reads=[st6.r], writes=[mv.r])
                        tsc("dve", rstd1[:], mv[:, 1:2], EPS, None, ALU.add, None, [mv.r], [rstd1.r])
                        act(rstd1[:], rstd1[:], AF.Sqrt, [rstd1.r], [rstd1.r])
                        rcp(rstd1[:], rstd1[:], [rstd1.r], [rstd1.r])
                        tsc("dve", v, v, mv[:, 0:1], rstd1[:, 0:1], ALU.subtract, ALU.mult, [vfm.r, mv.r, rstd1.r], [vfm.r])
                        tt("dve", v, v, glng[:], ALU.mult, [vfm.r, glng.r], [vfm.r])
                        tt("dve", vbf[t_][:], v, glnb[:], ALU.add, [vfm.r, glnb.r], [vbf[t_].r])
                    def spatial_stage():
                        for gg in range(8):
                            pS = bring.next()
                            for t_ in range(4):
                                mmgroup(pS[:, t_ * 128:(t_ + 1) * 128], [(vbf[t_][:, gg * 128:(gg + 1) * 128], wsm[:, gg, :])], [vbf[t_].r, wsm.r], [pS.r])
                            tq = tmpr.next()
                            tt("dve", tq[:].rearrange("p (c i) -> p c i", c=4), pS[:, :].rearrange("p (c i) -> p c i", c=4),
                               bsb[:, gg * 128:(gg + 1) * 128].unsqueeze(1).to_broadcast([128, 4, 128]), ALU.add, [pS.r, bsb.r], [tq.r])
                            tt("dve", y_c[:, gg, :], tq[:], u_sb[:, gg, :], ALU.mult, [tq.r, u_sb.r], [y_c.r])
                    for bi, (gcol, p_ap, K, y_t) in enumerate(((GA, Wl["p_a"], D, y_a), (GB, Wl["p_b"], 512, y_b), (GC, Wl["p_c"], D, y_c))):
                        kt = K // 128
                        if bi == 2:
                            spatial_stage()
                        for jb in range(2):
                            wG_, wGr_ = wload(w_in, gcol + jb * 512, 512, D)
                            wP_, wPr_ = wload(p_ap, jb * 512, 512, K)
                            for jj in range(4):
                                j = jb * 4 + jj
                                pg, py = bring.next(), bring.next()
                                mmgroup(pg[:, :], [(wG_[:, k, jj * 128:(jj + 1) * 128], xc[:, k, :]) for k in range(NT)], [wGr_, xc.r], [pg.r])
                                if bi == 1:
                                    prs = [(wP_[:, k, jj * 128:(jj + 1) * 128], y_b[:, k, c0:c0 + SP]) for k in range(kt)]
                                else:
                                    prs = [(wP_[:, k, jj * 128:(jj + 1) * 128], y_t[:, k, :]) for k in range(kt)]
                                mmgroup(py[:, :], prs, [wPr_, y_t.r], [py.r])
                                sg = tmpr.next()
                                act(sg[:], pg[:, :], AF.Sigmoid, [pg.r], [sg.r])
                                if bi == 0:
                                    tt("dve", mf(j), sg[:], py[:, :], ALU.mult, [sg.r, py.r], [vfm.r])
                                elif bi == 1:
                                    tt("dve", sg[:], sg[:], py[:, :], ALU.mult, [sg.r, py.r], [sg.r])
                                    tt("dve", mf(j), mf(j), sg[:], ALU.add, [sg.r, vfm.r], [vfm.r])
                                else:
                                    tt("dve", sg[:], sg[:], py[:, :], ALU.mult, [sg.r, py.r], [sg.r])
                                    tt("dve", m_sb[:, j, :], mf(j), sg[:], ALU.add, [sg.r, vfm.r], [m_sb.r])
                    lns = LNStats()
                    for jb in range(2):
                        wO, wOr = wload(Wl["w_o"], jb * 512, 512, D)
                        for jj in range(4):
                            j = jb * 4 + jj
                            po = bring.next()
                            mmgroup(po[:, :], [(wO[:, k, jj * 128:(jj + 1) * 128], m_sb[:, k, :]) for k in range(NT)], [wOr, m_sb.r], [po.r])
                            stt("dve", xs[:, j, :], xs[:, j, :], ALPHA, po[:, :], ALU.mult, ALU.add, [po.r, xs.r], [xs.r])
                            lns.tile_done(j)
                    lns.finish(ln1g, ln1b)
                    for fb in range(NF // 2):
                        wG_, wGr_ = wload(Wl["w_gate"], fb * 256, 256, D)
                        wU_, wUr_ = wload(Wl["w_up"], fb * 256, 256, D)
                        for ff in range(2):
                            f = fb * 2 + ff
                            pg, pu = bring.next(), bring.next()
                            mmgroup(pg[:, :], [(wG_[:, k, ff * 128:(ff + 1) * 128], xc[:, k, :]) for k in range(NT)], [wGr_, xc.r], [pg.r])
                            mmgroup(pu[:, :], [(wU_[:, k, ff * 128:(ff + 1) * 128], xc[:, k, :]) for k in range(NT)], [wUr_, xc.r], [pu.r])
                            sg = tmpr.next()
                            act(sg[:], pg[:, :], AF.Silu, [pg.r], [sg.r])
                            tt("dve", h_sb[:, f, :], sg[:], pu[:, :], ALU.mult, [sg.r, pu.r], [h_sb.r])
                    lns = LNStats()
                    for j in range(NT):
                        wD, wDr = wload(Wl["w_down"], j * 128, 128, DFF)
                        pd = bring.next()
                        mmgroup(pd[:, :], [(wD[:, k, :], h_sb[:, k, :]) for k in range(NF)], [wDr, h_sb.r], [pd.r])
                        stt("dve", xs[:, j, :], xs[:, j, :], ALPHA, pd[:, :], ALU.mult, ALU.add, [pd.r, xs.r], [xs.r])
                        lns.tile_done(j)
                    lns.finish(ln2g, ln2b)
                    dmac(out_ap[:, c0:c0 + SP].rearrange("(k p) t -> p k t", p=128), xs[:, :, :], reads=[xs.r], writes=([out_r] if out_r is not None else []))
                S.barrier()
        if plan is None:
            return {"seq": dry_seq, "last_use": dict(S.last_use)}
        S.finish("sp")
        S.emit(top)
    return nc


def _consts():
    ident = np.eye(128, dtype=np.float32)
    k = np.arange(128)[:, None]
    q = np.arange(128)[None, :]
    prev = (k >= q).astype(np.float32)
    cur = (k <= q).astype(np.float32)
    cm = np.concatenate([prev, cur], axis=1)
    cmask = np.tile(cm, (1, 4))
    esel = np.zeros((128, 16), np.float32)
    for hs in range(4):
        esel[:, hs * 4 + hs] = 1.0
    selb = np.zeros((4, 256), np.float32)
    for pr in range(2):
        for col in range(128):
            selb[2 * pr + col // 64, pr * 128 + col] = 1.0
    ones = np.ones((128, 128), np.float32)
    tril = cur.copy()
    half = 32
    inv_freq = (np.float32(10000.0) ** (-np.arange(half, dtype=np.float32) / np.float32(half))).astype(np.float32)
    invf = np.tile(inv_freq, 4).reshape(128, 1).astype(np.float32)
    return dict(ident=ident, cmask=cmask, esel=esel, selb=selb, ones=ones, tril=tril, invf=invf)


def _qk_perm():
    idx = []
    for qd in range(2):
        for ab in range(2):
            for hs in range(4):
                h = 4 * qd + hs
                idx.extend(range(h * 64 + ab * 32, h * 64 + ab * 32 + 32))
    return np.array(idx)


def _layer_weights(l, w_in, conv_w, gmlp_ln_g, gmlp_ln_b, w_s, b_s, p_a, p_b, p_c, w_o, ln1_g, ln1_b,
                   w_gate, w_up, w_down, ln2_g, ln2_b, suffix):
    perm = _qk_perm()
    wi = np.array(w_in[l], dtype=np.float32, copy=True)
    for g in range(3):
        base = ATT0 + g * 1536
        wi[:, base:base + 512] = w_in[l][:, base + perm]
        wi[:, base + 512:base + 1024] = w_in[l][:, base + 512 + perm]

    def pj(v):
        return np.ascontiguousarray(np.asarray(v, np.float32).reshape(NT, 128).T)
    cw = np.asarray(conv_w[l], np.float32)
    convw = np.ascontiguousarray(cw.reshape(3, NT, 128).transpose(2, 1, 0).reshape(128, NT * 3))
    d = {
        "w_in": wi,
        "convw": convw,
        "glng": np.asarray(gmlp_ln_g[l], np.float32).reshape(1, D),
        "glnb": np.asarray(gmlp_ln_b[l], np.float32).reshape(1, D),
        "wsT": np.ascontiguousarray(np.asarray(w_s[l], np.float32).transpose(0, 2, 1)),
        "bs": np.asarray(b_s[l], np.float32).reshape(1, D),
        "p_a": np.asarray(p_a[l], np.float32),
        "p_b": np.asarray(p_b[l], np.float32),
        "p_c": np.asarray(p_c[l], np.float32),
        "w_o": np.asarray(w_o[l], np.float32),
        "ln1g": pj(ln1_g[l]), "ln1b": pj(ln1_b[l]),
        "w_gate": np.asarray(w_gate[l], np.float32),
        "w_up": np.asarray(w_up[l], np.float32),
        "w_down": np.asarray(w_down[l], np.float32),
        "ln2g": pj(ln2_g[l]), "ln2b": pj(ln2_b[l]),
    }
    return {k + suffix: np.ascontiguousarray(v) for k, v in d.items()}


_NC_CACHE = {}


def kernel(x, positions, w_in, conv_w, gmlp_ln_g, gmlp_ln_b, w_s, b_s, p_a, p_b, p_c, w_o,
           ln1_g, ln1_b, w_gate, w_up, w_down, ln2_g, ln2_b):
    x = np.asarray(x, np.float32)
    positions = np.asarray(positions, np.int32)
    B, Sq, _ = x.shape
    consts = _consts()
    if 2 not in _NC_CACHE:
        plan = build_program(2)
        _NC_CACHE[2] = build_program(2, plan=plan)
    nc = _NC_CACHE[2]
    lw = {}
    for l in range(DEPTH):
        lw.update(_layer_weights(l, w_in, conv_w, gmlp_ln_g, gmlp_ln_b, w_s, b_s, p_a, p_b, p_c, w_o, ln1_g, ln1_b,
                                 w_gate, w_up, w_down, ln2_g, ln2_b, str(l)))
    zx = np.zeros((D, T), np.float32)
    zp = np.zeros((T,), np.int32)

    def xt(b, q):
        return np.ascontiguousarray(x[b, q * T:(q + 1) * T, :].T) if q >= 0 else zx

    def pp(b, q):
        return positions[b, q * T:(q + 1) * T] if q >= 0 else zp

    in_maps = []
    for c in range(8):
        b, qtr = c // 4, c % 4
        pos = np.concatenate([pp(b, qtr - 2), pp(b, qtr - 1), pp(b, qtr)]).reshape(1, 3 * T).astype(np.int32)
        m = {"xT": xt(b, qtr), "xhT": xt(b, qtr - 1), "xh2T": xt(b, qtr - 2), "pos": pos,
             "flag": np.full((128, 1), 1.0 if qtr >= 1 else 0.0, np.float32),
             "flag2": np.full((128, 1), 1.0 if qtr >= 2 else 0.0, np.float32)}
        m.update(consts)
        m.update(lw)
        in_maps.append(m)
    res = run_bass_kernel_spmd(nc, in_maps, core_ids=list(range(8)))
    out = np.empty((B, Sq, D), np.float32)
    for c in range(8):
        out[c // 4, (c % 4) * T:(c % 4 + 1) * T, :] = np.asarray(res.results[c]["out"]).T
    return out
```

```python
import math
from contextlib import ExitStack

import numpy as np
import concourse.bass as bass
import concourse.mybir as mybir
from concourse.bass_utils import run_bass_kernel_spmd

F32 = mybir.dt.float32
BF16 = mybir.dt.bfloat16
I32 = mybir.dt.int32
AF = mybir.ActivationFunctionType
ALU = mybir.AluOpType

D = 1024
NT = 8
T = 2048
SP = 512
NS = T // SP
DFF = 2816
NF = DFF // 128
DEPTH = 2
ALPHA = (2 * DEPTH) ** 0.25
EPS = 1e-5
N_IN = 12800
GA, GB, GC, CB, CC, CH = 0, 1024, 2048, 3072, 4096, 5120
ATT0 = 6144
UO, VO = 10752, 11776
DILS = (1, 4, 16)
MAGIC = 12582912.0
TWO_PI = 2.0 * math.pi
C1 = 6.28125
C2 = TWO_PI - C1


class Res:
    def __init__(self, name=""):
        self.name = name
        self.last_w = None
        self.reads = []


class Sched:
    ENG = ["pe", "act", "dve", "pool", "sp"]

    def __init__(self, nc, ndma=16):
        self.nc = nc
        self.ops = {e: [] for e in self.ENG}
        self.cnt = {e: 0 for e in self.ENG}
        self.known = {e: {} for e in self.ENG}
        self.ndma = ndma
        self.dma_cnt = [0] * ndma
        self.n_sp = 8
        self.rr_sp = 0
        self.rr_pool = 0
        self.opclock = 0
        self.on_op = None
        self.last_use = None

    def _need(self, eng, tok, waits):
        if tok is None:
            return
        kind, key, val = tok
        if kind == 'e' and key == eng and eng == 'pe':
            return
        k = (kind, key)
        if self.known[eng].get(k, 0) >= val:
            return
        self.known[eng][k] = val
        waits[k] = max(waits.get(k, 0), val)

    def _deps(self, eng, reads, writes, waits):
        for r in reads:
            self._need(eng, r.last_w, waits)
        for w in writes:
            self._need(eng, w.last_w, waits)
            for t in w.reads:
                self._need(eng, t, waits)

    def _commit(self, tok, reads, writes):
        for r in reads:
            r.reads.append(tok)
            if len(r.reads) > 64:
                r.reads = r.reads[-48:]
        for w in writes:
            w.last_w = tok
            w.reads = []

    def op(self, eng, fn, reads=(), writes=()):
        self.opclock += 1
        if self.last_use is not None:
            for r in reads:
                w = getattr(r, "widx", None)
                if w is not None:
                    self.last_use[w] = self.opclock
        if self.on_op is not None:
            self.on_op()
        waits = {}
        self._deps(eng, reads, writes, waits)
        self.cnt[eng] += 1
        tok = ('e', eng, self.cnt[eng])
        self.ops[eng].append((list(waits.items()), fn, ('e', eng, 1)))
        self._commit(tok, reads, writes)
        return tok

    def dma(self, fn, reads=(), writes=(), eng="sp"):
        if eng == "sp":
            ch = self.rr_sp
            self.rr_sp = (self.rr_sp + 1) % self.n_sp
        else:
            ch = self.n_sp + self.rr_pool
            self.rr_pool = (self.rr_pool + 1) % (self.ndma - self.n_sp)
        waits = {}
        if self.dma_cnt[ch] > 0:
            self._need(eng, ('d', ch, self.dma_cnt[ch] * 16), waits)
        self._deps(eng, reads, writes, waits)
        self.dma_cnt[ch] += 1
        tok = ('d', ch, self.dma_cnt[ch] * 16)
        self.ops[eng].append((list(waits.items()), fn, ('d', ch, 16)))
        self._commit(tok, reads, writes)
        return tok

    def barrier(self):
        toks = [('e', e2, self.cnt[e2]) for e2 in self.ENG if self.cnt[e2] > 0]
        toks += [('d', i, self.dma_cnt[i] * 16) for i in range(self.ndma) if self.dma_cnt[i] > 0]
        for eng in self.ENG:
            waits = {}
            for t in toks:
                if t[0] == 'e' and t[1] == eng:
                    continue
                self._need(eng, t, waits)
            self.ops[eng].append((list(waits.items()), None, None))

    def finish(self, eng="sp"):
        waits = {}
        for i in range(self.ndma):
            if self.dma_cnt[i] > 0:
                self._need(eng, ('d', i, self.dma_cnt[i] * 16), waits)
        self.ops[eng].append((list(waits.items()), None, None))

    def emit(self, stack):
        nc = self.nc
        esem = {e: stack.enter_context(nc.semaphore("s_" + e)) for e in self.ENG}
        dsem = [stack.enter_context(nc.semaphore("d_%d" % i)) for i in range(self.ndma)]

        def semof(k):
            return esem[k[1]] if k[0] == 'e' else dsem[k[1]]

        block = stack.enter_context(nc.Block())

        def run(engname):
            def body(e):
                for waits, fn, inc in self.ops[engname]:
                    for k, v in waits:
                        e.wait_ge(semof(k), v)
                    if fn is not None:
                        ins = fn(e)
                        ins.then_inc(semof(inc), inc[2])
            return body
        block.tensor(run("pe"))
        block.scalar(run("act"))
        block.vector(run("dve"))
        block.gpsimd(run("pool"))
        block.sync(run("sp"))


class Tile:
    def __init__(self, h, name):
        self.h = h
        self.r = Res(name)

    def __getitem__(self, k):
        return self.h[k]


class Ring:
    def __init__(self, tiles):
        self.tiles = tiles
        self.i = 0

    def next(self):
        t = self.tiles[self.i % len(self.tiles)]
        self.i += 1
        return t


def build_program(n_layers, kstop=99, plan=None):
    nc = bass.Bass("TRN2", target_bir_lowering=False)

    def din(name, shape, dt=F32):
        return nc.dram_tensor(name, list(shape), dt, kind="ExternalInput").ap()

    xT_d = din("xT", [D, T])
    xhT_d = din("xhT", [D, T])
    xh2T_d = din("xh2T", [D, T])
    pos_d = din("pos", [1, 3 * T], I32)
    flag_d = din("flag", [128, 1])
    flag2_d = din("flag2", [128, 1])
    ident_d = din("ident", [128, 128])
    cmask_d = din("cmask", [128, 1024])
    esel_d = din("esel", [128, 16])
    selb_d = din("selb", [4, 256])
    ones_d = din("ones", [128, 128])
    tril_d = din("tril", [128, 128])
    invf_d = din("invf", [128, 1])
    W = []
    for l in range(n_layers):
        W.append(dict(
            w_in=din("w_in%d" % l, [D, N_IN]),
            convw=din("convw%d" % l, [128, NT * 3]),
            glng=din("glng%d" % l, [1, D]),
            glnb=din("glnb%d" % l, [1, D]),
            wsT=din("wsT%d" % l, [8, 128, 128]),
            bs=din("bs%d" % l, [1, D]),
            p_a=din("p_a%d" % l, [D, D]),
            p_b=din("p_b%d" % l, [512, D]),
            p_c=din("p_c%d" % l, [D, D]),
            w_o=din("w_o%d" % l, [D, D]),
            ln1g=din("ln1g%d" % l, [128, NT]),
            ln1b=din("ln1b%d" % l, [128, NT]),
            w_gate=din("w_gate%d" % l, [D, DFF]),
            w_up=din("w_up%d" % l, [D, DFF]),
            w_down=din("w_down%d" % l, [DFF, D]),
            ln2g=din("ln2g%d" % l, [128, NT]),
            ln2b=din("ln2b%d" % l, [128, NT]),
        ))
    out_d = nc.dram_tensor("out", [D, T], F32, kind="ExternalOutput").ap()
    x1_d = nc.dram_tensor("x1_scratch", [D, T], F32).ap()
    x1h_d = nc.dram_tensor("x1h_scratch", [D, T], F32).ap()

    S = Sched(nc, ndma=24)

    with ExitStack() as top:
        def sb(st, name, shape, dt):
            return Tile(st.enter_context(nc.sbuf_tensor("sb_" + name, list(shape), dt)), name)

        def psum(st, name, shape, dt=F32):
            return Tile(st.enter_context(nc.psum_tensor("ps_" + name, list(shape), dt)), name)

        y_b = sb(top, "y_b", [128, 4, T], BF16)
        ident = sb(top, "ident", [128, 128], BF16)
        cmask = sb(top, "cmask", [128, 1024], BF16)
        esel = sb(top, "esel", [128, 16], BF16)
        eselF = sb(top, "eselF", [128, 16], BF16)
        selb = sb(top, "selb", [4, 256], F32)
        ones = sb(top, "ones", [128, 128], BF16)
        tril = sb(top, "tril", [128, 128], F32)
        invf = sb(top, "invf", [128, 1], F32)
        flag = sb(top, "flag", [128, 1], F32)
        flag2 = sb(top, "flag2", [128, 1], F32)
        eselF2 = sb(top, "eselF2", [128, 16], BF16)
        zhist = sb(top, "zhist", [128, NT, 2], F32)
        wring = Ring([sb(top, "wbuf%d" % i, [128, 4096], BF16) for i in range(5)])
        tmpr = Ring([sb(top, "tmp%d" % i, [128, 512], F32) for i in range(8)])

        def dmac(out, in_, writes, reads=()):
            return S.dma(lambda e, o=out, i=in_: e.dma_start(out=o, in_=i), reads=reads, writes=writes, eng="pool")

        def dmas(out, in_, writes=(), reads=()):
            return S.dma(lambda e, o=out, i=in_: e.dma_start(out=o, in_=i), reads=reads, writes=writes, eng="sp")

        NBUF = len(wring.tiles)
        PF = 2
        wst = {"idx": 0, "next": 0}
        WAP = {}
        for Wl_ in W:
            for ap_ in Wl_.values():
                WAP[ap_.tensor.name] = ap_
        if plan is None:
            S.last_use = {}
            dry_seq = []

        wcache = {}
        wcount = {}
        if plan is not None:
            for key_ in plan["seq"]:
                wcount[key_] = wcount.get(key_, 0) + 1

        def _issue(k):
            key = plan["seq"][k]
            (nm, c0, width, K) = key
            kt = K // 128
            t = wring.tiles[k % NBUF]
            flat = t.h[:, 0:kt * width]
            view = flat.rearrange("p (k c) -> p k c", k=kt)
            if wcount[key] < 3:
                dmac(view, WAP[nm][:, c0:c0 + width].rearrange("(k p) c -> p k c", p=128), writes=[t.r])
            elif key not in wcache:
                dmac(view, WAP[nm][:, c0:c0 + width].rearrange("(k p) c -> p k c", p=128), writes=[t.r])
                sc = nc.dram_tensor("wcache_%d" % len(wcache), [128, kt * width], BF16).ap()
                r = Res("wc")
                wcache[key] = (sc, r)
                dmas(sc, flat, writes=[r], reads=[t.r])
            else:
                sc, r = wcache[key]
                dmas(flat, sc, writes=[t.r], reads=[r])

        def _issue_upto(limit):
            limit = min(limit, len(plan["seq"]) - 1)
            while wst["next"] <= limit:
                k = wst["next"]
                if k >= NBUF and S.opclock < plan["last_use"].get(k - NBUF, 0):
                    break
                _issue(k)
                wst["next"] += 1

        if plan is not None:
            S.on_op = lambda: _issue_upto(wst["idx"] - 1 + PF)

        def wload(w_ap, c0, width, K):
            kt = K // 128
            key = (w_ap.tensor.name, c0, width, K)
            idx = wst["idx"]
            wst["idx"] += 1
            if plan is None:
                dry_seq.append(key)
                t = wring.next()
                view = t.h[:, 0:kt * width].rearrange("p (k c) -> p k c", k=kt)
                r = Res("w%d" % idx)
                r.widx = idx
                return view, r
            assert plan["seq"][idx] == key, (idx, key, plan["seq"][idx])
            _issue_upto(idx + PF)
            assert wst["next"] > idx, "weight ring too small: block %d not issuable" % idx
            t = wring.tiles[idx % NBUF]
            view = t.h[:, 0:kt * width].rearrange("p (k c) -> p k c", k=kt)
            return view, t.r

        def mmgroup(out_ap, pairs, reads, writes, tp=None):
            n = len(pairs)

            def fn(e, out_ap=out_ap, pairs=pairs, tp=tp):
                ins = None
                for i, (l, r) in enumerate(pairs):
                    if tp is None:
                        ins = e.matmul(out_ap, lhsT=l, rhs=r, start=(i == 0), stop=(i == n - 1))
                    else:
                        ins = e.matmul(out_ap, lhsT=l, rhs=r, start=(i == 0), stop=(i == n - 1), tile_position=tp)
                return ins
            return S.op("pe", fn, reads=reads, writes=writes)

        def tt(eng, out, in0, in1, op, reads, writes):
            return S.op(eng, lambda e: e.tensor_tensor(out=out, in0=in0, in1=in1, op=op), reads=reads, writes=writes)

        def tsc(eng, out, in0, s1, s2, op0, op1, reads, writes):
            if op1 is None:
                return S.op(eng, lambda e: e.tensor_scalar(out=out, in0=in0, scalar1=s1, scalar2=None, op0=op0), reads=reads, writes=writes)
            return S.op(eng, lambda e: e.tensor_scalar(out=out, in0=in0, scalar1=s1, scalar2=s2, op0=op0, op1=op1), reads=reads, writes=writes)

        def stt(eng, out, in0, scalar, in1, op0, op1, reads, writes):
            return S.op(eng, lambda e: e.scalar_tensor_tensor(out=out, in0=in0, scalar=scalar, in1=in1, op0=op0, op1=op1), reads=reads, writes=writes)

        def cpy(eng, out, in_, reads, writes):
            return S.op(eng, lambda e: e.tensor_copy(out=out, in_=in_), reads=reads, writes=writes)

        def rcp(out, in_, reads, writes):
            return S.op("dve", lambda e: e.reciprocal(out=out, in_=in_), reads=reads, writes=writes)

        def act(out, in_, func, reads, writes, **kw):
            return S.op("act", lambda e: e.activation(out=out, in_=in_, func=func, **kw), reads=reads, writes=writes)

        dmac(ident[:], ident_d, [ident.r])
        dmac(cmask[:], cmask_d, [cmask.r])
        dmac(esel[:], esel_d, [esel.r])
        dmac(ones[:], ones_d, [ones.r])
        dmas(selb[:], selb_d, [selb.r])
        dmas(tril[:], tril_d, [tril.r])
        dmas(invf[:], invf_d, [invf.r])
        dmas(flag[:], flag_d, [flag.r])
        dmas(flag2[:], flag2_d, [flag2.r])
        S.op("dve", lambda e: e.tensor_scalar(out=eselF2[:], in0=esel[:], scalar1=flag2[:, 0:1], scalar2=None, op0=ALU.mult),
             reads=[esel.r, flag2.r], writes=[eselF2.r])
        S.op("dve", lambda e: e.tensor_scalar(out=eselF[:], in0=esel[:], scalar1=flag[:, 0:1], scalar2=None, op0=ALU.mult),
             reads=[esel.r, flag.r], writes=[eselF.r])

        def rope_tables(p0, n, cos_ap, sin_ap, cos_r, sin_r, posi):
            posf, ta, tb = tmpr.next(), tmpr.next(), tmpr.next()
            dmas(posi[:, 0:n], pos_d[0:1, p0:p0 + n].partition_broadcast(128), writes=[posi.r])
            S.op("dve", lambda e: e.tensor_copy(out=posf[:, 0:n], in_=posi[:, 0:n]), reads=[posi.r], writes=[posf.r])
            S.op("dve", lambda e: e.tensor_scalar(out=posf[:, 0:n], in0=posf[:, 0:n], scalar1=invf[:, 0:1], scalar2=None, op0=ALU.mult),
                 reads=[posf.r, invf.r], writes=[posf.r])
            S.op("dve", lambda e: e.tensor_scalar(out=ta[:, 0:n], in0=posf[:, 0:n], scalar1=1.0 / TWO_PI, scalar2=MAGIC, op0=ALU.mult, op1=ALU.add),
                 reads=[posf.r], writes=[ta.r])
            S.op("dve", lambda e: e.tensor_scalar(out=ta[:, 0:n], in0=ta[:, 0:n], scalar1=MAGIC, scalar2=None, op0=ALU.subtract),
                 reads=[ta.r], writes=[ta.r])
            S.op("dve", lambda e: e.scalar_tensor_tensor(out=posf[:, 0:n], in0=ta[:, 0:n], scalar=-C1, in1=posf[:, 0:n], op0=ALU.mult, op1=ALU.add),
                 reads=[ta.r, posf.r], writes=[posf.r])
            S.op("dve", lambda e: e.scalar_tensor_tensor(out=posf[:, 0:n], in0=ta[:, 0:n], scalar=-C2, in1=posf[:, 0:n], op0=ALU.mult, op1=ALU.add),
                 reads=[ta.r, posf.r], writes=[posf.r])
            S.op("dve", lambda e: e.tensor_scalar(out=posf[:, 0:n], in0=posf[:, 0:n], scalar1=-3.1415925, scalar2=3.1415925, op0=ALU.max, op1=ALU.min),
                 reads=[posf.r], writes=[posf.r])
            act(sin_ap, posf[:, 0:n], AF.Sin, [posf.r], [sin_r])
            S.op("dve", lambda e: e.scalar_tensor_tensor(out=tb[:, 0:n], in0=posf[:, 0:n], scalar=-1.0, in1=posf[:, 0:n], op0=ALU.mult, op1=ALU.max),
                 reads=[posf.r], writes=[tb.r])
            S.op("dve", lambda e: e.tensor_scalar(out=tb[:, 0:n], in0=tb[:, 0:n], scalar1=-1.0, scalar2=math.pi / 2, op0=ALU.mult, op1=ALU.add),
                 reads=[tb.r], writes=[tb.r])
            act(cos_ap, tb[:, 0:n], AF.Sin, [tb.r], [cos_r])

        xin_r = Res("xin")
        x1_r = Res("x1")
        x1h_r = Res("x1h")
        if n_layers == 1:
            passes = [(0, xT_d, xhT_d, T, flag, eselF, out_d, xin_r, xin_r, None)]
        else:
            passes = [
                (0, xhT_d, xh2T_d, 0, flag2, eselF2, x1h_d, xin_r, xin_r, x1h_r),
                (0, xT_d, xhT_d, T, flag, eselF, x1_d, xin_r, xin_r, x1_r),
                (1, x1_d, x1h_d, T, flag, eselF, out_d, x1_r, x1h_r, None),
            ]
        for pi, (l, x_src, xh_src, pos0, flag_t, eselF_t, out_ap, xsrc_r, xhsrc_r, out_r) in enumerate(passes):
            Wl = W[l]
            w_in = Wl["w_in"]
            with ExitStack() as pa:
                bring = Ring([psum(pa, "bankA%d_%d" % (i, pi), [128, 512]) for i in range(7)])
                bank_t = psum(pa, "bank_t_%d" % pi, [128, 1024], BF16)
                cos_o = sb(pa, "cos_o" + "_%d" % pi, [128, T], F32)
                sin_o = sb(pa, "sin_o" + "_%d" % pi, [128, T], F32)
                cos_h = sb(pa, "cos_h" + "_%d" % pi, [128, SP], F32)
                sin_h = sb(pa, "sin_h" + "_%d" % pi, [128, SP], F32)
                posi = sb(pa, "posi" + "_%d" % pi, [128, SP], I32)
                acc = sb(pa, "acc" + "_%d" % pi, [128, 2, T], F32)
                accd = sb(pa, "accd" + "_%d" % pi, [4, T], F32)
                KT = sb(pa, "KT" + "_%d" % pi, [128, 2, 2 * T], BF16)
                VT = sb(pa, "VT" + "_%d" % pi, [128, 2, 2 * T], BF16)
                QT = sb(pa, "QT" + "_%d" % pi, [128, 2, T], BF16)
                vbr = Ring([sb(pa, "vb%d_%d" % (i, pi), [128, 256], BF16) for i in range(6)])
                P_t = Ring([sb(pa, "P%d_%d" % (i, pi), [128, 1024], BF16) for i in range(3)])
                xcr = Ring([sb(pa, "xca%d_%d" % (i, pi), [128, NT, SP], BF16) for i in range(2)])
                for s in range(NS):
                    rope_tables(pos0 + T + s * SP, SP, cos_o[:, s * SP:(s + 1) * SP], sin_o[:, s * SP:(s + 1) * SP], cos_o.r, sin_o.r, posi)

                for qd in range(2):
                    if kstop <= 1:
                        break
                    S.op("pool", lambda e, acc=acc: e.memset(acc[:], 0.0), writes=[acc.r])
                    S.op("pool", lambda e, accd=accd: e.memset(accd[:], 0.0), writes=[accd.r])
                    for g, dil in enumerate(DILS):
                        HL = 128 * dil
                        base = ATT0 + g * 1536
                        wq_v, wq_r = wload(w_in, base + qd * 256, 256, D)
                        wk_v, wk_r = wload(w_in, base + 512 + qd * 256, 256, D)
                        wv_v, wv_r = wload(w_in, base + 1024 + qd * 256, 256, D)
                        chunks = []
                        hs0 = T - HL
                        n_h = min(HL, SP)
                        for a in range(hs0, T, n_h):
                            chunks.append((True, a, n_h, a - hs0))
                        for s in range(NS):
                            chunks.append((False, s * SP, SP, HL + s * SP))
                        loaded = {}

                        def issue_chunk(ci, chunks=chunks, loaded=loaded):
                            (is_h_, t0_, n_, c0_) = chunks[ci]
                            xc_ = xcr.next()
                            src_ = xh_src if is_h_ else x_src
                            dmac(xc_[:, :, 0:n_], src_[:, t0_:t0_ + n_].rearrange("(k p) t -> p k t", p=128), [xc_.r], reads=[xhsrc_r if is_h_ else xsrc_r])
                            loaded[ci] = xc_
                        issue_chunk(0)
                        for ci, (is_h, t0, n, c0) in enumerate(chunks):
                            if ci + 1 < len(chunks):
                                issue_chunk(ci + 1)
                            xc = loaded[ci]
                            if is_h:
                                rope_tables(pos0 + t0, n, cos_h[:, 0:n], sin_h[:, 0:n], cos_h.r, sin_h.r, posi)
                                cs_ap, sn_ap, cs_r, sn_r = cos_h[:, 0:n], sin_h[:, 0:n], cos_h.r, sin_h.r
                            else:
                                cs_ap, sn_ap, cs_r, sn_r = cos_o[:, t0:t0 + n], sin_o[:, t0:t0 + n], cos_o.r, sin_o.r
                            todo = [(wk_v, wk_r, KT, c0)]
                            if not is_h:
                                todo.append((wq_v, wq_r, QT, t0))
                            for (wt, wr, dst, dc0) in todo:
                                pA = bring.next()
                                pB = bring.next()
                                mmgroup(pA[:, 0:n], [(wt[:, k, 0:128], xc[:, k, 0:n]) for k in range(NT)], [wr, xc.r], [pA.r])
                                mmgroup(pB[:, 0:n], [(wt[:, k, 128:256], xc[:, k, 0:n]) for k in range(NT)], [wr, xc.r], [pB.r])
                                t1, t2, t3, t4 = tmpr.next(), tmpr.next(), tmpr.next(), tmpr.next()
                                tt("dve", t1[:, 0:n], pA[:, 0:n], cs_ap, ALU.mult, [pA.r, cs_r], [t1.r])
                                tt("dve", t2[:, 0:n], pB[:, 0:n], sn_ap, ALU.mult, [pB.r, sn_r], [t2.r])
                                tt("pool", dst[:, 0, dc0:dc0 + n], t1[:, 0:n], t2[:, 0:n], ALU.subtract, [t1.r, t2.r], [dst.r])
                                tt("dve", t3[:, 0:n], pB[:, 0:n], cs_ap, ALU.mult, [pB.r, cs_r], [t3.r])
                                tt("dve", t4[:, 0:n], pA[:, 0:n], sn_ap, ALU.mult, [pA.r, sn_r], [t4.r])
                                tt("pool", dst[:, 1, dc0:dc0 + n], t3[:, 0:n], t4[:, 0:n], ALU.add, [t3.r, t4.r], [dst.r])
                            for vt in range(2):
                                pV = bring.next()
                                mmgroup(pV[:, 0:n], [(wv_v[:, k, vt * 128:(vt + 1) * 128], xc[:, k, 0:n]) for k in range(NT)], [wv_r, xc.r], [pV.r])
                                act(VT[:, vt, c0:c0 + n], pV[:, 0:n], AF.Copy, [pV.r], [VT.r])
                        if kstop <= 2:
                            break
                        nb = T // (128 * dil)
                        Wd = HL + T

                        def kview(tl, lo, hi, ab, m, r, dil=dil, Wd=Wd):
                            return tl.h[lo:hi, ab, 0:Wd].rearrange("p (m i r) -> p m r i", i=128, r=dil)[:, m, r, :]

                        def qview(tl, lo, hi, ab, m, r, dil=dil):
                            return tl.h[lo:hi, ab, 0:T].rearrange("p (m i r) -> p m r i", i=128, r=dil)[:, m, r, :]

                        pending = [None]
                        for r in range(dil):
                            vprev = None
                            for m in range(nb + 1):
                                vb = vbr.next()

                                v0_ap = kview(VT, 0, 128, 0, m, r)
                                v1_ap = kview(VT, 0, 128, 1, m, r)

                                def tr2(e, v0_ap=v0_ap, v1_ap=v1_ap, o0=bank_t[:, 0:128], o1=bank_t[:, 128:256], idn=ident[:]):
                                    e.transpose(out=o0, in_=v0_ap, identity=idn)
                                    return e.transpose(out=o1, in_=v1_ap, identity=idn)
                                S.op("pe", tr2, reads=[VT.r, ident.r], writes=[bank_t.r])
                                if m == 0:
                                    act(vb[:], bank_t[:, 0:256], AF.Copy, [bank_t.r, flag_t.r], [vb.r], scale=flag_t[:, 0:1])
                                    vprev = vb
                                    continue
                                act(vb[:], bank_t[:, 0:256], AF.Copy, [bank_t.r], [vb.r])
                                n_q = m - 1
                                sbks = [bring.next() for _ in range(4)]
                                for hs in range(4):
                                    sbk = sbks[hs]
                                    lo, hi = 32 * hs, 32 * hs + 32
                                    for kt in range(2):
                                        o0 = kt * 128
                                        mmgroup(sbk[:, o0:o0 + 128],
                                                [(kview(KT, lo, hi, 0, n_q + kt, r), qview(QT, lo, hi, 0, n_q, r)),
                                                 (kview(KT, lo, hi, 1, n_q + kt, r), qview(QT, lo, hi, 1, n_q, r))],
                                                [KT.r, QT.r], [sbk.r], tp=(32 * hs, 0))
                                P = P_t.next()
                                for hs in range(4):
                                    act(P[:, hs * 256:(hs + 1) * 256], sbks[hs][:, 0:256], AF.Exp, [sbks[hs].r], [P.r], scale=0.125)
                                tt("pool", P[:], P[:], cmask[:], ALU.mult, [P.r, cmask.r], [P.r])
                                def pv_stage(vprev=vprev, vb=vb, P=P, m=m, n_q=n_q, r=r, dil=dil, acc=acc, accd=accd):
                                    nd = bring.next()
                                    vbs = (vprev, vb)
                                    for pr in range(2):
                                        for hh in range(2):
                                            hs = 2 * pr + hh
                                            mmgroup(nd[64 * hh:64 * hh + 64, pr * 128:(pr + 1) * 128],
                                                    [(vbs[kt][:, hs * 64:(hs + 1) * 64], P[:, hs * 256 + kt * 128: hs * 256 + kt * 128 + 128]) for kt in range(2)],
                                                    [vprev.r, vb.r, P.r], [nd.r], tp=(0, 64 * hh))
                                    es_prev = eselF_t if m == 1 else esel
                                    pairs = []
                                    for hs in range(4):
                                        pairs.append((es_prev[:, hs * 4:(hs + 1) * 4], P[:, hs * 256: hs * 256 + 128]))
                                        pairs.append((esel[:, hs * 4:(hs + 1) * 4], P[:, hs * 256 + 128: hs * 256 + 256]))
                                    mmgroup(nd[0:4, 256:384], pairs, [P.r, esel.r, eselF_t.r], [nd.r])
                                    accv = acc.h[:, :, :].rearrange("p a (m i r) -> p a m r i", i=128, r=dil)[:, :, n_q, r, :]
                                    tt("dve", accv, accv, nd[:, 0:256].rearrange("p (a i) -> p a i", a=2), ALU.add, [nd.r, acc.r], [acc.r])
                                    adv = accd.h[:, :].rearrange("p (m i r) -> p m r i", i=128, r=dil)[:, n_q, r, :]
                                    tt("dve", adv, adv, nd[0:4, 256:384], ALU.add, [nd.r, accd.r], [accd.r])
                                if pending[0] is not None:
                                    pending[0]()
                                pending[0] = pv_stage
                                vprev = vb
                        if pending[0] is not None:
                            pending[0]()
                            pending[0] = None
                    if kstop <= 3:
                        break
                    rcp(accd[:], accd[:], [accd.r], [accd.r])
                    if kstop <= 4:
                        break
                    for pr in range(2):
                        for s in range(NS):
                            bc = bring.next()
                            mmgroup(bc[:, :], [(selb[:, pr * 128:(pr + 1) * 128], accd[:, s * SP:(s + 1) * SP])], [selb.r, accd.r], [bc.r])
                            tt("dve", y_b[:, 2 * qd + pr, s * SP:(s + 1) * SP], acc[:, pr, s * SP:(s + 1) * SP], bc[:, :], ALU.mult,
                               [bc.r, acc.r], [y_b.r])

                S.barrier()
            if kstop <= 5:
                break
            with ExitStack() as pb:
                bring = Ring([psum(pb, "bankB%d_%d" % (i, pi), [128, 512]) for i in range(6)])
                stat_banks = [psum(pb, "bankS%d_%d" % (i, pi), [128, 512]) for i in range(2)]
                xs = sb(pb, "xs" + "_%d" % pi, [128, NT, SP], F32)
                xc = sb(pb, "xcb" + "_%d" % pi, [128, NT, SP], BF16)
                vfm = sb(pb, "vfm" + "_%d" % pi, [128, 4 * D], F32)
                u_sb = sb(pb, "u_sb" + "_%d" % pi, [128, NT, SP], BF16)
                y_a = sb(pb, "y_a" + "_%d" % pi, [128, NT, SP], BF16)
                y_c = sb(pb, "y_c" + "_%d" % pi, [128, NT, SP], BF16)
                vbf = [sb(pb, "vbf%d_%d" % (i, pi), [128, D], BF16) for i in range(4)]
                m_sb = sb(pb, "m_sb" + "_%d" % pi, [128, NT, SP], BF16)
                h_sb = sb(pb, "h_sb" + "_%d" % pi, [128, NF, SP], BF16)
                zbuf = sb(pb, "zbuf" + "_%d" % pi, [128, SP + 2], F32)
                convw = sb(pb, "convw" + "_%d" % pi, [128, NT * 3], F32)
                glng = sb(pb, "glng" + "_%d" % pi, [128, D], F32)
                glnb = sb(pb, "glnb" + "_%d" % pi, [128, D], F32)
                bsb = sb(pb, "bsb" + "_%d" % pi, [128, D], F32)
                wsf = sb(pb, "wsf" + "_%d" % pi, [128, 8, 128], F32)
                wsm = sb(pb, "wsm" + "_%d" % pi, [128, 8, 128], BF16)
                ln1g = sb(pb, "ln1g" + "_%d" % pi, [128, NT], F32)
                ln1b = sb(pb, "ln1b" + "_%d" % pi, [128, NT], F32)
                ln2g = sb(pb, "ln2g" + "_%d" % pi, [128, NT], F32)
                ln2b = sb(pb, "ln2b" + "_%d" % pi, [128, NT], F32)
                st6 = sb(pb, "st6" + "_%d" % pi, [128, 2, 6], F32)
                mv = sb(pb, "mv" + "_%d" % pi, [128, 2], F32)
                rstd1 = sb(pb, "rstd1" + "_%d" % pi, [128, 1], F32)
                xh16 = sb(pb, "xh16" + "_%d" % pi, [128, NT, 16], BF16)
                mean_t = sb(pb, "mean_t" + "_%d" % pi, [128, SP], F32)
                rstd_t = sb(pb, "rstd_t" + "_%d" % pi, [128, SP], F32)

                dmas(convw[:], Wl["convw"], [convw.r])
                dmas(glng[:], Wl["glng"].partition_broadcast(128), [glng.r])
                dmas(glnb[:], Wl["glnb"].partition_broadcast(128), [glnb.r])
                dmas(bsb[:], Wl["bs"].partition_broadcast(128), [bsb.r])
                dmas(wsf[:], Wl["wsT"].rearrange("g j i -> j g i"), [wsf.r])
                dmas(ln1g[:], Wl["ln1g"], [ln1g.r])
                dmas(ln1b[:], Wl["ln1b"], [ln1b.r])
                dmas(ln2g[:], Wl["ln2g"], [ln2g.r])
                dmas(ln2b[:], Wl["ln2b"], [ln2b.r])
                tt("dve", wsm[:], wsf[:], tril[:].unsqueeze(1).to_broadcast([128, 8, 128]), ALU.mult, [wsf.r, tril.r], [wsm.r])
                dmac(xh16[:], xh_src[:, T - 16:T].rearrange("(k p) t -> p k t", p=128), [xh16.r], reads=[xhsrc_r])
                for jb in range(2):
                    wC, wCr = wload(w_in, CC + jb * 512, 512, D)
                    wH, wHr = wload(w_in, CH + jb * 512, 512, D)
                    for jj in range(4):
                        j = jb * 4 + jj
                        pC = bring.next()
                        pH = bring.next()
                        mmgroup(pC[:, 0:2], [(wC[:, k, jj * 128:(jj + 1) * 128], xh16[:, k, 14:16]) for k in range(NT)], [wCr, xh16.r], [pC.r])
                        mmgroup(pH[:, 0:2], [(wH[:, k, jj * 128:(jj + 1) * 128], xh16[:, k, 14:16]) for k in range(NT)], [wHr, xh16.r], [pH.r])
                        tz = tmpr.next()
                        act(tz[:, 0:2], pC[:, 0:2], AF.Copy, [pC.r, flag_t.r], [tz.r], scale=flag_t[:, 0:1])
                        tt("dve", zhist[:, j, :], tz[:, 0:2], pH[:, 0:2], ALU.mult, [tz.r, pH.r], [zhist.r])

                class LNStats:
                    def __init__(self):
                        self.rb, self.rsq = y_a, y_c
                        self.pm = stat_banks[0]
                        self.pq = stat_banks[1]
                        self.lag = None

                    def _mm(self, j):
                        rb, rsq, pm, pq = self.rb, self.rsq, self.pm, self.pq
                        S.op("pe", lambda e: e.matmul(pm[:, :], lhsT=ones[:], rhs=rb[:, j, :], start=(j == 0), stop=(j == NT - 1)),
                             reads=[ones.r, rb.r], writes=[pm.r])
                        S.op("pe", lambda e: e.matmul(pq[:, :], lhsT=ones[:], rhs=rsq[:, j, :], start=(j == 0), stop=(j == NT - 1)),
                             reads=[ones.r, rsq.r], writes=[pq.r])

                    def tile_done(self, j):
                        act(self.rb[:, j, :], xs[:, j, :], AF.Identity, [xs.r], [self.rb.r])
                        tt("dve", self.rsq[:, j, :], xs[:, j, :], xs[:, j, :], ALU.mult, [xs.r], [self.rsq.r])
                        if self.lag is not None:
                            self._mm(self.lag)
                        self.lag = j

                    def finish(self, g_t, b_t, write_xc=True):
                        self._mm(self.lag)
                        pm, pq = self.pm, self.pq
                        msq = tmpr.next()
                        tsc("dve", mean_t[:], pm[:, :], 1.0 / D, None, ALU.mult, None, [pm.r], [mean_t.r])
                        tt("dve", msq[:], mean_t[:], mean_t[:], ALU.mult, [mean_t.r], [msq.r])
                        stt("dve", msq[:], pq[:, :], 1.0 / D, msq[:], ALU.mult, ALU.subtract, [pq.r, msq.r], [msq.r])
                        tsc("dve", msq[:], msq[:], EPS, None, ALU.add, None, [msq.r], [msq.r])
                        act(msq[:], msq[:], AF.Sqrt, [msq.r], [msq.r])
                        rcp(rstd_t[:], msq[:], [msq.r], [rstd_t.r])
                        ts_ = []
                        for j in range(NT):
                            t = tmpr.next()
                            eng = "dve" if j % 2 == 0 else "pool"
                            tt(eng, t[:], xs[:, j, :], mean_t[:], ALU.subtract, [xs.r, mean_t.r], [t.r])
                            tt(eng, t[:], t[:], rstd_t[:], ALU.mult, [t.r, rstd_t.r], [t.r])
                            if write_xc:
                                act(xc[:, j, :], t[:], AF.Identity, [t.r, g_t.r, b_t.r], [xc.r], scale=g_t[:, j:j + 1], bias=b_t[:, j:j + 1])
                            ts_.append(t)
                        for j in range(NT):
                            t = ts_[j]
                            act(xs[:, j, :], t[:], AF.Identity, [t.r, g_t.r, b_t.r], [xs.r], scale=g_t[:, j:j + 1], bias=b_t[:, j:j + 1])

                mf = lambda j: vfm[:, j * SP:(j + 1) * SP]
                vf = lambda t_: vfm[:, t_ * D:(t_ + 1) * D]

                for s in range(NS):
                    c0 = s * SP
                    vfm3 = vfm.h[:, :].rearrange("p (k t) -> p k t", k=NT)
                    if s == 0:
                        dmac(xs[:, :, :], x_src[:, c0:c0 + SP].rearrange("(k p) t -> p k t", p=128), [xs.r], reads=[xsrc_r])
                        act(xc[:, :, :], xs[:, :, :], AF.Copy, [xs.r], [xc.r])
                    else:
                        cpy("pool", xs[:, :, :], vfm3, [vfm.r], [xs.r])
                    for jb in range(2):
                        wB, wBr = wload(w_in, CB + jb * 512, 512, D)
                        wC, wCr = wload(w_in, CC + jb * 512, 512, D)
                        wH, wHr = wload(w_in, CH + jb * 512, 512, D)
                        for jj in range(4):
                            j = jb * 4 + jj
                            pB_, pC, pH = bring.next(), bring.next(), bring.next()
                            mmgroup(pC[:, :], [(wC[:, k, jj * 128:(jj + 1) * 128], xc[:, k, :]) for k in range(NT)], [wCr, xc.r], [pC.r])
                            mmgroup(pH[:, :], [(wH[:, k, jj * 128:(jj + 1) * 128], xc[:, k, :]) for k in range(NT)], [wHr, xc.r], [pH.r])
                            mmgroup(pB_[:, :], [(wB[:, k, jj * 128:(jj + 1) * 128], xc[:, k, :]) for k in range(NT)], [wBr, xc.r], [pB_.r])
                            tc_ = tmpr.next()
                            ta_ = tmpr.next()
                            act(tc_[:], pC[:, :], AF.Copy, [pC.r], [tc_.r])
                            act(zbuf[:, 0:2], zhist[:, j, :], AF.Copy, [zhist.r], [zbuf.r])
                            tt("dve", zbuf[:, 2:SP + 2], tc_[:], pH[:, :], ALU.mult, [tc_.r, pH.r], [zbuf.r])
                            act(zhist[:, j, :], zbuf[:, SP:SP + 2], AF.Copy, [zbuf.r], [zhist.r])
                            tsc("dve", ta_[:], zbuf[:, 0:SP], convw[:, 3 * j:3 * j + 1], None, ALU.mult, None, [zbuf.r, convw.r], [ta_.r])
                            stt("dve", ta_[:], zbuf[:, 1:SP + 1], convw[:, 3 * j + 1:3 * j + 2], ta_[:], ALU.mult, ALU.add, [zbuf.r, convw.r, ta_.r], [ta_.r])
                            stt("dve", ta_[:], zbuf[:, 2:SP + 2], convw[:, 3 * j + 2:3 * j + 3], ta_[:], ALU.mult, ALU.add, [zbuf.r, convw.r, ta_.r], [ta_.r])
                            tt("dve", y_a[:, j, :], ta_[:], pB_[:, :], ALU.mult, [ta_.r, pB_.r], [y_a.r])
                    for jb in range(2):
                        wU, wUr = wload(w_in, UO + jb * 512, 512, D)
                        for jj in range(4):
                            j = jb * 4 + jj
                            pU = bring.next()
                            mmgroup(pU[:, :], [(wU[:, k, jj * 128:(jj + 1) * 128], xc[:, k, :]) for k in range(NT)], [wUr, xc.r], [pU.r])
                            act(u_sb[:, j, :], pU[:, :], AF.Gelu, [pU.r], [u_sb.r])
                    for half in range(2):
                        wV, wVr = wload(w_in, VO + half * 512, 512, D)
                        for t_ in range(4):
                            pV = bring.next()
                            mmgroup(pV[:, :], [(xc[:, k, t_ * 128:(t_ + 1) * 128], wV[:, k, :]) for k in range(NT)], [wVr, xc.r], [pV.r])
                            act(vfm[:, t_ * D + half * 512: t_ * D + (half + 1) * 512], pV[:, :], AF.Gelu, [pV.r], [vfm.r])
                    for t_ in range(4):
                        v = vf(t_)
                        S.op("dve", lambda e, o_=st6[:, 0, :], i_=v[:, 0:512]: e.bn_stats(out=o_, in_=i_), reads=[vfm.r], writes=[st6.r])
                        S.op("dve", lambda e, o_=st6[:, 1, :], i_=v[:, 512:1024]: e.bn_stats(out=o_, in_=i_), reads=[vfm.r, st6.r], writes=[st6.r])
                        S.op("dve", lambda e, o_=mv[:], i_=st6[:].rearrange("p a b -> p (a b)"): e.bn_aggr(out=o_, in_=i_), reads=[st6.r], writes=[mv.r])
                        tsc("dve", rstd1[:], mv[:, 1:2], EPS, None, ALU.add, None, [mv.r], [rstd1.r])
                        act(rstd1[:], rstd1[:], AF.Sqrt, [rstd1.r], [rstd1.r])
                        rcp(rstd1[:], rstd1[:], [rstd1.r], [rstd1.r])
                        tsc("dve", v, v, mv[:, 0:1], rstd1[:, 0:1], ALU.subtract, ALU.mult, [vfm.r, mv.r, rstd1.r], [vfm.r])
                        tt("dve", v, v, glng[:], ALU.mult, [vfm.r, glng.r], [vfm.r])
                        tt("dve", vbf[t_][:], v, glnb[:], ALU.add, [vfm.r, glnb.r], [vbf[t_].r])
                    def spatial_stage():
                        for gg in range(8):
                            pS = bring.next()
                            for t_ in range(4):
                                mmgroup(pS[:, t_ * 128:(t_ + 1) * 128], [(vbf[t_][:, gg * 128:(gg + 1) * 128], wsm[:, gg, :])], [vbf[t_].r, wsm.r], [pS.r])
                            tq = tmpr.next()
                            tt("dve", tq[:].rearrange("p (c i) -> p c i", c=4), pS[:, :].rearrange("p (c i) -> p c i", c=4),
                               bsb[:, gg * 128:(gg + 1) * 128].unsqueeze(1).to_broadcast([128, 4, 128]), ALU.add, [pS.r, bsb.r], [tq.r])
                            tt("dve", y_c[:, gg, :], tq[:], u_sb[:, gg, :], ALU.mult, [tq.r, u_sb.r], [y_c.r])
                    for bi, (gcol, p_ap, K, y_t) in enumerate(((GA, Wl["p_a"], D, y_a), (GB, Wl["p_b"], 512, y_b), (GC, Wl["p_c"], D, y_c))):
                        kt = K // 128
                        if bi == 2:
                            spatial_stage()
                        for jb in range(2):
                            wG_, wGr_ = wload(w_in, gcol + jb * 512, 512, D)
                            wP_, wPr_ = wload(p_ap, jb * 512, 512, K)
                            for jj in range(4):
                                j = jb * 4 + jj
                                pg, py = bring.next(), bring.next()
                                mmgroup(pg[:, :], [(wG_[:, k, jj * 128:(jj + 1) * 128], xc[:, k, :]) for k in range(NT)], [wGr_, xc.r], [pg.r])
                                if bi == 1:
                                    prs = [(wP_[:, k, jj * 128:(jj + 1) * 128], y_b[:, k, c0:c0 + SP]) for k in range(kt)]
                                else:
                                    prs = [(wP_[:, k, jj * 128:(jj + 1) * 128], y_t[:, k, :]) for k in range(kt)]
                                mmgroup(py[:, :], prs, [wPr_, y_t.r], [py.r])
                                sg = tmpr.next()
                                act(sg[:], pg[:, :], AF.Sigmoid, [pg.r], [sg.r])
                                if bi == 0:
                                    tt("dve", mf(j), sg[:], py[:, :], ALU.mult, [sg.r, py.r], [vfm.r])
                                elif bi == 1:
                                    tt("dve", sg[:], sg[:], py[:, :], ALU.mult, [sg.r, py.r], [sg.r])
                                    tt("dve", mf(j), mf(j), sg[:], ALU.add, [sg.r, vfm.r], [vfm.r])
                                else:
                                    tt("dve", sg[:], sg[:], py[:, :], ALU.mult, [sg.r, py.r], [sg.r])
                                    tt("dve", m_sb[:, j, :], mf(j), sg[:], ALU.add, [sg.r, vfm.r], [m_sb.r])
                    if s + 1 < NS:
                        dmac(vfm3, x_src[:, c0 + SP:c0 + 2 * SP].rearrange("(k p) t -> p k t", p=128), [vfm.r], reads=[xsrc_r])
                    lns = LNStats()
                    for jb in range(2):
                        wO, wOr = wload(Wl["w_o"], jb * 512, 512, D)
                        for jj in range(4):
                            j = jb * 4 + jj
                            po = bring.next()
                            mmgroup(po[:, :], [(wO[:, k, jj * 128:(jj + 1) * 128], m_sb[:, k, :]) for k in range(NT)], [wOr, m_sb.r], [po.r])
                            stt("dve", xs[:, j, :], xs[:, j, :], ALPHA, po[:, :], ALU.mult, ALU.add, [po.r, xs.r], [xs.r])
                            lns.tile_done(j)
                    lns.finish(ln1g, ln1b)
                    for fb in range(NF // 2):
                        wG_, wGr_ = wload(Wl["w_gate"], fb * 256, 256, D)
                        wU_, wUr_ = wload(Wl["w_up"], fb * 256, 256, D)
                        for ff in range(2):
                            f = fb * 2 + ff
                            pg, pu = bring.next(), bring.next()
                            mmgroup(pg[:, :], [(wG_[:, k, ff * 128:(ff + 1) * 128], xc[:, k, :]) for k in range(NT)], [wGr_, xc.r], [pg.r])
                            mmgroup(pu[:, :], [(wU_[:, k, ff * 128:(ff + 1) * 128], xc[:, k, :]) for k in range(NT)], [wUr_, xc.r], [pu.r])
                            sg = tmpr.next()
                            act(sg[:], pg[:, :], AF.Silu, [pg.r], [sg.r])
                            tt("dve", h_sb[:, f, :], sg[:], pu[:, :], ALU.mult, [sg.r, pu.r], [h_sb.r])
                    if s + 1 < NS:
                        act(xc[:, :, :], vfm3, AF.Identity, [vfm.r], [xc.r])
                    lns = LNStats()
                    for j in range(NT):
                        wD, wDr = wload(Wl["w_down"], j * 128, 128, DFF)
                        pd = bring.next()
                        mmgroup(pd[:, :], [(wD[:, k, :], h_sb[:, k, :]) for k in range(NF)], [wDr, h_sb.r], [pd.r])
                        stt("dve", xs[:, j, :], xs[:, j, :], ALPHA, pd[:, :], ALU.mult, ALU.add, [pd.r, xs.r], [xs.r])
                        lns.tile_done(j)
                    lns.finish(ln2g, ln2b, write_xc=False)
                    dmac(out_ap[:, c0:c0 + SP].rearrange("(k p) t -> p k t", p=128), xs[:, :, :], reads=[xs.r], writes=([out_r] if out_r is not None else []))
                S.barrier()
        if plan is None:
            return {"seq": dry_seq, "last_use": dict(S.last_use)}
        S.finish("sp")
        S.emit(top)
    return nc


def _consts():
    ident = np.eye(128, dtype=np.float32)
    k = np.arange(128)[:, None]
    q = np.arange(128)[None, :]
    prev = (k >= q).astype(np.float32)
    cur = (k <= q).astype(np.float32)
    cm = np.concatenate([prev, cur], axis=1)
    cmask = np.tile(cm, (1, 4))
    esel = np.zeros((128, 16), np.float32)
    for hs in range(4):
        esel[:, hs * 4 + hs] = 1.0
    selb = np.zeros((4, 256), np.float32)
    for pr in range(2):
        for col in range(128):
            selb[2 * pr + col // 64, pr * 128 + col] = 1.0
    ones = np.ones((128, 128), np.float32)
    tril = cur.copy()
    half = 32
    inv_freq = (np.float32(10000.0) ** (-np.arange(half, dtype=np.float32) / np.float32(half))).astype(np.float32)
    invf = np.tile(inv_freq, 4).reshape(128, 1).astype(np.float32)
    return dict(ident=ident, cmask=cmask, esel=esel, selb=selb, ones=ones, tril=tril, invf=invf)


def _qk_perm():
    idx = []
    for qd in range(2):
        for ab in range(2):
            for hs in range(4):
                h = 4 * qd + hs
                idx.extend(range(h * 64 + ab * 32, h * 64 + ab * 32 + 32))
    return np.array(idx)


def _layer_weights(l, w_in, conv_w, gmlp_ln_g, gmlp_ln_b, w_s, b_s, p_a, p_b, p_c, w_o, ln1_g, ln1_b,
                   w_gate, w_up, w_down, ln2_g, ln2_b, suffix):
    perm = _qk_perm()
    wi = np.array(w_in[l], dtype=np.float32, copy=True)
    for g in range(3):
        base = ATT0 + g * 1536
        wi[:, base:base + 512] = w_in[l][:, base + perm]
        wi[:, base + 512:base + 1024] = w_in[l][:, base + 512 + perm]

    def pj(v):
        return np.ascontiguousarray(np.asarray(v, np.float32).reshape(NT, 128).T)
    cw = np.asarray(conv_w[l], np.float32)
    convw = np.ascontiguousarray(cw.reshape(3, NT, 128).transpose(2, 1, 0).reshape(128, NT * 3))
    d = {
        "w_in": wi,
        "convw": convw,
        "glng": np.asarray(gmlp_ln_g[l], np.float32).reshape(1, D),
        "glnb": np.asarray(gmlp_ln_b[l], np.float32).reshape(1, D),
        "wsT": np.ascontiguousarray(np.asarray(w_s[l], np.float32).transpose(0, 2, 1)),
        "bs": np.asarray(b_s[l], np.float32).reshape(1, D),
        "p_a": np.asarray(p_a[l], np.float32),
        "p_b": np.asarray(p_b[l], np.float32),
        "p_c": np.asarray(p_c[l], np.float32),
        "w_o": np.asarray(w_o[l], np.float32),
        "ln1g": pj(ln1_g[l]), "ln1b": pj(ln1_b[l]),
        "w_gate": np.asarray(w_gate[l], np.float32),
        "w_up": np.asarray(w_up[l], np.float32),
        "w_down": np.asarray(w_down[l], np.float32),
        "ln2g": pj(ln2_g[l]), "ln2b": pj(ln2_b[l]),
    }
    return {k + suffix: np.ascontiguousarray(v) for k, v in d.items()}


_NC_CACHE = {}


def kernel(x, positions, w_in, conv_w, gmlp_ln_g, gmlp_ln_b, w_s, b_s, p_a, p_b, p_c, w_o,
           ln1_g, ln1_b, w_gate, w_up, w_down, ln2_g, ln2_b):
    x = np.asarray(x, np.float32)
    positions = np.asarray(positions, np.int32)
    B, Sq, _ = x.shape
    consts = _consts()
    if 2 not in _NC_CACHE:
        plan = build_program(2)
        _NC_CACHE[2] = build_program(2, plan=plan)
    nc = _NC_CACHE[2]
    lw = {}
    for l in range(DEPTH):
        lw.update(_layer_weights(l, w_in, conv_w, gmlp_ln_g, gmlp_ln_b, w_s, b_s, p_a, p_b, p_c, w_o, ln1_g, ln1_b,
                                 w_gate, w_up, w_down, ln2_g, ln2_b, str(l)))
    zx = np.zeros((D, T), np.float32)
    zp = np.zeros((T,), np.int32)

    def xt(b, q):
        return np.ascontiguousarray(x[b, q * T:(q + 1) * T, :].T) if q >= 0 else zx

    def pp(b, q):
        return positions[b, q * T:(q + 1) * T] if q >= 0 else zp

    in_maps = []
    for c in range(8):
        b, qtr = c // 4, c % 4
        pos = np.concatenate([pp(b, qtr - 2), pp(b, qtr - 1), pp(b, qtr)]).reshape(1, 3 * T).astype(np.int32)
        m = {"xT": xt(b, qtr), "xhT": xt(b, qtr - 1), "xh2T": xt(b, qtr - 2), "pos": pos,
             "flag": np.full((128, 1), 1.0 if qtr >= 1 else 0.0, np.float32),
             "flag2": np.full((128, 1), 1.0 if qtr >= 2 else 0.0, np.float32)}
        m.update(consts)
        m.update(lw)
        in_maps.append(m)
    res = run_bass_kernel_spmd(nc, in_maps, core_ids=list(range(8)))
    out = np.empty((B, Sq, D), np.float32)
    for c in range(8):
        out[c // 4, (c % 4) * T:(c % 4 + 1) * T, :] = np.asarray(res.results[c]["out"]).T
    return out
```

```python
import math
from contextlib import ExitStack

import numpy as np
import concourse.bass as bass
import concourse.mybir as mybir
from concourse.bass_utils import run_bass_kernel_spmd

F32 = mybir.dt.float32
BF16 = mybir.dt.bfloat16
I32 = mybir.dt.int32
AF = mybir.ActivationFunctionType
ALU = mybir.AluOpType

D = 1024
NT = 8
T = 2048
SP = 512
NS = T // SP
DFF = 2816
NF = DFF // 128
DEPTH = 2
ALPHA = (2 * DEPTH) ** 0.25
EPS = 1e-5
N_IN = 12800
GA, GB, GC, CB, CC, CH = 0, 1024, 2048, 3072, 4096, 5120
ATT0 = 6144
UO, VO = 10752, 11776
DILS = (1, 4, 16)
MAGIC = 12582912.0
TWO_PI = 2.0 * math.pi
C1 = 6.28125
C2 = TWO_PI - C1


class Res:
    def __init__(self, name=""):
        self.name = name
        self.last_w = None
        self.reads = []


class Sched:
    ENG = ["pe", "act", "dve", "pool", "sp"]

    def __init__(self, nc, ndma=16):
        self.nc = nc
        self.ops = {e: [] for e in self.ENG}
        self.cnt = {e: 0 for e in self.ENG}
        self.known = {e: {} for e in self.ENG}
        self.ndma = ndma
        self.dma_cnt = [0] * ndma
        self.n_sp = 8
        self.rr_sp = 0
        self.rr_pool = 0
        self.opclock = 0
        self.on_op = None
        self.last_use = None

    def _need(self, eng, tok, waits):
        if tok is None:
            return
        kind, key, val = tok
        if kind == 'e' and key == eng and eng == 'pe':
            return
        k = (kind, key)
        if self.known[eng].get(k, 0) >= val:
            return
        self.known[eng][k] = val
        waits[k] = max(waits.get(k, 0), val)

    def _deps(self, eng, reads, writes, waits):
        for r in reads:
            self._need(eng, r.last_w, waits)
        for w in writes:
            self._need(eng, w.last_w, waits)
            for t in w.reads:
                self._need(eng, t, waits)

    def _commit(self, tok, reads, writes):
        for r in reads:
            r.reads.append(tok)
            if len(r.reads) > 64:
                r.reads = r.reads[-48:]
        for w in writes:
            w.last_w = tok
            w.reads = []

    def op(self, eng, fn, reads=(), writes=()):
        self.opclock += 1
        if self.last_use is not None:
            for r in reads:
                w = getattr(r, "widx", None)
                if w is not None:
                    self.last_use[w] = self.opclock
        if self.on_op is not None:
            self.on_op()
        waits = {}
        self._deps(eng, reads, writes, waits)
        self.cnt[eng] += 1
        tok = ('e', eng, self.cnt[eng])
        self.ops[eng].append((list(waits.items()), fn, ('e', eng, 1)))
        self._commit(tok, reads, writes)
        return tok

    def dma(self, fn, reads=(), writes=(), eng="sp"):
        if eng == "sp":
            ch = self.rr_sp
            self.rr_sp = (self.rr_sp + 1) % self.n_sp
        else:
            ch = self.n_sp + self.rr_pool
            self.rr_pool = (self.rr_pool + 1) % (self.ndma - self.n_sp)
        waits = {}
        if self.dma_cnt[ch] > 0:
            self._need(eng, ('d', ch, self.dma_cnt[ch] * 16), waits)
        self._deps(eng, reads, writes, waits)
        self.dma_cnt[ch] += 1
        tok = ('d', ch, self.dma_cnt[ch] * 16)
        self.ops[eng].append((list(waits.items()), fn, ('d', ch, 16)))
        self._commit(tok, reads, writes)
        return tok

    def barrier(self):
        toks = [('e', e2, self.cnt[e2]) for e2 in self.ENG if self.cnt[e2] > 0]
        toks += [('d', i, self.dma_cnt[i] * 16) for i in range(self.ndma) if self.dma_cnt[i] > 0]
        for eng in self.ENG:
            waits = {}
            for t in toks:
                if t[0] == 'e' and t[1] == eng:
                    continue
                self._need(eng, t, waits)
            self.ops[eng].append((list(waits.items()), None, None))

    def finish(self, eng="sp"):
        waits = {}
        for i in range(self.ndma):
            if self.dma_cnt[i] > 0:
                self._need(eng, ('d', i, self.dma_cnt[i] * 16), waits)
        self.ops[eng].append((list(waits.items()), None, None))

    def emit(self, stack):
        nc = self.nc
        esem = {e: stack.enter_context(nc.semaphore("s_" + e)) for e in self.ENG}
        dsem = [stack.enter_context(nc.semaphore("d_%d" % i)) for i in range(self.ndma)]

        def semof(k):
            return esem[k[1]] if k[0] == 'e' else dsem[k[1]]

        block = stack.enter_context(nc.Block())

        def run(engname):
            def body(e):
                for waits, fn, inc in self.ops[engname]:
                    for k, v in waits:
                        e.wait_ge(semof(k), v)
                    if fn is not None:
                        ins = fn(e)
                        ins.then_inc(semof(inc), inc[2])
            return body
        block.tensor(run("pe"))
        block.scalar(run("act"))
        block.vector(run("dve"))
        block.gpsimd(run("pool"))
        block.sync(run("sp"))


class Tile:
    def __init__(self, h, name):
        self.h = h
        self.r = Res(name)

    def __getitem__(self, k):
        return self.h[k]


class Ring:
    def __init__(self, tiles):
        self.tiles = tiles
        self.i = 0

    def next(self):
        t = self.tiles[self.i % len(self.tiles)]
        self.i += 1
        return t


def build_program(n_layers, kstop=99, plan=None):
    nc = bass.Bass("TRN2", target_bir_lowering=False)

    def din(name, shape, dt=F32):
        return nc.dram_tensor(name, list(shape), dt, kind="ExternalInput").ap()

    xT_d = din("xT", [D, T])
    xhT_d = din("xhT", [D, T])
    xh2T_d = din("xh2T", [D, T])
    pos_d = din("pos", [1, 3 * T], I32)
    flag_d = din("flag", [128, 1])
    flag2_d = din("flag2", [128, 1])
    ident_d = din("ident", [128, 128])
    cmask_d = din("cmask", [128, 1024])
    esel_d = din("esel", [128, 16])
    selb_d = din("selb", [4, 256])
    ones_d = din("ones", [128, 128])
    tril_d = din("tril", [128, 128])
    invf_d = din("invf", [128, 1])
    W = []
    for l in range(n_layers):
        W.append(dict(
            w_in=din("w_in%d" % l, [D, N_IN]),
            convw=din("convw%d" % l, [128, NT * 3]),
            glng=din("glng%d" % l, [1, D]),
            glnb=din("glnb%d" % l, [1, D]),
            wsT=din("wsT%d" % l, [8, 128, 128]),
            bs=din("bs%d" % l, [1, D]),
            p_a=din("p_a%d" % l, [D, D]),
            p_b=din("p_b%d" % l, [512, D]),
            p_c=din("p_c%d" % l, [D, D]),
            w_o=din("w_o%d" % l, [D, D]),
            ln1g=din("ln1g%d" % l, [128, NT]),
            ln1b=din("ln1b%d" % l, [128, NT]),
            w_gate=din("w_gate%d" % l, [D, DFF]),
            w_up=din("w_up%d" % l, [D, DFF]),
            w_down=din("w_down%d" % l, [DFF, D]),
            ln2g=din("ln2g%d" % l, [128, NT]),
            ln2b=din("ln2b%d" % l, [128, NT]),
        ))
    out_d = nc.dram_tensor("out", [D, T], F32, kind="ExternalOutput").ap()
    x1_d = nc.dram_tensor("x1_scratch", [D, T], F32).ap()
    x1h_d = nc.dram_tensor("x1h_scratch", [D, T], F32).ap()

    S = Sched(nc, ndma=24)

    with ExitStack() as top:
        def sb(st, name, shape, dt):
            return Tile(st.enter_context(nc.sbuf_tensor("sb_" + name, list(shape), dt)), name)

        def psum(st, name, shape, dt=F32):
            return Tile(st.enter_context(nc.psum_tensor("ps_" + name, list(shape), dt)), name)

        y_b = sb(top, "y_b", [128, 4, T], BF16)
        ident = sb(top, "ident", [128, 128], BF16)
        cmask = sb(top, "cmask", [128, 1024], BF16)
        esel = sb(top, "esel", [128, 16], BF16)
        eselF = sb(top, "eselF", [128, 16], BF16)
        selb = sb(top, "selb", [4, 256], F32)
        ones = sb(top, "ones", [128, 128], BF16)
        tril = sb(top, "tril", [128, 128], F32)
        invf = sb(top, "invf", [128, 1], F32)
        flag = sb(top, "flag", [128, 1], F32)
        flag2 = sb(top, "flag2", [128, 1], F32)
        eselF2 = sb(top, "eselF2", [128, 16], BF16)
        zhist = sb(top, "zhist", [128, NT, 2], F32)
        wring = Ring([sb(top, "wbuf%d" % i, [128, 4096], BF16) for i in range(5)])
        tmpr = Ring([sb(top, "tmp%d" % i, [128, 512], F32) for i in range(8)])

        def dmac(out, in_, writes, reads=()):
            return S.dma(lambda e, o=out, i=in_: e.dma_start(out=o, in_=i), reads=reads, writes=writes, eng="pool")

        def dmas(out, in_, writes=(), reads=()):
            return S.dma(lambda e, o=out, i=in_: e.dma_start(out=o, in_=i), reads=reads, writes=writes, eng="sp")

        NBUF = len(wring.tiles)
        PF = 3
        wst = {"idx": 0, "next": 0}
        WAP = {}
        for Wl_ in W:
            for ap_ in Wl_.values():
                WAP[ap_.tensor.name] = ap_
        if plan is None:
            S.last_use = {}
            dry_seq = []

        wcache = {}
        wcount = {}
        if plan is not None:
            for key_ in plan["seq"]:
                wcount[key_] = wcount.get(key_, 0) + 1

        def _issue(k):
            key = plan["seq"][k]
            (nm, c0, width, K) = key
            kt = K // 128
            t = wring.tiles[k % NBUF]
            flat = t.h[:, 0:kt * width]
            view = flat.rearrange("p (k c) -> p k c", k=kt)
            if wcount[key] < 3:
                dmac(view, WAP[nm][:, c0:c0 + width].rearrange("(k p) c -> p k c", p=128), writes=[t.r])
            elif key not in wcache:
                dmac(view, WAP[nm][:, c0:c0 + width].rearrange("(k p) c -> p k c", p=128), writes=[t.r])
                sc = nc.dram_tensor("wcache_%d" % len(wcache), [128, kt * width], BF16).ap()
                r = Res("wc")
                wcache[key] = (sc, r)
                dmas(sc, flat, writes=[r], reads=[t.r])
            else:
                sc, r = wcache[key]
                dmas(flat, sc, writes=[t.r], reads=[r])

        def _issue_upto(limit):
            limit = min(limit, len(plan["seq"]) - 1)
            while wst["next"] <= limit:
                k = wst["next"]
                if k >= NBUF and S.opclock <= plan["last_use"].get(k - NBUF, 0):
                    break
                _issue(k)
                wst["next"] += 1

        if plan is not None:
            S.on_op = lambda: _issue_upto(wst["idx"] - 1 + PF)

        def wload(w_ap, c0, width, K):
            kt = K // 128
            key = (w_ap.tensor.name, c0, width, K)
            idx = wst["idx"]
            wst["idx"] += 1
            if plan is None:
                dry_seq.append(key)
                t = wring.next()
                view = t.h[:, 0:kt * width].rearrange("p (k c) -> p k c", k=kt)
                r = Res("w%d" % idx)
                r.widx = idx
                return view, r
            assert plan["seq"][idx] == key, (idx, key, plan["seq"][idx])
            _issue_upto(idx + PF)
            assert wst["next"] > idx, "weight ring too small: block %d not issuable" % idx
            t = wring.tiles[idx % NBUF]
            view = t.h[:, 0:kt * width].rearrange("p (k c) -> p k c", k=kt)
            return view, t.r

        def mmgroup(out_ap, pairs, reads, writes, tp=None):
            n = len(pairs)

            def fn(e, out_ap=out_ap, pairs=pairs, tp=tp):
                ins = None
                for i, (l, r) in enumerate(pairs):
                    if tp is None:
                        ins = e.matmul(out_ap, lhsT=l, rhs=r, start=(i == 0), stop=(i == n - 1))
                    else:
                        ins = e.matmul(out_ap, lhsT=l, rhs=r, start=(i == 0), stop=(i == n - 1), tile_position=tp)
                return ins
            return S.op("pe", fn, reads=reads, writes=writes)

        def tt(eng, out, in0, in1, op, reads, writes):
            return S.op(eng, lambda e: e.tensor_tensor(out=out, in0=in0, in1=in1, op=op), reads=reads, writes=writes)

        def tsc(eng, out, in0, s1, s2, op0, op1, reads, writes):
            if op1 is None:
                return S.op(eng, lambda e: e.tensor_scalar(out=out, in0=in0, scalar1=s1, scalar2=None, op0=op0), reads=reads, writes=writes)
            return S.op(eng, lambda e: e.tensor_scalar(out=out, in0=in0, scalar1=s1, scalar2=s2, op0=op0, op1=op1), reads=reads, writes=writes)

        def stt(eng, out, in0, scalar, in1, op0, op1, reads, writes):
            return S.op(eng, lambda e: e.scalar_tensor_tensor(out=out, in0=in0, scalar=scalar, in1=in1, op0=op0, op1=op1), reads=reads, writes=writes)

        def cpy(eng, out, in_, reads, writes):
            return S.op(eng, lambda e: e.tensor_copy(out=out, in_=in_), reads=reads, writes=writes)

        def rcp(out, in_, reads, writes):
            return S.op("dve", lambda e: e.reciprocal(out=out, in_=in_), reads=reads, writes=writes)

        def act(out, in_, func, reads, writes, **kw):
            return S.op("act", lambda e: e.activation(out=out, in_=in_, func=func, **kw), reads=reads, writes=writes)

        dmac(ident[:], ident_d, [ident.r])
        dmac(cmask[:], cmask_d, [cmask.r])
        dmac(esel[:], esel_d, [esel.r])
        dmac(ones[:], ones_d, [ones.r])
        dmas(selb[:], selb_d, [selb.r])
        dmas(tril[:], tril_d, [tril.r])
        dmas(invf[:], invf_d, [invf.r])
        dmas(flag[:], flag_d, [flag.r])
        dmas(flag2[:], flag2_d, [flag2.r])
        S.op("dve", lambda e: e.tensor_scalar(out=eselF2[:], in0=esel[:], scalar1=flag2[:, 0:1], scalar2=None, op0=ALU.mult),
             reads=[esel.r, flag2.r], writes=[eselF2.r])
        S.op("dve", lambda e: e.tensor_scalar(out=eselF[:], in0=esel[:], scalar1=flag[:, 0:1], scalar2=None, op0=ALU.mult),
             reads=[esel.r, flag.r], writes=[eselF.r])

        def rope_tables(p0, n, cos_ap, sin_ap, cos_r, sin_r, posi):
            posf, ta, tb = tmpr.next(), tmpr.next(), tmpr.next()
            dmas(posi[:, 0:n], pos_d[0:1, p0:p0 + n].partition_broadcast(128), writes=[posi.r])
            S.op("dve", lambda e: e.tensor_copy(out=posf[:, 0:n], in_=posi[:, 0:n]), reads=[posi.r], writes=[posf.r])
            S.op("dve", lambda e: e.tensor_scalar(out=posf[:, 0:n], in0=posf[:, 0:n], scalar1=invf[:, 0:1], scalar2=None, op0=ALU.mult),
                 reads=[posf.r, invf.r], writes=[posf.r])
            S.op("dve", lambda e: e.tensor_scalar(out=ta[:, 0:n], in0=posf[:, 0:n], scalar1=1.0 / TWO_PI, scalar2=MAGIC, op0=ALU.mult, op1=ALU.add),
                 reads=[posf.r], writes=[ta.r])
            S.op("dve", lambda e: e.tensor_scalar(out=ta[:, 0:n], in0=ta[:, 0:n], scalar1=MAGIC, scalar2=None, op0=ALU.subtract),
                 reads=[ta.r], writes=[ta.r])
            S.op("dve", lambda e: e.scalar_tensor_tensor(out=posf[:, 0:n], in0=ta[:, 0:n], scalar=-C1, in1=posf[:, 0:n], op0=ALU.mult, op1=ALU.add),
                 reads=[ta.r, posf.r], writes=[posf.r])
            S.op("dve", lambda e: e.scalar_tensor_tensor(out=posf[:, 0:n], in0=ta[:, 0:n], scalar=-C2, in1=posf[:, 0:n], op0=ALU.mult, op1=ALU.add),
                 reads=[ta.r, posf.r], writes=[posf.r])
            S.op("dve", lambda e: e.tensor_scalar(out=posf[:, 0:n], in0=posf[:, 0:n], scalar1=-3.1415925, scalar2=3.1415925, op0=ALU.max, op1=ALU.min),
                 reads=[posf.r], writes=[posf.r])
            act(sin_ap, posf[:, 0:n], AF.Sin, [posf.r], [sin_r])
            S.op("dve", lambda e: e.scalar_tensor_tensor(out=tb[:, 0:n], in0=posf[:, 0:n], scalar=-1.0, in1=posf[:, 0:n], op0=ALU.mult, op1=ALU.max),
                 reads=[posf.r], writes=[tb.r])
            S.op("dve", lambda e: e.tensor_scalar(out=tb[:, 0:n], in0=tb[:, 0:n], scalar1=-1.0, scalar2=math.pi / 2, op0=ALU.mult, op1=ALU.add),
                 reads=[tb.r], writes=[tb.r])
            act(cos_ap, tb[:, 0:n], AF.Sin, [tb.r], [cos_r])

        xin_r = Res("xin")
        x1_r = Res("x1")
        x1h_r = Res("x1h")
        if n_layers == 1:
            passes = [(0, xT_d, xhT_d, T, flag, eselF, out_d, xin_r, xin_r, None)]
        else:
            passes = [
                (0, xhT_d, xh2T_d, 0, flag2, eselF2, x1h_d, xin_r, xin_r, x1h_r),
                (0, xT_d, xhT_d, T, flag, eselF, x1_d, xin_r, xin_r, x1_r),
                (1, x1_d, x1h_d, T, flag, eselF, out_d, x1_r, x1h_r, None),
            ]
        for pi, (l, x_src, xh_src, pos0, flag_t, eselF_t, out_ap, xsrc_r, xhsrc_r, out_r) in enumerate(passes):
            Wl = W[l]
            w_in = Wl["w_in"]
            with ExitStack() as pa:
                bring = Ring([psum(pa, "bankA%d_%d" % (i, pi), [128, 512]) for i in range(7)])
                bank_t = psum(pa, "bank_t_%d" % pi, [128, 1024], BF16)
                cos_o = sb(pa, "cos_o" + "_%d" % pi, [128, T], F32)
                sin_o = sb(pa, "sin_o" + "_%d" % pi, [128, T], F32)
                cos_h = sb(pa, "cos_h" + "_%d" % pi, [128, SP], F32)
                sin_h = sb(pa, "sin_h" + "_%d" % pi, [128, SP], F32)
                posi = sb(pa, "posi" + "_%d" % pi, [128, SP], I32)
                acc = sb(pa, "acc" + "_%d" % pi, [128, 2, T], F32)
                accd = sb(pa, "accd" + "_%d" % pi, [4, T], F32)
                KT = sb(pa, "KT" + "_%d" % pi, [128, 2, 2 * T], BF16)
                VT = sb(pa, "VT" + "_%d" % pi, [128, 2, 2 * T], BF16)
                QT = sb(pa, "QT" + "_%d" % pi, [128, 2, T], BF16)
                vbr = Ring([sb(pa, "vb%d_%d" % (i, pi), [128, 256], BF16) for i in range(6)])
                P_t = Ring([sb(pa, "P%d_%d" % (i, pi), [128, 1024], BF16) for i in range(3)])
                xcr = Ring([sb(pa, "xca%d_%d" % (i, pi), [128, NT, SP], BF16) for i in range(2)])
                for s in range(NS):
                    rope_tables(pos0 + T + s * SP, SP, cos_o[:, s * SP:(s + 1) * SP], sin_o[:, s * SP:(s + 1) * SP], cos_o.r, sin_o.r, posi)

                def chunk_list(dil_):
                    HL_ = 128 * dil_
                    ch_ = []
                    hs0_ = T - HL_
                    n_h_ = min(HL_, SP)
                    for a_ in range(hs0_, T, n_h_):
                        ch_.append((True, a_, n_h_, a_ - hs0_))
                    for s_ in range(NS):
                        ch_.append((False, s_ * SP, SP, HL_ + s_ * SP))
                    return ch_

                def issue_first(chunks_):
                    (is_h_, t0_, n_, c0_) = chunks_[0]
                    xc_ = xcr.next()
                    src_ = xh_src if is_h_ else x_src
                    dmac(xc_[:, :, 0:n_], src_[:, t0_:t0_ + n_].rearrange("(k p) t -> p k t", p=128), [xc_.r], reads=[xhsrc_r if is_h_ else xsrc_r])
                    return xc_
                phases = [(qd_, g_) for qd_ in range(2) for g_ in range(3)]
                pre_first = {0: issue_first(chunk_list(DILS[0]))}
                for qd in range(2):
                    if kstop <= 1:
                        break
                    S.op("pool", lambda e, acc=acc: e.memset(acc[:], 0.0), writes=[acc.r])
                    S.op("pool", lambda e, accd=accd: e.memset(accd[:], 0.0), writes=[accd.r])
                    for g, dil in enumerate(DILS):
                        HL = 128 * dil
                        base = ATT0 + g * 1536
                        wq_v, wq_r = wload(w_in, base + qd * 256, 256, D)
                        wk_v, wk_r = wload(w_in, base + 512 + qd * 256, 256, D)
                        wv_v, wv_r = wload(w_in, base + 1024 + qd * 256, 256, D)
                        chunks = []
                        hs0 = T - HL
                        n_h = min(HL, SP)
                        for a in range(hs0, T, n_h):
                            chunks.append((True, a, n_h, a - hs0))
                        for s in range(NS):
                            chunks.append((False, s * SP, SP, HL + s * SP))
                        loaded = {}

                        def issue_chunk(ci, chunks=chunks, loaded=loaded):
                            (is_h_, t0_, n_, c0_) = chunks[ci]
                            xc_ = xcr.next()
                            src_ = xh_src if is_h_ else x_src
                            dmac(xc_[:, :, 0:n_], src_[:, t0_:t0_ + n_].rearrange("(k p) t -> p k t", p=128), [xc_.r], reads=[xhsrc_r if is_h_ else xsrc_r])
                            loaded[ci] = xc_
                        ph_i = qd * 3 + g
                        loaded[0] = pre_first.pop(ph_i)
                        for ci, (is_h, t0, n, c0) in enumerate(chunks):
                            if ci + 1 < len(chunks):
                                issue_chunk(ci + 1)
                            xc = loaded[ci]
                            if is_h:
                                rope_tables(pos0 + t0, n, cos_h[:, 0:n], sin_h[:, 0:n], cos_h.r, sin_h.r, posi)
                                cs_ap, sn_ap, cs_r, sn_r = cos_h[:, 0:n], sin_h[:, 0:n], cos_h.r, sin_h.r
                            else:
                                cs_ap, sn_ap, cs_r, sn_r = cos_o[:, t0:t0 + n], sin_o[:, t0:t0 + n], cos_o.r, sin_o.r
                            todo = [(wk_v, wk_r, KT, c0)]
                            if not is_h:
                                todo.append((wq_v, wq_r, QT, t0))
                            for (wt, wr, dst, dc0) in todo:
                                pA = bring.next()
                                pB = bring.next()
                                mmgroup(pA[:, 0:n], [(wt[:, k, 0:128], xc[:, k, 0:n]) for k in range(NT)], [wr, xc.r], [pA.r])
                                mmgroup(pB[:, 0:n], [(wt[:, k, 128:256], xc[:, k, 0:n]) for k in range(NT)], [wr, xc.r], [pB.r])
                                t1, t2, t3, t4 = tmpr.next(), tmpr.next(), tmpr.next(), tmpr.next()
                                tt("dve", t1[:, 0:n], pA[:, 0:n], cs_ap, ALU.mult, [pA.r, cs_r], [t1.r])
                                tt("dve", t2[:, 0:n], pB[:, 0:n], sn_ap, ALU.mult, [pB.r, sn_r], [t2.r])
                                tt("pool", dst[:, 0, dc0:dc0 + n], t1[:, 0:n], t2[:, 0:n], ALU.subtract, [t1.r, t2.r], [dst.r])
                                tt("dve", t3[:, 0:n], pB[:, 0:n], cs_ap, ALU.mult, [pB.r, cs_r], [t3.r])
                                tt("dve", t4[:, 0:n], pA[:, 0:n], sn_ap, ALU.mult, [pA.r, sn_r], [t4.r])
                                tt("pool", dst[:, 1, dc0:dc0 + n], t3[:, 0:n], t4[:, 0:n], ALU.add, [t3.r, t4.r], [dst.r])
                            for vt in range(2):
                                pV = bring.next()
                                mmgroup(pV[:, 0:n], [(wv_v[:, k, vt * 128:(vt + 1) * 128], xc[:, k, 0:n]) for k in range(NT)], [wv_r, xc.r], [pV.r])
                                act(VT[:, vt, c0:c0 + n], pV[:, 0:n], AF.Copy, [pV.r], [VT.r])
                        if kstop <= 2:
                            break
                        if ph_i + 1 < len(phases):
                            pre_first[ph_i + 1] = issue_first(chunk_list(DILS[phases[ph_i + 1][1]]))
                        nb = T // (128 * dil)
                        Wd = HL + T

                        def kview(tl, lo, hi, ab, m, r, dil=dil, Wd=Wd):
                            return tl.h[lo:hi, ab, 0:Wd].rearrange("p (m i r) -> p m r i", i=128, r=dil)[:, m, r, :]

                        def qview(tl, lo, hi, ab, m, r, dil=dil):
                            return tl.h[lo:hi, ab, 0:T].rearrange("p (m i r) -> p m r i", i=128, r=dil)[:, m, r, :]

                        pending = [None]
                        for r in range(dil):
                            vprev = None
                            for m in range(nb + 1):
                                vb = vbr.next()

                                v0_ap = kview(VT, 0, 128, 0, m, r)
                                v1_ap = kview(VT, 0, 128, 1, m, r)

                                def tr2(e, v0_ap=v0_ap, v1_ap=v1_ap, o0=bank_t[:, 0:128], o1=bank_t[:, 128:256], idn=ident[:]):
                                    e.transpose(out=o0, in_=v0_ap, identity=idn)
                                    return e.transpose(out=o1, in_=v1_ap, identity=idn)
                                S.op("pe", tr2, reads=[VT.r, ident.r], writes=[bank_t.r])
                                if m == 0:
                                    act(vb[:], bank_t[:, 0:256], AF.Copy, [bank_t.r, flag_t.r], [vb.r], scale=flag_t[:, 0:1])
                                    vprev = vb
                                    continue
                                act(vb[:], bank_t[:, 0:256], AF.Copy, [bank_t.r], [vb.r])
                                n_q = m - 1
                                sbks = [bring.next() for _ in range(4)]
                                for hs in range(4):
                                    sbk = sbks[hs]
                                    lo, hi = 32 * hs, 32 * hs + 32
                                    for kt in range(2):
                                        o0 = kt * 128
                                        mmgroup(sbk[:, o0:o0 + 128],
                                                [(kview(KT, lo, hi, 0, n_q + kt, r), qview(QT, lo, hi, 0, n_q, r)),
                                                 (kview(KT, lo, hi, 1, n_q + kt, r), qview(QT, lo, hi, 1, n_q, r))],
                                                [KT.r, QT.r], [sbk.r], tp=(32 * hs, 0))
                                P = P_t.next()
                                for hs in range(4):
                                    act(P[:, hs * 256:(hs + 1) * 256], sbks[hs][:, 0:256], AF.Exp, [sbks[hs].r], [P.r], scale=0.125)
                                tt("pool", P[:], P[:], cmask[:], ALU.mult, [P.r, cmask.r], [P.r])
                                def pv_stage(vprev=vprev, vb=vb, P=P, m=m, n_q=n_q, r=r, dil=dil, acc=acc, accd=accd):
                                    nd = bring.next()
                                    vbs = (vprev, vb)
                                    for pr in range(2):
                                        for hh in range(2):
                                            hs = 2 * pr + hh
                                            mmgroup(nd[64 * hh:64 * hh + 64, pr * 128:(pr + 1) * 128],
                                                    [(vbs[kt][:, hs * 64:(hs + 1) * 64], P[:, hs * 256 + kt * 128: hs * 256 + kt * 128 + 128]) for kt in range(2)],
                                                    [vprev.r, vb.r, P.r], [nd.r], tp=(0, 64 * hh))
                                    es_prev = eselF_t if m == 1 else esel
                                    pairs = []
                                    for hs in range(4):
                                        pairs.append((es_prev[:, hs * 4:(hs + 1) * 4], P[:, hs * 256: hs * 256 + 128]))
                                        pairs.append((esel[:, hs * 4:(hs + 1) * 4], P[:, hs * 256 + 128: hs * 256 + 256]))
                                    mmgroup(nd[0:4, 256:384], pairs, [P.r, esel.r, eselF_t.r], [nd.r])
                                    accv = acc.h[:, :, :].rearrange("p a (m i r) -> p a m r i", i=128, r=dil)[:, :, n_q, r, :]
                                    tt("dve", accv, accv, nd[:, 0:256].rearrange("p (a i) -> p a i", a=2), ALU.add, [nd.r, acc.r], [acc.r])
                                    adv = accd.h[:, :].rearrange("p (m i r) -> p m r i", i=128, r=dil)[:, n_q, r, :]
                                    tt("dve", adv, adv, nd[0:4, 256:384], ALU.add, [nd.r, accd.r], [accd.r])
                                if pending[0] is not None:
                                    pending[0]()
                                pending[0] = pv_stage
                                vprev = vb
                        if pending[0] is not None:
                            pending[0]()
                            pending[0] = None
                    if kstop <= 3:
                        break
                    rcp(accd[:], accd[:], [accd.r], [accd.r])
                    if kstop <= 4:
                        break
                    for pr in range(2):
                        for s in range(NS):
                            bc = bring.next()
                            mmgroup(bc[:, :], [(selb[:, pr * 128:(pr + 1) * 128], accd[:, s * SP:(s + 1) * SP])], [selb.r, accd.r], [bc.r])
                            tt("dve", y_b[:, 2 * qd + pr, s * SP:(s + 1) * SP], acc[:, pr, s * SP:(s + 1) * SP], bc[:, :], ALU.mult,
                               [bc.r, acc.r], [y_b.r])

                S.barrier()
            if kstop <= 5:
                break
            with ExitStack() as pb:
                bring = Ring([psum(pb, "bankB%d_%d" % (i, pi), [128, 512]) for i in range(6)])
                stat_banks = [psum(pb, "bankS%d_%d" % (i, pi), [128, 512]) for i in range(2)]
                xs = sb(pb, "xs" + "_%d" % pi, [128, NT, SP], F32)
                xc = sb(pb, "xcb" + "_%d" % pi, [128, NT, SP], BF16)
                vfm = sb(pb, "vfm" + "_%d" % pi, [128, 4 * D], F32)
                u_sb = sb(pb, "u_sb" + "_%d" % pi, [128, NT, SP], BF16)
                y_a = sb(pb, "y_a" + "_%d" % pi, [128, NT, SP], BF16)
                y_c = sb(pb, "y_c" + "_%d" % pi, [128, NT, SP], BF16)
                vbf = [sb(pb, "vbf%d_%d" % (i, pi), [128, D], BF16) for i in range(4)]
                m_sb = sb(pb, "m_sb" + "_%d" % pi, [128, NT, SP], BF16)
                h_sb = sb(pb, "h_sb" + "_%d" % pi, [128, NF, SP], BF16)
                zbuf = sb(pb, "zbuf" + "_%d" % pi, [128, SP + 2], F32)
                convw = sb(pb, "convw" + "_%d" % pi, [128, NT * 3], F32)
                glng = sb(pb, "glng" + "_%d" % pi, [128, D], F32)
                glnb = sb(pb, "glnb" + "_%d" % pi, [128, D], F32)
                bsb = sb(pb, "bsb" + "_%d" % pi, [128, D], F32)
                wsf = sb(pb, "wsf" + "_%d" % pi, [128, 8, 128], F32)
                wsm = sb(pb, "wsm" + "_%d" % pi, [128, 8, 128], BF16)
                ln1g = sb(pb, "ln1g" + "_%d" % pi, [128, NT], F32)
                ln1b = sb(pb, "ln1b" + "_%d" % pi, [128, NT], F32)
                ln2g = sb(pb, "ln2g" + "_%d" % pi, [128, NT], F32)
                ln2b = sb(pb, "ln2b" + "_%d" % pi, [128, NT], F32)
                st6 = sb(pb, "st6" + "_%d" % pi, [128, 2, 6], F32)
                mv = sb(pb, "mv" + "_%d" % pi, [128, 2], F32)
                rstd1 = sb(pb, "rstd1" + "_%d" % pi, [128, 1], F32)
                xh16 = sb(pb, "xh16" + "_%d" % pi, [128, NT, 16], BF16)
                mean_t = sb(pb, "mean_t" + "_%d" % pi, [128, SP], F32)
                rstd_t = sb(pb, "rstd_t" + "_%d" % pi, [128, SP], F32)

                dmas(convw[:], Wl["convw"], [convw.r])
                dmas(glng[:], Wl["glng"].partition_broadcast(128), [glng.r])
                dmas(glnb[:], Wl["glnb"].partition_broadcast(128), [glnb.r])
                dmas(bsb[:], Wl["bs"].partition_broadcast(128), [bsb.r])
                dmas(wsf[:], Wl["wsT"].rearrange("g j i -> j g i"), [wsf.r])
                dmas(ln1g[:], Wl["ln1g"], [ln1g.r])
                dmas(ln1b[:], Wl["ln1b"], [ln1b.r])
                dmas(ln2g[:], Wl["ln2g"], [ln2g.r])
                dmas(ln2b[:], Wl["ln2b"], [ln2b.r])
                tt("dve", wsm[:], wsf[:], tril[:].unsqueeze(1).to_broadcast([128, 8, 128]), ALU.mult, [wsf.r, tril.r], [wsm.r])
                dmac(xh16[:], xh_src[:, T - 16:T].rearrange("(k p) t -> p k t", p=128), [xh16.r], reads=[xhsrc_r])
                for jb in range(2):
                    wC, wCr = wload(w_in, CC + jb * 512, 512, D)
                    wH, wHr = wload(w_in, CH + jb * 512, 512, D)
                    for jj in range(4):
                        j = jb * 4 + jj
                        pC = bring.next()
                        pH = bring.next()
                        mmgroup(pC[:, 0:2], [(wC[:, k, jj * 128:(jj + 1) * 128], xh16[:, k, 14:16]) for k in range(NT)], [wCr, xh16.r], [pC.r])
                        mmgroup(pH[:, 0:2], [(wH[:, k, jj * 128:(jj + 1) * 128], xh16[:, k, 14:16]) for k in range(NT)], [wHr, xh16.r], [pH.r])
                        tz = tmpr.next()
                        act(tz[:, 0:2], pC[:, 0:2], AF.Copy, [pC.r, flag_t.r], [tz.r], scale=flag_t[:, 0:1])
                        tt("dve", zhist[:, j, :], tz[:, 0:2], pH[:, 0:2], ALU.mult, [tz.r, pH.r], [zhist.r])

                class LNStats:
                    def __init__(self):
                        self.rb, self.rsq = y_a, y_c
                        self.pm = stat_banks[0]
                        self.pq = stat_banks[1]
                        self.lag = None

                    def _mm(self, j):
                        rb, rsq, pm, pq = self.rb, self.rsq, self.pm, self.pq
                        S.op("pe", lambda e: e.matmul(pm[:, :], lhsT=ones[:], rhs=rb[:, j, :], start=(j == 0), stop=(j == NT - 1)),
                             reads=[ones.r, rb.r], writes=[pm.r])
                        S.op("pe", lambda e: e.matmul(pq[:, :], lhsT=ones[:], rhs=rsq[:, j, :], start=(j == 0), stop=(j == NT - 1)),
                             reads=[ones.r, rsq.r], writes=[pq.r])

                    def tile_done(self, j):
                        act(self.rb[:, j, :], xs[:, j, :], AF.Identity, [xs.r], [self.rb.r])
                        tt("dve", self.rsq[:, j, :], xs[:, j, :], xs[:, j, :], ALU.mult, [xs.r], [self.rsq.r])
                        if self.lag is not None:
                            self._mm(self.lag)
                        self.lag = j

                    def finish(self, g_t, b_t, write_xc=True):
                        self._mm(self.lag)
                        pm, pq = self.pm, self.pq
                        msq = tmpr.next()
                        tsc("dve", mean_t[:], pm[:, :], 1.0 / D, None, ALU.mult, None, [pm.r], [mean_t.r])
                        tt("dve", msq[:], mean_t[:], mean_t[:], ALU.mult, [mean_t.r], [msq.r])
                        stt("dve", msq[:], pq[:, :], 1.0 / D, msq[:], ALU.mult, ALU.subtract, [pq.r, msq.r], [msq.r])
                        tsc("dve", msq[:], msq[:], EPS, None, ALU.add, None, [msq.r], [msq.r])
                        act(msq[:], msq[:], AF.Sqrt, [msq.r], [msq.r])
                        rcp(rstd_t[:], msq[:], [msq.r], [rstd_t.r])
                        ts_ = []
                        for j in range(NT):
                            t = tmpr.next()
                            eng = "dve" if j % 2 == 0 else "pool"
                            tt(eng, t[:], xs[:, j, :], mean_t[:], ALU.subtract, [xs.r, mean_t.r], [t.r])
                            tt(eng, t[:], t[:], rstd_t[:], ALU.mult, [t.r, rstd_t.r], [t.r])
                            if write_xc:
                                act(xc[:, j, :], t[:], AF.Identity, [t.r, g_t.r, b_t.r], [xc.r], scale=g_t[:, j:j + 1], bias=b_t[:, j:j + 1])
                            ts_.append(t)
                        for j in range(NT):
                            t = ts_[j]
                            act(xs[:, j, :], t[:], AF.Identity, [t.r, g_t.r, b_t.r], [xs.r], scale=g_t[:, j:j + 1], bias=b_t[:, j:j + 1])

                mf = lambda j: vfm[:, j * SP:(j + 1) * SP]
                vf = lambda t_: vfm[:, t_ * D:(t_ + 1) * D]

                for s in range(NS):
                    c0 = s * SP
                    vfm3 = vfm.h[:, :].rearrange("p (k t) -> p k t", k=NT)
                    if s == 0:
                        dmac(xs[:, :, :], x_src[:, c0:c0 + SP].rearrange("(k p) t -> p k t", p=128), [xs.r], reads=[xsrc_r])
                        act(xc[:, :, :], xs[:, :, :], AF.Copy, [xs.r], [xc.r])
                    else:
                        cpy("pool", xs[:, :, :], vfm3, [vfm.r], [xs.r])
                    for jb in range(2):
                        wB, wBr = wload(w_in, CB + jb * 512, 512, D)
                        wC, wCr = wload(w_in, CC + jb * 512, 512, D)
                        wH, wHr = wload(w_in, CH + jb * 512, 512, D)
                        for jj in range(4):
                            j = jb * 4 + jj
                            pB_, pC, pH = bring.next(), bring.next(), bring.next()
                            mmgroup(pC[:, :], [(wC[:, k, jj * 128:(jj + 1) * 128], xc[:, k, :]) for k in range(NT)], [wCr, xc.r], [pC.r])
                            mmgroup(pH[:, :], [(wH[:, k, jj * 128:(jj + 1) * 128], xc[:, k, :]) for k in range(NT)], [wHr, xc.r], [pH.r])
                            mmgroup(pB_[:, :], [(wB[:, k, jj * 128:(jj + 1) * 128], xc[:, k, :]) for k in range(NT)], [wBr, xc.r], [pB_.r])
                            tc_ = tmpr.next()
                            ta_ = tmpr.next()
                            act(tc_[:], pC[:, :], AF.Copy, [pC.r], [tc_.r])
                            act(zbuf[:, 0:2], zhist[:, j, :], AF.Copy, [zhist.r], [zbuf.r])
                            tt("dve", zbuf[:, 2:SP + 2], tc_[:], pH[:, :], ALU.mult, [tc_.r, pH.r], [zbuf.r])
                            act(zhist[:, j, :], zbuf[:, SP:SP + 2], AF.Copy, [zbuf.r], [zhist.r])
                            tsc("dve", ta_[:], zbuf[:, 0:SP], convw[:, 3 * j:3 * j + 1], None, ALU.mult, None, [zbuf.r, convw.r], [ta_.r])
                            stt("dve", ta_[:], zbuf[:, 1:SP + 1], convw[:, 3 * j + 1:3 * j + 2], ta_[:], ALU.mult, ALU.add, [zbuf.r, convw.r, ta_.r], [ta_.r])
                            stt("dve", ta_[:], zbuf[:, 2:SP + 2], convw[:, 3 * j + 2:3 * j + 3], ta_[:], ALU.mult, ALU.add, [zbuf.r, convw.r, ta_.r], [ta_.r])
                            tt("dve", y_a[:, j, :], ta_[:], pB_[:, :], ALU.mult, [ta_.r, pB_.r], [y_a.r])
                    for jb in range(2):
                        wU, wUr = wload(w_in, UO + jb * 512, 512, D)
                        for jj in range(4):
                            j = jb * 4 + jj
                            pU = bring.next()
                            mmgroup(pU[:, :], [(wU[:, k, jj * 128:(jj + 1) * 128], xc[:, k, :]) for k in range(NT)], [wUr, xc.r], [pU.r])
                            act(u_sb[:, j, :], pU[:, :], AF.Gelu, [pU.r], [u_sb.r])
                    for half in range(2):
                        wV, wVr = wload(w_in, VO + half * 512, 512, D)
                        for t_ in range(4):
                            pV = bring.next()
                            mmgroup(pV[:, :], [(xc[:, k, t_ * 128:(t_ + 1) * 128], wV[:, k, :]) for k in range(NT)], [wVr, xc.r], [pV.r])
                            act(vfm[:, t_ * D + half * 512: t_ * D + (half + 1) * 512], pV[:, :], AF.Gelu, [pV.r], [vfm.r])
                    for t_ in range(4):
                        v = vf(t_)
                        S.op("dve", lambda e, o_=st6[:, 0, :], i_=v[:, 0:512]: e.bn_stats(out=o_, in_=i_), reads=[vfm.r], writes=[st6.r])
                        S.op("dve", lambda e, o_=st6[:, 1, :], i_=v[:, 512:1024]: e.bn_stats(out=o_, in_=i_), reads=[vfm.r, st6.r], writes=[st6.r])
                        S.op("dve", lambda e, o_=mv[:], i_=st6[:].rearrange("p a b -> p (a b)"): e.bn_aggr(out=o_, in_=i_), reads=[st6.r], writes=[mv.r])
                        tsc("dve", rstd1[:], mv[:, 1:2], EPS, None, ALU.add, None, [mv.r], [rstd1.r])
                        act(rstd1[:], rstd1[:], AF.Sqrt, [rstd1.r], [rstd1.r])
                        rcp(rstd1[:], rstd1[:], [rstd1.r], [rstd1.r])
                        tsc("dve", v, v, mv[:, 0:1], rstd1[:, 0:1], ALU.subtract, ALU.mult, [vfm.r, mv.r, rstd1.r], [vfm.r])
                        tt("dve", v, v, glng[:], ALU.mult, [vfm.r, glng.r], [vfm.r])
                        tt("dve", vbf[t_][:], v, glnb[:], ALU.add, [vfm.r, glnb.r], [vbf[t_].r])
                    def spatial_stage():
                        for gg in range(8):
                            pS = bring.next()
                            for t_ in range(4):
                                mmgroup(pS[:, t_ * 128:(t_ + 1) * 128], [(vbf[t_][:, gg * 128:(gg + 1) * 128], wsm[:, gg, :])], [vbf[t_].r, wsm.r], [pS.r])
                            tq = tmpr.next()
                            tt("dve", tq[:].rearrange("p (c i) -> p c i", c=4), pS[:, :].rearrange("p (c i) -> p c i", c=4),
                               bsb[:, gg * 128:(gg + 1) * 128].unsqueeze(1).to_broadcast([128, 4, 128]), ALU.add, [pS.r, bsb.r], [tq.r])
                            tt("dve", y_c[:, gg, :], tq[:], u_sb[:, gg, :], ALU.mult, [tq.r, u_sb.r], [y_c.r])
                    for bi, (gcol, p_ap, K, y_t) in enumerate(((GA, Wl["p_a"], D, y_a), (GB, Wl["p_b"], 512, y_b), (GC, Wl["p_c"], D, y_c))):
                        kt = K // 128
                        if bi == 2:
                            spatial_stage()
                        for jb in range(2):
                            wG_, wGr_ = wload(w_in, gcol + jb * 512, 512, D)
                            wP_, wPr_ = wload(p_ap, jb * 512, 512, K)
                            for jj in range(4):
                                j = jb * 4 + jj
                                pg, py = bring.next(), bring.next()
                                mmgroup(pg[:, :], [(wG_[:, k, jj * 128:(jj + 1) * 128], xc[:, k, :]) for k in range(NT)], [wGr_, xc.r], [pg.r])
                                if bi == 1:
                                    prs = [(wP_[:, k, jj * 128:(jj + 1) * 128], y_b[:, k, c0:c0 + SP]) for k in range(kt)]
                                else:
                                    prs = [(wP_[:, k, jj * 128:(jj + 1) * 128], y_t[:, k, :]) for k in range(kt)]
                                mmgroup(py[:, :], prs, [wPr_, y_t.r], [py.r])
                                sg = tmpr.next()
                                act(sg[:], pg[:, :], AF.Sigmoid, [pg.r], [sg.r])
                                if bi == 0:
                                    tt("dve", mf(j), sg[:], py[:, :], ALU.mult, [sg.r, py.r], [vfm.r])
                                elif bi == 1:
                                    tt("dve", sg[:], sg[:], py[:, :], ALU.mult, [sg.r, py.r], [sg.r])
                                    tt("dve", mf(j), mf(j), sg[:], ALU.add, [sg.r, vfm.r], [vfm.r])
                                else:
                                    tt("dve", sg[:], sg[:], py[:, :], ALU.mult, [sg.r, py.r], [sg.r])
                                    tt("dve", m_sb[:, j, :], mf(j), sg[:], ALU.add, [sg.r, vfm.r], [m_sb.r])
                    if s + 1 < NS:
                        dmac(vfm3, x_src[:, c0 + SP:c0 + 2 * SP].rearrange("(k p) t -> p k t", p=128), [vfm.r], reads=[xsrc_r])
                    lns = LNStats()
                    for jb in range(2):
                        wO, wOr = wload(Wl["w_o"], jb * 512, 512, D)
                        for jj in range(4):
                            j = jb * 4 + jj
                            po = bring.next()
                            mmgroup(po[:, :], [(wO[:, k, jj * 128:(jj + 1) * 128], m_sb[:, k, :]) for k in range(NT)], [wOr, m_sb.r], [po.r])
                            stt("dve", xs[:, j, :], xs[:, j, :], ALPHA, po[:, :], ALU.mult, ALU.add, [po.r, xs.r], [xs.r])
                            lns.tile_done(j)
                    lns.finish(ln1g, ln1b)
                    for fb in range(NF // 2):
                        wG_, wGr_ = wload(Wl["w_gate"], fb * 256, 256, D)
                        wU_, wUr_ = wload(Wl["w_up"], fb * 256, 256, D)
                        for ff in range(2):
                            f = fb * 2 + ff
                            pg, pu = bring.next(), bring.next()
                            mmgroup(pg[:, :], [(wG_[:, k, ff * 128:(ff + 1) * 128], xc[:, k, :]) for k in range(NT)], [wGr_, xc.r], [pg.r])
                            mmgroup(pu[:, :], [(wU_[:, k, ff * 128:(ff + 1) * 128], xc[:, k, :]) for k in range(NT)], [wUr_, xc.r], [pu.r])
                            sg = tmpr.next()
                            act(sg[:], pg[:, :], AF.Silu, [pg.r], [sg.r])
                            tt("dve", h_sb[:, f, :], sg[:], pu[:, :], ALU.mult, [sg.r, pu.r], [h_sb.r])
                    if s + 1 < NS:
                        act(xc[:, :, :], vfm3, AF.Identity, [vfm.r], [xc.r])
                    lns = LNStats()
                    for j in range(NT):
                        wD, wDr = wload(Wl["w_down"], j * 128, 128, DFF)
                        pd = bring.next()
                        mmgroup(pd[:, :], [(wD[:, k, :], h_sb[:, k, :]) for k in range(NF)], [wDr, h_sb.r], [pd.r])
                        stt("dve", xs[:, j, :], xs[:, j, :], ALPHA, pd[:, :], ALU.mult, ALU.add, [pd.r, xs.r], [xs.r])
                        lns.tile_done(j)
                    lns.finish(ln2g, ln2b, write_xc=False)
                    dmac(out_ap[:, c0:c0 + SP].rearrange("(k p) t -> p k t", p=128), xs[:, :, :], reads=[xs.r], writes=([out_r] if out_r is not None else []))
                S.barrier()
        if plan is None:
            return {"seq": dry_seq, "last_use": dict(S.last_use)}
        S.finish("sp")
        S.emit(top)
    return nc


def _consts():
    ident = np.eye(128, dtype=np.float32)
    k = np.arange(128)[:, None]
    q = np.arange(128)[None, :]
    prev = (k >= q).astype(np.float32)
    cur = (k <= q).astype(np.float32)
    cm = np.concatenate([prev, cur], axis=1)
    cmask = np.tile(cm, (1, 4))
    esel = np.zeros((128, 16), np.float32)
    for hs in range(4):
        esel[:, hs * 4 + hs] = 1.0
    selb = np.zeros((4, 256), np.float32)
    for pr in range(2):
        for col in range(128):
            selb[2 * pr + col // 64, pr * 128 + col] = 1.0
    ones = np.ones((128, 128), np.float32)
    tril = cur.copy()
    half = 32
    inv_freq = (np.float32(10000.0) ** (-np.arange(half, dtype=np.float32) / np.float32(half))).astype(np.float32)
    invf = np.tile(inv_freq, 4).reshape(128, 1).astype(np.float32)
    return dict(ident=ident, cmask=cmask, esel=esel, selb=selb, ones=ones, tril=tril, invf=invf)


def _qk_perm():
    idx = []
    for qd in range(2):
        for ab in range(2):
            for hs in range(4):
                h = 4 * qd + hs
                idx.extend(range(h * 64 + ab * 32, h * 64 + ab * 32 + 32))
    return np.array(idx)


def _layer_weights(l, w_in, conv_w, gmlp_ln_g, gmlp_ln_b, w_s, b_s, p_a, p_b, p_c, w_o, ln1_g, ln1_b,
                   w_gate, w_up, w_down, ln2_g, ln2_b, suffix):
    perm = _qk_perm()
    wi = np.array(w_in[l], dtype=np.float32, copy=True)
    for g in range(3):
        base = ATT0 + g * 1536
        wi[:, base:base + 512] = w_in[l][:, base + perm]
        wi[:, base + 512:base + 1024] = w_in[l][:, base + 512 + perm]

    def pj(v):
        return np.ascontiguousarray(np.asarray(v, np.float32).reshape(NT, 128).T)
    cw = np.asarray(conv_w[l], np.float32)
    convw = np.ascontiguousarray(cw.reshape(3, NT, 128).transpose(2, 1, 0).reshape(128, NT * 3))
    d = {
        "w_in": wi,
        "convw": convw,
        "glng": np.asarray(gmlp_ln_g[l], np.float32).reshape(1, D),
        "glnb": np.asarray(gmlp_ln_b[l], np.float32).reshape(1, D),
        "wsT": np.ascontiguousarray(np.asarray(w_s[l], np.float32).transpose(0, 2, 1)),
        "bs": np.asarray(b_s[l], np.float32).reshape(1, D),
        "p_a": np.asarray(p_a[l], np.float32),
        "p_b": np.asarray(p_b[l], np.float32),
        "p_c": np.asarray(p_c[l], np.float32),
        "w_o": np.asarray(w_o[l], np.float32),
        "ln1g": pj(ln1_g[l]), "ln1b": pj(ln1_b[l]),
        "w_gate": np.asarray(w_gate[l], np.float32),
        "w_up": np.asarray(w_up[l], np.float32),
        "w_down": np.asarray(w_down[l], np.float32),
        "ln2g": pj(ln2_g[l]), "ln2b": pj(ln2_b[l]),
    }
    return {k + suffix: np.ascontiguousarray(v) for k, v in d.items()}


_NC_CACHE = {}


def kernel(x, positions, w_in, conv_w, gmlp_ln_g, gmlp_ln_b, w_s, b_s, p_a, p_b, p_c, w_o,
           ln1_g, ln1_b, w_gate, w_up, w_down, ln2_g, ln2_b):
    x = np.asarray(x, np.float32)
    positions = np.asarray(positions, np.int32)
    B, Sq, _ = x.shape
    consts = _consts()
    if 2 not in _NC_CACHE:
        plan = build_program(2)
        _NC_CACHE[2] = build_program(2, plan=plan)
    nc = _NC_CACHE[2]
    lw = {}
    for l in range(DEPTH):
        lw.update(_layer_weights(l, w_in, conv_w, gmlp_ln_g, gmlp_ln_b, w_s, b_s, p_a, p_b, p_c, w_o, ln1_g, ln1_b,
                                 w_gate, w_up, w_down, ln2_g, ln2_b, str(l)))
    zx = np.zeros((D, T), np.float32)
    zp = np.zeros((T,), np.int32)

    def xt(b, q):
        return np.ascontiguousarray(x[b, q * T:(q + 1) * T, :].T) if q >= 0 else zx

    def pp(b, q):
        return positions[b, q * T:(q + 1) * T] if q >= 0 else zp

    in_maps = []
    for c in range(8):
        b, qtr = c // 4, c % 4
        pos = np.concatenate([pp(b, qtr - 2), pp(b, qtr - 1), pp(b, qtr)]).reshape(1, 3 * T).astype(np.int32)
        m = {"xT": xt(b, qtr), "xhT": xt(b, qtr - 1), "xh2T": xt(b, qtr - 2), "pos": pos,
             "flag": np.full((128, 1), 1.0 if qtr >= 1 else 0.0, np.float32),
             "flag2": np.full((128, 1), 1.0 if qtr >= 2 else 0.0, np.float32)}
        m.update(consts)
        m.update(lw)
        in_maps.append(m)
    res = run_bass_kernel_spmd(nc, in_maps, core_ids=list(range(8)))
    out = np.empty((B, Sq, D), np.float32)
    for c in range(8):
        out[c // 4, (c % 4) * T:(c % 4 + 1) * T, :] = np.asarray(res.results[c]["out"]).T
    return out
```

```python
import math
from contextlib import ExitStack

import numpy as np
import concourse.bass as bass
import concourse.mybir as mybir
from concourse.bass_utils import run_bass_kernel_spmd

F32 = mybir.dt.float32
BF16 = mybir.dt.bfloat16
I32 = mybir.dt.int32
AF = mybir.ActivationFunctionType
ALU = mybir.AluOpType

D = 1024
NT = 8
T = 2048
SP = 512
NS = T // SP
DFF = 2816
NF = DFF // 128
DEPTH = 2
ALPHA = (2 * DEPTH) ** 0.25
EPS = 1e-5
N_IN = 12800
GA, GB, GC, CB, CC, CH = 0, 1024, 2048, 3072, 4096, 5120
ATT0 = 6144
UO, VO = 10752, 11776
DILS = (1, 4, 16)
MAGIC = 12582912.0
TWO_PI = 2.0 * math.pi
C1 = 6.28125
C2 = TWO_PI - C1


class Res:
    def __init__(self, name=""):
        self.name = name
        self.last_w = None
        self.reads = []


class Sched:
    ENG = ["pe", "act", "dve", "pool", "sp"]

    def __init__(self, nc, ndma=16):
        self.nc = nc
        self.ops = {e: [] for e in self.ENG}
        self.cnt = {e: 0 for e in self.ENG}
        self.known = {e: {} for e in self.ENG}
        self.ndma = ndma
        self.dma_cnt = [0] * ndma
        self.n_sp = 8
        self.rr_sp = 0
        self.rr_pool = 0
        self.opclock = 0
        self.on_op = None
        self.last_use = None

    def _need(self, eng, tok, waits):
        if tok is None:
            return
        kind, key, val = tok
        if kind == 'e' and key == eng and eng == 'pe':
            return
        k = (kind, key)
        if self.known[eng].get(k, 0) >= val:
            return
        self.known[eng][k] = val
        waits[k] = max(waits.get(k, 0), val)

    def _deps(self, eng, reads, writes, waits):
        for r in reads:
            self._need(eng, r.last_w, waits)
        for w in writes:
            self._need(eng, w.last_w, waits)
            for t in w.reads:
                self._need(eng, t, waits)

    def _commit(self, tok, reads, writes):
        for r in reads:
            r.reads.append(tok)
            if len(r.reads) > 64:
                r.reads = r.reads[-48:]
        for w in writes:
            w.last_w = tok
            w.reads = []

    def op(self, eng, fn, reads=(), writes=()):
        self.opclock += 1
        if self.last_use is not None:
            for r in reads:
                w = getattr(r, "widx", None)
                if w is not None:
                    self.last_use[w] = self.opclock
        if self.on_op is not None:
            self.on_op()
        waits = {}
        self._deps(eng, reads, writes, waits)
        self.cnt[eng] += 1
        tok = ('e', eng, self.cnt[eng])
        self.ops[eng].append((list(waits.items()), fn, ('e', eng, 1)))
        self._commit(tok, reads, writes)
        return tok

    def dma(self, fn, reads=(), writes=(), eng="sp"):
        if eng == "sp":
            ch = self.rr_sp
            self.rr_sp = (self.rr_sp + 1) % self.n_sp
        else:
            ch = self.n_sp + self.rr_pool
            self.rr_pool = (self.rr_pool + 1) % (self.ndma - self.n_sp)
        waits = {}
        if self.dma_cnt[ch] > 0:
            self._need(eng, ('d', ch, self.dma_cnt[ch] * 16), waits)
        self._deps(eng, reads, writes, waits)
        self.dma_cnt[ch] += 1
        tok = ('d', ch, self.dma_cnt[ch] * 16)
        self.ops[eng].append((list(waits.items()), fn, ('d', ch, 16)))
        self._commit(tok, reads, writes)
        return tok

    def barrier(self):
        toks = [('e', e2, self.cnt[e2]) for e2 in self.ENG if self.cnt[e2] > 0]
        toks += [('d', i, self.dma_cnt[i] * 16) for i in range(self.ndma) if self.dma_cnt[i] > 0]
        for eng in self.ENG:
            waits = {}
            for t in toks:
                if t[0] == 'e' and t[1] == eng:
                    continue
                self._need(eng, t, waits)
            self.ops[eng].append((list(waits.items()), None, None))

    def finish(self, eng="sp"):
        waits = {}
        for i in range(self.ndma):
            if self.dma_cnt[i] > 0:
                self._need(eng, ('d', i, self.dma_cnt[i] * 16), waits)
        self.ops[eng].append((list(waits.items()), None, None))

    def emit(self, stack):
        nc = self.nc
        esem = {e: stack.enter_context(nc.semaphore("s_" + e)) for e in self.ENG}
        dsem = [stack.enter_context(nc.semaphore("d_%d" % i)) for i in range(self.ndma)]

        def semof(k):
            return esem[k[1]] if k[0] == 'e' else dsem[k[1]]

        block = stack.enter_context(nc.Block())

        def run(engname):
            def body(e):
                for waits, fn, inc in self.ops[engname]:
                    for k, v in waits:
                        e.wait_ge(semof(k), v)
                    if fn is not None:
                        ins = fn(e)
                        ins.then_inc(semof(inc), inc[2])
            return body
        block.tensor(run("pe"))
        block.scalar(run("act"))
        block.vector(run("dve"))
        block.gpsimd(run("pool"))
        block.sync(run("sp"))


class Tile:
    def __init__(self, h, name):
        self.h = h
        self.r = Res(name)

    def __getitem__(self, k):
        return self.h[k]


class Ring:
    def __init__(self, tiles):
        self.tiles = tiles
        self.i = 0

    def next(self):
        t = self.tiles[self.i % len(self.tiles)]
        self.i += 1
        return t


def build_program(n_layers, kstop=99, plan=None):
    nc = bass.Bass("TRN2", target_bir_lowering=False)

    def din(name, shape, dt=F32):
        return nc.dram_tensor(name, list(shape), dt, kind="ExternalInput").ap()

    xT_d = din("xT", [D, T])
    xhT_d = din("xhT", [D, T])
    xh2T_d = din("xh2T", [D, T])
    pos_d = din("pos", [1, 3 * T], I32)
    flag_d = din("flag", [128, 1])
    flag2_d = din("flag2", [128, 1])
    ident_d = din("ident", [128, 128])
    cmask_d = din("cmask", [128, 1024])
    esel_d = din("esel", [128, 16])
    selb_d = din("selb", [4, 256])
    ones_d = din("ones", [128, 128])
    tril_d = din("tril", [128, 128])
    invf_d = din("invf", [128, 1])
    W = []
    for l in range(n_layers):
        W.append(dict(
            w_in=din("w_in%d" % l, [D, N_IN]),
            convw=din("convw%d" % l, [128, NT * 3]),
            glng=din("glng%d" % l, [1, D]),
            glnb=din("glnb%d" % l, [1, D]),
            wsT=din("wsT%d" % l, [8, 128, 128]),
            bs=din("bs%d" % l, [1, D]),
            p_a=din("p_a%d" % l, [D, D]),
            p_b=din("p_b%d" % l, [512, D]),
            p_c=din("p_c%d" % l, [D, D]),
            w_o=din("w_o%d" % l, [D, D]),
            ln1g=din("ln1g%d" % l, [128, NT]),
            ln1b=din("ln1b%d" % l, [128, NT]),
            w_gate=din("w_gate%d" % l, [D, DFF]),
            w_up=din("w_up%d" % l, [D, DFF]),
            w_down=din("w_down%d" % l, [DFF, D]),
            ln2g=din("ln2g%d" % l, [128, NT]),
            ln2b=din("ln2b%d" % l, [128, NT]),
        ))
    out_d = nc.dram_tensor("out", [D, T], F32, kind="ExternalOutput").ap()
    x1_d = nc.dram_tensor("x1_scratch", [D, T], F32).ap()
    x1h_d = nc.dram_tensor("x1h_scratch", [D, T], F32).ap()

    S = Sched(nc, ndma=24)

    with ExitStack() as top:
        def sb(st, name, shape, dt):
            return Tile(st.enter_context(nc.sbuf_tensor("sb_" + name, list(shape), dt)), name)

        def psum(st, name, shape, dt=F32):
            return Tile(st.enter_context(nc.psum_tensor("ps_" + name, list(shape), dt)), name)

        y_b = sb(top, "y_b", [128, 4, T], BF16)
        ident = sb(top, "ident", [128, 128], BF16)
        cmask = sb(top, "cmask", [128, 1024], BF16)
        esel = sb(top, "esel", [128, 16], BF16)
        eselF = sb(top, "eselF", [128, 16], BF16)
        selb = sb(top, "selb", [4, 256], F32)
        ones = sb(top, "ones", [128, 128], BF16)
        tril = sb(top, "tril", [128, 128], F32)
        invf = sb(top, "invf", [128, 1], F32)
        flag = sb(top, "flag", [128, 1], F32)
        flag2 = sb(top, "flag2", [128, 1], F32)
        eselF2 = sb(top, "eselF2", [128, 16], BF16)
        zhist = sb(top, "zhist", [128, NT, 2], F32)
        wring = Ring([sb(top, "wbuf%d" % i, [128, 4096], BF16) for i in range(6)])
        tmpr = Ring([sb(top, "tmp%d" % i, [128, 512], F32) for i in range(8)])

        def dmac(out, in_, writes, reads=()):
            return S.dma(lambda e, o=out, i=in_: e.dma_start(out=o, in_=i), reads=reads, writes=writes, eng="pool")

        def dmas(out, in_, writes=(), reads=()):
            return S.dma(lambda e, o=out, i=in_: e.dma_start(out=o, in_=i), reads=reads, writes=writes, eng="sp")

        NBUF = len(wring.tiles)
        PF = 4
        wst = {"idx": 0, "next": 0}
        WAP = {}
        for Wl_ in W:
            for ap_ in Wl_.values():
                WAP[ap_.tensor.name] = ap_
        if plan is None:
            S.last_use = {}
            dry_seq = []

        wcache = {}
        wcount = {}
        if plan is not None:
            for key_ in plan["seq"]:
                wcount[key_] = wcount.get(key_, 0) + 1

        def _issue(k):
            key = plan["seq"][k]
            (nm, c0, width, K) = key
            kt = K // 128
            t = wring.tiles[k % NBUF]
            flat = t.h[:, 0:kt * width]
            view = flat.rearrange("p (k c) -> p k c", k=kt)
            if wcount[key] < 3:
                dmac(view, WAP[nm][:, c0:c0 + width].rearrange("(k p) c -> p k c", p=128), writes=[t.r])
            elif key not in wcache:
                dmac(view, WAP[nm][:, c0:c0 + width].rearrange("(k p) c -> p k c", p=128), writes=[t.r])
                sc = nc.dram_tensor("wcache_%d" % len(wcache), [128, kt * width], BF16).ap()
                r = Res("wc")
                wcache[key] = (sc, r)
                dmas(sc, flat, writes=[r], reads=[t.r])
            else:
                sc, r = wcache[key]
                dmas(flat, sc, writes=[t.r], reads=[r])

        def _issue_upto(limit):
            limit = min(limit, len(plan["seq"]) - 1)
            while wst["next"] <= limit:
                k = wst["next"]
                if k >= NBUF and S.opclock <= plan["last_use"].get(k - NBUF, 0):
                    break
                _issue(k)
                wst["next"] += 1

        if plan is not None:
            S.on_op = lambda: _issue_upto(wst["idx"] - 1 + PF)

        def wload(w_ap, c0, width, K):
            kt = K // 128
            key = (w_ap.tensor.name, c0, width, K)
            idx = wst["idx"]
            wst["idx"] += 1
            if plan is None:
                dry_seq.append(key)
                t = wring.next()
                view = t.h[:, 0:kt * width].rearrange("p (k c) -> p k c", k=kt)
                r = Res("w%d" % idx)
                r.widx = idx
                return view, r
            assert plan["seq"][idx] == key, (idx, key, plan["seq"][idx])
            _issue_upto(idx + PF)
            assert wst["next"] > idx, "weight ring too small: block %d not issuable" % idx
            t = wring.tiles[idx % NBUF]
            view = t.h[:, 0:kt * width].rearrange("p (k c) -> p k c", k=kt)
            return view, t.r

        def mmgroup(out_ap, pairs, reads, writes, tp=None):
            n = len(pairs)

            def fn(e, out_ap=out_ap, pairs=pairs, tp=tp):
                ins = None
                for i, (l, r) in enumerate(pairs):
                    if tp is None:
                        ins = e.matmul(out_ap, lhsT=l, rhs=r, start=(i == 0), stop=(i == n - 1))
                    else:
                        ins = e.matmul(out_ap, lhsT=l, rhs=r, start=(i == 0), stop=(i == n - 1), tile_position=tp)
                return ins
            return S.op("pe", fn, reads=reads, writes=writes)

        def tt(eng, out, in0, in1, op, reads, writes):
            return S.op(eng, lambda e: e.tensor_tensor(out=out, in0=in0, in1=in1, op=op), reads=reads, writes=writes)

        def tsc(eng, out, in0, s1, s2, op0, op1, reads, writes):
            if op1 is None:
                return S.op(eng, lambda e: e.tensor_scalar(out=out, in0=in0, scalar1=s1, scalar2=None, op0=op0), reads=reads, writes=writes)
            return S.op(eng, lambda e: e.tensor_scalar(out=out, in0=in0, scalar1=s1, scalar2=s2, op0=op0, op1=op1), reads=reads, writes=writes)

        def stt(eng, out, in0, scalar, in1, op0, op1, reads, writes):
            return S.op(eng, lambda e: e.scalar_tensor_tensor(out=out, in0=in0, scalar=scalar, in1=in1, op0=op0, op1=op1), reads=reads, writes=writes)

        def cpy(eng, out, in_, reads, writes):
            return S.op(eng, lambda e: e.tensor_copy(out=out, in_=in_), reads=reads, writes=writes)

        def rcp(out, in_, reads, writes):
            return S.op("dve", lambda e: e.reciprocal(out=out, in_=in_), reads=reads, writes=writes)

        def act(out, in_, func, reads, writes, **kw):
            return S.op("act", lambda e: e.activation(out=out, in_=in_, func=func, **kw), reads=reads, writes=writes)

        dmac(ident[:], ident_d, [ident.r])
        dmac(cmask[:], cmask_d, [cmask.r])
        dmac(esel[:], esel_d, [esel.r])
        dmac(ones[:], ones_d, [ones.r])
        dmas(selb[:], selb_d, [selb.r])
        dmas(tril[:], tril_d, [tril.r])
        dmas(invf[:], invf_d, [invf.r])
        dmas(flag[:], flag_d, [flag.r])
        dmas(flag2[:], flag2_d, [flag2.r])
        S.op("dve", lambda e: e.tensor_scalar(out=eselF2[:], in0=esel[:], scalar1=flag2[:, 0:1], scalar2=None, op0=ALU.mult),
             reads=[esel.r, flag2.r], writes=[eselF2.r])
        S.op("dve", lambda e: e.tensor_scalar(out=eselF[:], in0=esel[:], scalar1=flag[:, 0:1], scalar2=None, op0=ALU.mult),
             reads=[esel.r, flag.r], writes=[eselF.r])

        def rope_tables(p0, n, cos_ap, sin_ap, cos_r, sin_r, posi):
            posf, ta, tb = tmpr.next(), tmpr.next(), tmpr.next()
            dmas(posi[:, 0:n], pos_d[0:1, p0:p0 + n].partition_broadcast(128), writes=[posi.r])
            S.op("dve", lambda e: e.tensor_copy(out=posf[:, 0:n], in_=posi[:, 0:n]), reads=[posi.r], writes=[posf.r])
            S.op("dve", lambda e: e.tensor_scalar(out=posf[:, 0:n], in0=posf[:, 0:n], scalar1=invf[:, 0:1], scalar2=None, op0=ALU.mult),
                 reads=[posf.r, invf.r], writes=[posf.r])
            S.op("dve", lambda e: e.tensor_scalar(out=ta[:, 0:n], in0=posf[:, 0:n], scalar1=1.0 / TWO_PI, scalar2=MAGIC, op0=ALU.mult, op1=ALU.add),
                 reads=[posf.r], writes=[ta.r])
            S.op("dve", lambda e: e.tensor_scalar(out=ta[:, 0:n], in0=ta[:, 0:n], scalar1=MAGIC, scalar2=None, op0=ALU.subtract),
                 reads=[ta.r], writes=[ta.r])
            S.op("dve", lambda e: e.scalar_tensor_tensor(out=posf[:, 0:n], in0=ta[:, 0:n], scalar=-C1, in1=posf[:, 0:n], op0=ALU.mult, op1=ALU.add),
                 reads=[ta.r, posf.r], writes=[posf.r])
            S.op("dve", lambda e: e.scalar_tensor_tensor(out=posf[:, 0:n], in0=ta[:, 0:n], scalar=-C2, in1=posf[:, 0:n], op0=ALU.mult, op1=ALU.add),
                 reads=[ta.r, posf.r], writes=[posf.r])
            S.op("dve", lambda e: e.tensor_scalar(out=posf[:, 0:n], in0=posf[:, 0:n], scalar1=-3.1415925, scalar2=3.1415925, op0=ALU.max, op1=ALU.min),
                 reads=[posf.r], writes=[posf.r])
            act(sin_ap, posf[:, 0:n], AF.Sin, [posf.r], [sin_r])
            S.op("dve", lambda e: e.scalar_tensor_tensor(out=tb[:, 0:n], in0=posf[:, 0:n], scalar=-1.0, in1=posf[:, 0:n], op0=ALU.mult, op1=ALU.max),
                 reads=[posf.r], writes=[tb.r])
            S.op("dve", lambda e: e.tensor_scalar(out=tb[:, 0:n], in0=tb[:, 0:n], scalar1=-1.0, scalar2=math.pi / 2, op0=ALU.mult, op1=ALU.add),
                 reads=[tb.r], writes=[tb.r])
            act(cos_ap, tb[:, 0:n], AF.Sin, [tb.r], [cos_r])

        xin_r = Res("xin")
        x1_r = Res("x1")
        x1h_r = Res("x1h")
        if n_layers == 1:
            passes = [(0, xT_d, xhT_d, T, flag, eselF, out_d, xin_r, xin_r, None)]
        else:
            passes = [
                (0, xhT_d, xh2T_d, 0, flag2, eselF2, x1h_d, xin_r, xin_r, x1h_r),
                (0, xT_d, xhT_d, T, flag, eselF, x1_d, xin_r, xin_r, x1_r),
                (1, x1_d, x1h_d, T, flag, eselF, out_d, x1_r, x1h_r, None),
            ]
        for pi, (l, x_src, xh_src, pos0, flag_t, eselF_t, out_ap, xsrc_r, xhsrc_r, out_r) in enumerate(passes):
            Wl = W[l]
            w_in = Wl["w_in"]
            with ExitStack() as pa:
                bring = Ring([psum(pa, "bankA%d_%d" % (i, pi), [128, 512]) for i in range(7)])
                bank_t = psum(pa, "bank_t_%d" % pi, [128, 1024], BF16)
                cos_o = sb(pa, "cos_o" + "_%d" % pi, [128, T], F32)
                sin_o = sb(pa, "sin_o" + "_%d" % pi, [128, T], F32)
                cos_h = sb(pa, "cos_h" + "_%d" % pi, [128, SP], F32)
                sin_h = sb(pa, "sin_h" + "_%d" % pi, [128, SP], F32)
                posi = sb(pa, "posi" + "_%d" % pi, [128, SP], I32)
                acc = sb(pa, "acc" + "_%d" % pi, [128, 2, T], F32)
                accd = sb(pa, "accd" + "_%d" % pi, [4, T], F32)
                KT = sb(pa, "KT" + "_%d" % pi, [128, 2, 2 * T], BF16)
                VT = sb(pa, "VT" + "_%d" % pi, [128, 2, 2 * T], BF16)
                QT = sb(pa, "QT" + "_%d" % pi, [128, 2, T], BF16)
                vbr = Ring([sb(pa, "vb%d_%d" % (i, pi), [128, 256], BF16) for i in range(6)])
                P_t = Ring([sb(pa, "P%d_%d" % (i, pi), [128, 1024], BF16) for i in range(3)])
                xcr = Ring([sb(pa, "xca%d_%d" % (i, pi), [128, NT, SP], BF16) for i in range(2)])
                for s in range(NS):
                    rope_tables(pos0 + T + s * SP, SP, cos_o[:, s * SP:(s + 1) * SP], sin_o[:, s * SP:(s + 1) * SP], cos_o.r, sin_o.r, posi)

                def chunk_list(dil_):
                    HL_ = 128 * dil_
                    ch_ = []
                    hs0_ = T - HL_
                    n_h_ = min(HL_, SP)
                    for a_ in range(hs0_, T, n_h_):
                        ch_.append((True, a_, n_h_, a_ - hs0_))
                    for s_ in range(NS):
                        ch_.append((False, s_ * SP, SP, HL_ + s_ * SP))
                    return ch_

                def issue_first(chunks_):
                    (is_h_, t0_, n_, c0_) = chunks_[0]
                    xc_ = xcr.next()
                    src_ = xh_src if is_h_ else x_src
                    dmac(xc_[:, :, 0:n_], src_[:, t0_:t0_ + n_].rearrange("(k p) t -> p k t", p=128), [xc_.r], reads=[xhsrc_r if is_h_ else xsrc_r])
                    return xc_
                phases = [(qd_, g_) for qd_ in range(2) for g_ in range(3)]
                pre_first = {0: issue_first(chunk_list(DILS[0]))}
                for qd in range(2):
                    if kstop <= 1:
                        break
                    S.op("pool", lambda e, acc=acc: e.memset(acc[:], 0.0), writes=[acc.r])
                    S.op("pool", lambda e, accd=accd: e.memset(accd[:], 0.0), writes=[accd.r])
                    for g, dil in enumerate(DILS):
                        HL = 128 * dil
                        base = ATT0 + g * 1536
                        wq_v, wq_r = wload(w_in, base + qd * 256, 256, D)
                        wk_v, wk_r = wload(w_in, base + 512 + qd * 256, 256, D)
                        wv_v, wv_r = wload(w_in, base + 1024 + qd * 256, 256, D)
                        chunks = []
                        hs0 = T - HL
                        n_h = min(HL, SP)
                        for a in range(hs0, T, n_h):
                            chunks.append((True, a, n_h, a - hs0))
                        for s in range(NS):
                            chunks.append((False, s * SP, SP, HL + s * SP))
                        loaded = {}

                        def issue_chunk(ci, chunks=chunks, loaded=loaded):
                            (is_h_, t0_, n_, c0_) = chunks[ci]
                            xc_ = xcr.next()
                            src_ = xh_src if is_h_ else x_src
                            dmac(xc_[:, :, 0:n_], src_[:, t0_:t0_ + n_].rearrange("(k p) t -> p k t", p=128), [xc_.r], reads=[xhsrc_r if is_h_ else xsrc_r])
                            loaded[ci] = xc_
                        ph_i = qd * 3 + g
                        loaded[0] = pre_first.pop(ph_i)
                        for ci, (is_h, t0, n, c0) in enumerate(chunks):
                            if ci + 1 < len(chunks):
                                issue_chunk(ci + 1)
                            xc = loaded[ci]
                            if is_h:
                                rope_tables(pos0 + t0, n, cos_h[:, 0:n], sin_h[:, 0:n], cos_h.r, sin_h.r, posi)
                                cs_ap, sn_ap, cs_r, sn_r = cos_h[:, 0:n], sin_h[:, 0:n], cos_h.r, sin_h.r
                            else:
                                cs_ap, sn_ap, cs_r, sn_r = cos_o[:, t0:t0 + n], sin_o[:, t0:t0 + n], cos_o.r, sin_o.r
                            todo = [(wk_v, wk_r, KT, c0)]
                            if not is_h:
                                todo.append((wq_v, wq_r, QT, t0))
                            for (wt, wr, dst, dc0) in todo:
                                pA = bring.next()
                                pB = bring.next()
                                mmgroup(pA[:, 0:n], [(wt[:, k, 0:128], xc[:, k, 0:n]) for k in range(NT)], [wr, xc.r], [pA.r])
                                mmgroup(pB[:, 0:n], [(wt[:, k, 128:256], xc[:, k, 0:n]) for k in range(NT)], [wr, xc.r], [pB.r])
                                t1, t2, t3, t4 = tmpr.next(), tmpr.next(), tmpr.next(), tmpr.next()
                                tt("dve", t1[:, 0:n], pA[:, 0:n], cs_ap, ALU.mult, [pA.r, cs_r], [t1.r])
                                tt("dve", t2[:, 0:n], pB[:, 0:n], sn_ap, ALU.mult, [pB.r, sn_r], [t2.r])
                                tt("pool", dst[:, 0, dc0:dc0 + n], t1[:, 0:n], t2[:, 0:n], ALU.subtract, [t1.r, t2.r], [dst.r])
                                tt("dve", t3[:, 0:n], pB[:, 0:n], cs_ap, ALU.mult, [pB.r, cs_r], [t3.r])
                                tt("dve", t4[:, 0:n], pA[:, 0:n], sn_ap, ALU.mult, [pA.r, sn_r], [t4.r])
                                tt("pool", dst[:, 1, dc0:dc0 + n], t3[:, 0:n], t4[:, 0:n], ALU.add, [t3.r, t4.r], [dst.r])
                            for vt in range(2):
                                pV = bring.next()
                                mmgroup(pV[:, 0:n], [(wv_v[:, k, vt * 128:(vt + 1) * 128], xc[:, k, 0:n]) for k in range(NT)], [wv_r, xc.r], [pV.r])
                                act(VT[:, vt, c0:c0 + n], pV[:, 0:n], AF.Copy, [pV.r], [VT.r])
                        if kstop <= 2:
                            break
                        if ph_i + 1 < len(phases):
                            pre_first[ph_i + 1] = issue_first(chunk_list(DILS[phases[ph_i + 1][1]]))
                        nb = T // (128 * dil)
                        Wd = HL + T

                        def kview(tl, lo, hi, ab, m, r, dil=dil, Wd=Wd):
                            return tl.h[lo:hi, ab, 0:Wd].rearrange("p (m i r) -> p m r i", i=128, r=dil)[:, m, r, :]

                        def qview(tl, lo, hi, ab, m, r, dil=dil):
                            return tl.h[lo:hi, ab, 0:T].rearrange("p (m i r) -> p m r i", i=128, r=dil)[:, m, r, :]

                        pending = [None]
                        for r in range(dil):
                            vprev = None
                            for m in range(nb + 1):
                                vb = vbr.next()

                                v0_ap = kview(VT, 0, 128, 0, m, r)
                                v1_ap = kview(VT, 0, 128, 1, m, r)

                                def tr2(e, v0_ap=v0_ap, v1_ap=v1_ap, o0=bank_t[:, 0:128], o1=bank_t[:, 128:256], idn=ident[:]):
                                    e.transpose(out=o0, in_=v0_ap, identity=idn)
                                    return e.transpose(out=o1, in_=v1_ap, identity=idn)
                                S.op("pe", tr2, reads=[VT.r, ident.r], writes=[bank_t.r])
                                if m == 0:
                                    act(vb[:], bank_t[:, 0:256], AF.Copy, [bank_t.r, flag_t.r], [vb.r], scale=flag_t[:, 0:1])
                                    vprev = vb
                                    continue
                                act(vb[:], bank_t[:, 0:256], AF.Copy, [bank_t.r], [vb.r])
                                n_q = m - 1
                                sbks = [bring.next() for _ in range(4)]
                                for hs in range(4):
                                    sbk = sbks[hs]
                                    lo, hi = 32 * hs, 32 * hs + 32
                                    for kt in range(2):
                                        o0 = kt * 128
                                        mmgroup(sbk[:, o0:o0 + 128],
                                                [(kview(KT, lo, hi, 0, n_q + kt, r), qview(QT, lo, hi, 0, n_q, r)),
                                                 (kview(KT, lo, hi, 1, n_q + kt, r), qview(QT, lo, hi, 1, n_q, r))],
                                                [KT.r, QT.r], [sbk.r], tp=(32 * hs, 0))
                                P = P_t.next()
                                for hs in range(4):
                                    act(P[:, hs * 256:(hs + 1) * 256], sbks[hs][:, 0:256], AF.Exp, [sbks[hs].r], [P.r], scale=0.125)
                                tt("pool", P[:], P[:], cmask[:], ALU.mult, [P.r, cmask.r], [P.r])
                                def pv_stage(vprev=vprev, vb=vb, P=P, m=m, n_q=n_q, r=r, dil=dil, acc=acc, accd=accd):
                                    nd = bring.next()
                                    vbs = (vprev, vb)
                                    for pr in range(2):
                                        for hh in range(2):
                                            hs = 2 * pr + hh
                                            mmgroup(nd[64 * hh:64 * hh + 64, pr * 128:(pr + 1) * 128],
                                                    [(vbs[kt][:, hs * 64:(hs + 1) * 64], P[:, hs * 256 + kt * 128: hs * 256 + kt * 128 + 128]) for kt in range(2)],
                                                    [vprev.r, vb.r, P.r], [nd.r], tp=(0, 64 * hh))
                                    es_prev = eselF_t if m == 1 else esel
                                    pairs = []
                                    for hs in range(4):
                                        pairs.append((es_prev[:, hs * 4:(hs + 1) * 4], P[:, hs * 256: hs * 256 + 128]))
                                        pairs.append((esel[:, hs * 4:(hs + 1) * 4], P[:, hs * 256 + 128: hs * 256 + 256]))
                                    mmgroup(nd[0:4, 256:384], pairs, [P.r, esel.r, eselF_t.r], [nd.r])
                                    accv = acc.h[:, :, :].rearrange("p a (m i r) -> p a m r i", i=128, r=dil)[:, :, n_q, r, :]
                                    tt("dve", accv, accv, nd[:, 0:256].rearrange("p (a i) -> p a i", a=2), ALU.add, [nd.r, acc.r], [acc.r])
                                    adv = accd.h[:, :].rearrange("p (m i r) -> p m r i", i=128, r=dil)[:, n_q, r, :]
                                    tt("dve", adv, adv, nd[0:4, 256:384], ALU.add, [nd.r, accd.r], [accd.r])
                                if pending[0] is not None:
                                    pending[0]()
                                pending[0] = pv_stage
                                vprev = vb
                        if pending[0] is not None:
                            pending[0]()
                            pending[0] = None
                    if kstop <= 3:
                        break
                    rcp(accd[:], accd[:], [accd.r], [accd.r])
                    if kstop <= 4:
                        break
                    for pr in range(2):
                        for s in range(NS):
                            bc = bring.next()
                            mmgroup(bc[:, :], [(selb[:, pr * 128:(pr + 1) * 128], accd[:, s * SP:(s + 1) * SP])], [selb.r, accd.r], [bc.r])
                            tt("dve", y_b[:, 2 * qd + pr, s * SP:(s + 1) * SP], acc[:, pr, s * SP:(s + 1) * SP], bc[:, :], ALU.mult,
                               [bc.r, acc.r], [y_b.r])

                S.barrier()
            if kstop <= 5:
                break
            with ExitStack() as pb:
                bring = Ring([psum(pb, "bankB%d_%d" % (i, pi), [128, 512]) for i in range(6)])
                stat_banks = [psum(pb, "bankS%d_%d" % (i, pi), [128, 512]) for i in range(2)]
                xs = sb(pb, "xs" + "_%d" % pi, [128, NT, SP], F32)
                xc = sb(pb, "xcb" + "_%d" % pi, [128, NT, SP], BF16)
                vfm = sb(pb, "vfm" + "_%d" % pi, [128, 4 * D], F32)
                u_sb = sb(pb, "u_sb" + "_%d" % pi, [128, NT, SP], BF16)
                y_a = sb(pb, "y_a" + "_%d" % pi, [128, NT, SP], BF16)
                y_c = sb(pb, "y_c" + "_%d" % pi, [128, NT, SP], BF16)
                vbf = [sb(pb, "vbf%d_%d" % (i, pi), [128, D], BF16) for i in range(4)]
                m_sb = u_sb
                h_sb = sb(pb, "h_sb" + "_%d" % pi, [128, NF, SP], BF16)
                zbuf = sb(pb, "zbuf" + "_%d" % pi, [128, SP + 2], F32)
                convw = sb(pb, "convw" + "_%d" % pi, [128, NT * 3], F32)
                glng = sb(pb, "glng" + "_%d" % pi, [128, D], F32)
                glnb = sb(pb, "glnb" + "_%d" % pi, [128, D], F32)
                bsb = sb(pb, "bsb" + "_%d" % pi, [128, D], F32)
                wsf = sb(pb, "wsf" + "_%d" % pi, [128, 8, 128], F32)
                wsm = sb(pb, "wsm" + "_%d" % pi, [128, 8, 128], BF16)
                ln1g = sb(pb, "ln1g" + "_%d" % pi, [128, NT], F32)
                ln1b = sb(pb, "ln1b" + "_%d" % pi, [128, NT], F32)
                ln2g = sb(pb, "ln2g" + "_%d" % pi, [128, NT], F32)
                ln2b = sb(pb, "ln2b" + "_%d" % pi, [128, NT], F32)
                st6 = sb(pb, "st6" + "_%d" % pi, [128, 2, 6], F32)
                mv = sb(pb, "mv" + "_%d" % pi, [128, 2], F32)
                rstd1 = sb(pb, "rstd1" + "_%d" % pi, [128, 1], F32)
                xh16 = sb(pb, "xh16" + "_%d" % pi, [128, NT, 16], BF16)
                mean_t = sb(pb, "mean_t" + "_%d" % pi, [128, SP], F32)
                rstd_t = sb(pb, "rstd_t" + "_%d" % pi, [128, SP], F32)

                dmas(convw[:], Wl["convw"], [convw.r])
                dmas(glng[:], Wl["glng"].partition_broadcast(128), [glng.r])
                dmas(glnb[:], Wl["glnb"].partition_broadcast(128), [glnb.r])
                dmas(bsb[:], Wl["bs"].partition_broadcast(128), [bsb.r])
                dmas(wsf[:], Wl["wsT"].rearrange("g j i -> j g i"), [wsf.r])
                dmas(ln1g[:], Wl["ln1g"], [ln1g.r])
                dmas(ln1b[:], Wl["ln1b"], [ln1b.r])
                dmas(ln2g[:], Wl["ln2g"], [ln2g.r])
                dmas(ln2b[:], Wl["ln2b"], [ln2b.r])
                tt("dve", wsm[:], wsf[:], tril[:].unsqueeze(1).to_broadcast([128, 8, 128]), ALU.mult, [wsf.r, tril.r], [wsm.r])
                dmac(xh16[:], xh_src[:, T - 16:T].rearrange("(k p) t -> p k t", p=128), [xh16.r], reads=[xhsrc_r])
                for jb in range(2):
                    wC, wCr = wload(w_in, CC + jb * 512, 512, D)
                    wH, wHr = wload(w_in, CH + jb * 512, 512, D)
                    for jj in range(4):
                        j = jb * 4 + jj
                        pC = bring.next()
                        pH = bring.next()
                        mmgroup(pC[:, 0:2], [(wC[:, k, jj * 128:(jj + 1) * 128], xh16[:, k, 14:16]) for k in range(NT)], [wCr, xh16.r], [pC.r])
                        mmgroup(pH[:, 0:2], [(wH[:, k, jj * 128:(jj + 1) * 128], xh16[:, k, 14:16]) for k in range(NT)], [wHr, xh16.r], [pH.r])
                        tz = tmpr.next()
                        act(tz[:, 0:2], pC[:, 0:2], AF.Copy, [pC.r, flag_t.r], [tz.r], scale=flag_t[:, 0:1])
                        tt("dve", zhist[:, j, :], tz[:, 0:2], pH[:, 0:2], ALU.mult, [tz.r, pH.r], [zhist.r])

                class LNStats:
                    def __init__(self):
                        self.rb, self.rsq = y_a, y_c
                        self.pm = stat_banks[0]
                        self.pq = stat_banks[1]
                        self.lag = None

                    def _mm(self, j):
                        rb, rsq, pm, pq = self.rb, self.rsq, self.pm, self.pq
                        S.op("pe", lambda e: e.matmul(pm[:, :], lhsT=ones[:], rhs=rb[:, j, :], start=(j == 0), stop=(j == NT - 1)),
                             reads=[ones.r, rb.r], writes=[pm.r])
                        S.op("pe", lambda e: e.matmul(pq[:, :], lhsT=ones[:], rhs=rsq[:, j, :], start=(j == 0), stop=(j == NT - 1)),
                             reads=[ones.r, rsq.r], writes=[pq.r])

                    def tile_done(self, j):
                        act(self.rb[:, j, :], xs[:, j, :], AF.Identity, [xs.r], [self.rb.r])
                        tt("dve", self.rsq[:, j, :], xs[:, j, :], xs[:, j, :], ALU.mult, [xs.r], [self.rsq.r])
                        if self.lag is not None:
                            self._mm(self.lag)
                        self.lag = j

                    def finish(self, g_t, b_t, write_xc=True):
                        self._mm(self.lag)
                        pm, pq = self.pm, self.pq
                        msq = tmpr.next()
                        tsc("dve", mean_t[:], pm[:, :], 1.0 / D, None, ALU.mult, None, [pm.r], [mean_t.r])
                        tt("dve", msq[:], mean_t[:], mean_t[:], ALU.mult, [mean_t.r], [msq.r])
                        stt("dve", msq[:], pq[:, :], 1.0 / D, msq[:], ALU.mult, ALU.subtract, [pq.r, msq.r], [msq.r])
                        tsc("dve", msq[:], msq[:], EPS, None, ALU.add, None, [msq.r], [msq.r])
                        act(msq[:], msq[:], AF.Sqrt, [msq.r], [msq.r])
                        rcp(rstd_t[:], msq[:], [msq.r], [rstd_t.r])
                        ts_ = []
                        for j in range(NT):
                            t = tmpr.next()
                            eng = "dve" if j % 2 == 0 else "pool"
                            tt(eng, t[:], xs[:, j, :], mean_t[:], ALU.subtract, [xs.r, mean_t.r], [t.r])
                            tt(eng, t[:], t[:], rstd_t[:], ALU.mult, [t.r, rstd_t.r], [t.r])
                            if write_xc:
                                act(xc[:, j, :], t[:], AF.Identity, [t.r, g_t.r, b_t.r], [xc.r], scale=g_t[:, j:j + 1], bias=b_t[:, j:j + 1])
                            ts_.append(t)
                        for j in range(NT):
                            t = ts_[j]
                            act(xs[:, j, :], t[:], AF.Identity, [t.r, g_t.r, b_t.r], [xs.r], scale=g_t[:, j:j + 1], bias=b_t[:, j:j + 1])

                mf = lambda j: vfm[:, j * SP:(j + 1) * SP]
                vf = lambda t_: vfm[:, t_ * D:(t_ + 1) * D]

                for s in range(NS):
                    c0 = s * SP
                    vfm3 = vfm.h[:, :].rearrange("p (k t) -> p k t", k=NT)
                    if s == 0:
                        dmac(xs[:, :, :], x_src[:, c0:c0 + SP].rearrange("(k p) t -> p k t", p=128), [xs.r], reads=[xsrc_r])
                        act(xc[:, :, :], xs[:, :, :], AF.Copy, [xs.r], [xc.r])
                    else:
                        cpy("pool", xs[:, :, :], vfm3, [vfm.r], [xs.r])
                    for jb in range(2):
                        wB, wBr = wload(w_in, CB + jb * 512, 512, D)
                        wC, wCr = wload(w_in, CC + jb * 512, 512, D)
                        wH, wHr = wload(w_in, CH + jb * 512, 512, D)
                        for jj in range(4):
                            j = jb * 4 + jj
                            pB_, pC, pH = bring.next(), bring.next(), bring.next()
                            mmgroup(pC[:, :], [(wC[:, k, jj * 128:(jj + 1) * 128], xc[:, k, :]) for k in range(NT)], [wCr, xc.r], [pC.r])
                            mmgroup(pH[:, :], [(wH[:, k, jj * 128:(jj + 1) * 128], xc[:, k, :]) for k in range(NT)], [wHr, xc.r], [pH.r])
                            mmgroup(pB_[:, :], [(wB[:, k, jj * 128:(jj + 1) * 128], xc[:, k, :]) for k in range(NT)], [wBr, xc.r], [pB_.r])
                            tc_ = tmpr.next()
                            ta_ = tmpr.next()
                            act(tc_[:], pC[:, :], AF.Copy, [pC.r], [tc_.r])
                            act(zbuf[:, 0:2], zhist[:, j, :], AF.Copy, [zhist.r], [zbuf.r])
                            tt("dve", zbuf[:, 2:SP + 2], tc_[:], pH[:, :], ALU.mult, [tc_.r, pH.r], [zbuf.r])
                            act(zhist[:, j, :], zbuf[:, SP:SP + 2], AF.Copy, [zbuf.r], [zhist.r])
                            tsc("dve", ta_[:], zbuf[:, 0:SP], convw[:, 3 * j:3 * j + 1], None, ALU.mult, None, [zbuf.r, convw.r], [ta_.r])
                            stt("dve", ta_[:], zbuf[:, 1:SP + 1], convw[:, 3 * j + 1:3 * j + 2], ta_[:], ALU.mult, ALU.add, [zbuf.r, convw.r, ta_.r], [ta_.r])
                            stt("dve", ta_[:], zbuf[:, 2:SP + 2], convw[:, 3 * j + 2:3 * j + 3], ta_[:], ALU.mult, ALU.add, [zbuf.r, convw.r, ta_.r], [ta_.r])
                            tt("dve", y_a[:, j, :], ta_[:], pB_[:, :], ALU.mult, [ta_.r, pB_.r], [y_a.r])
                    for jb in range(2):
                        wU, wUr = wload(w_in, UO + jb * 512, 512, D)
                        for jj in range(4):
                            j = jb * 4 + jj
                            pU = bring.next()
                            mmgroup(pU[:, :], [(wU[:, k, jj * 128:(jj + 1) * 128], xc[:, k, :]) for k in range(NT)], [wUr, xc.r], [pU.r])
                            act(u_sb[:, j, :], pU[:, :], AF.Gelu, [pU.r], [u_sb.r])
                    for half in range(2):
                        wV, wVr = wload(w_in, VO + half * 512, 512, D)
                        for t_ in range(4):
                            pV = bring.next()
                            mmgroup(pV[:, :], [(xc[:, k, t_ * 128:(t_ + 1) * 128], wV[:, k, :]) for k in range(NT)], [wVr, xc.r], [pV.r])
                            act(vfm[:, t_ * D + half * 512: t_ * D + (half + 1) * 512], pV[:, :], AF.Gelu, [pV.r], [vfm.r])
                    for t_ in range(4):
                        v = vf(t_)
                        S.op("dve", lambda e, o_=st6[:, 0, :], i_=v[:, 0:512]: e.bn_stats(out=o_, in_=i_), reads=[vfm.r], writes=[st6.r])
                        S.op("dve", lambda e, o_=st6[:, 1, :], i_=v[:, 512:1024]: e.bn_stats(out=o_, in_=i_), reads=[vfm.r, st6.r], writes=[st6.r])
                        S.op("dve", lambda e, o_=mv[:], i_=st6[:].rearrange("p a b -> p (a b)"): e.bn_aggr(out=o_, in_=i_), reads=[st6.r], writes=[mv.r])
                        tsc("dve", rstd1[:], mv[:, 1:2], EPS, None, ALU.add, None, [mv.r], [rstd1.r])
                        act(rstd1[:], rstd1[:], AF.Sqrt, [rstd1.r], [rstd1.r])
                        rcp(rstd1[:], rstd1[:], [rstd1.r], [rstd1.r])
                        tsc("dve", v, v, mv[:, 0:1], rstd1[:, 0:1], ALU.subtract, ALU.mult, [vfm.r, mv.r, rstd1.r], [vfm.r])
                        tt("dve", v, v, glng[:], ALU.mult, [vfm.r, glng.r], [vfm.r])
                        tt("dve", vbf[t_][:], v, glnb[:], ALU.add, [vfm.r, glnb.r], [vbf[t_].r])
                    def spatial_stage():
                        for gg in range(8):
                            pS = bring.next()
                            for t_ in range(4):
                                mmgroup(pS[:, t_ * 128:(t_ + 1) * 128], [(vbf[t_][:, gg * 128:(gg + 1) * 128], wsm[:, gg, :])], [vbf[t_].r, wsm.r], [pS.r])
                            tq = tmpr.next()
                            tt("dve", tq[:].rearrange("p (c i) -> p c i", c=4), pS[:, :].rearrange("p (c i) -> p c i", c=4),
                               bsb[:, gg * 128:(gg + 1) * 128].unsqueeze(1).to_broadcast([128, 4, 128]), ALU.add, [pS.r, bsb.r], [tq.r])
                            tt("dve", y_c[:, gg, :], tq[:], u_sb[:, gg, :], ALU.mult, [tq.r, u_sb.r], [y_c.r])
                    for bi, (gcol, p_ap, K, y_t) in enumerate(((GA, Wl["p_a"], D, y_a), (GB, Wl["p_b"], 512, y_b), (GC, Wl["p_c"], D, y_c))):
                        kt = K // 128
                        if bi == 2:
                            spatial_stage()
                        for jb in range(2):
                            wG_, wGr_ = wload(w_in, gcol + jb * 512, 512, D)
                            wP_, wPr_ = wload(p_ap, jb * 512, 512, K)
                            for jj in range(4):
                                j = jb * 4 + jj
                                pg, py = bring.next(), bring.next()
                                mmgroup(pg[:, :], [(wG_[:, k, jj * 128:(jj + 1) * 128], xc[:, k, :]) for k in range(NT)], [wGr_, xc.r], [pg.r])
                                if bi == 1:
                                    prs = [(wP_[:, k, jj * 128:(jj + 1) * 128], y_b[:, k, c0:c0 + SP]) for k in range(kt)]
                                else:
                                    prs = [(wP_[:, k, jj * 128:(jj + 1) * 128], y_t[:, k, :]) for k in range(kt)]
                                mmgroup(py[:, :], prs, [wPr_, y_t.r], [py.r])
                                sg = tmpr.next()
                                act(sg[:], pg[:, :], AF.Sigmoid, [pg.r], [sg.r])
                                if bi == 0:
                                    tt("dve", mf(j), sg[:], py[:, :], ALU.mult, [sg.r, py.r], [vfm.r])
                                elif bi == 1:
                                    tt("dve", sg[:], sg[:], py[:, :], ALU.mult, [sg.r, py.r], [sg.r])
                                    tt("dve", mf(j), mf(j), sg[:], ALU.add, [sg.r, vfm.r], [vfm.r])
                                else:
                                    tt("dve", sg[:], sg[:], py[:, :], ALU.mult, [sg.r, py.r], [sg.r])
                                    tt("dve", m_sb[:, j, :], mf(j), sg[:], ALU.add, [sg.r, vfm.r], [m_sb.r])
                    if s + 1 < NS:
                        dmac(vfm3, x_src[:, c0 + SP:c0 + 2 * SP].rearrange("(k p) t -> p k t", p=128), [vfm.r], reads=[xsrc_r])
                    lns = LNStats()
                    for jb in range(2):
                        wO, wOr = wload(Wl["w_o"], jb * 512, 512, D)
                        for jj in range(4):
                            j = jb * 4 + jj
                            po = bring.next()
                            mmgroup(po[:, :], [(wO[:, k, jj * 128:(jj + 1) * 128], m_sb[:, k, :]) for k in range(NT)], [wOr, m_sb.r], [po.r])
                            stt("dve", xs[:, j, :], xs[:, j, :], ALPHA, po[:, :], ALU.mult, ALU.add, [po.r, xs.r], [xs.r])
                            lns.tile_done(j)
                    lns.finish(ln1g, ln1b)
                    for fb in range(NF // 2):
                        wG_, wGr_ = wload(Wl["w_gate"], fb * 256, 256, D)
                        wU_, wUr_ = wload(Wl["w_up"], fb * 256, 256, D)
                        for ff in range(2):
                            f = fb * 2 + ff
                            pg, pu = bring.next(), bring.next()
                            mmgroup(pg[:, :], [(wG_[:, k, ff * 128:(ff + 1) * 128], xc[:, k, :]) for k in range(NT)], [wGr_, xc.r], [pg.r])
                            mmgroup(pu[:, :], [(wU_[:, k, ff * 128:(ff + 1) * 128], xc[:, k, :]) for k in range(NT)], [wUr_, xc.r], [pu.r])
                            sg = tmpr.next()
                            act(sg[:], pg[:, :], AF.Silu, [pg.r], [sg.r])
                            tt("dve", h_sb[:, f, :], sg[:], pu[:, :], ALU.mult, [sg.r, pu.r], [h_sb.r])
                    if s + 1 < NS:
                        act(xc[:, :, :], vfm3, AF.Identity, [vfm.r], [xc.r])
                    lns = LNStats()
                    for j in range(NT):
                        wD, wDr = wload(Wl["w_down"], j * 128, 128, DFF)
                        pd = bring.next()
                        mmgroup(pd[:, :], [(wD[:, k, :], h_sb[:, k, :]) for k in range(NF)], [wDr, h_sb.r], [pd.r])
                        stt("dve", xs[:, j, :], xs[:, j, :], ALPHA, pd[:, :], ALU.mult, ALU.add, [pd.r, xs.r], [xs.r])
                        lns.tile_done(j)
                    lns.finish(ln2g, ln2b, write_xc=False)
                    dmac(out_ap[:, c0:c0 + SP].rearrange("(k p) t -> p k t", p=128), xs[:, :, :], reads=[xs.r], writes=([out_r] if out_r is not None else []))
                S.barrier()
        if plan is None:
            return {"seq": dry_seq, "last_use": dict(S.last_use)}
        S.finish("sp")
        S.emit(top)
    return nc


def _consts():
    ident = np.eye(128, dtype=np.float32)
    k = np.arange(128)[:, None]
    q = np.arange(128)[None, :]
    prev = (k >= q).astype(np.float32)
    cur = (k <= q).astype(np.float32)
    cm = np.concatenate([prev, cur], axis=1)
    cmask = np.tile(cm, (1, 4))
    esel = np.zeros((128, 16), np.float32)
    for hs in range(4):
        esel[:, hs * 4 + hs] = 1.0
    selb = np.zeros((4, 256), np.float32)
    for pr in range(2):
        for col in range(128):
            selb[2 * pr + col // 64, pr * 128 + col] = 1.0
    ones = np.ones((128, 128), np.float32)
    tril = cur.copy()
    half = 32
    inv_freq = (np.float32(10000.0) ** (-np.arange(half, dtype=np.float32) / np.float32(half))).astype(np.float32)
    invf = np.tile(inv_freq, 4).reshape(128, 1).astype(np.float32)
    return dict(ident=ident, cmask=cmask, esel=esel, selb=selb, ones=ones, tril=tril, invf=invf)


def _qk_perm():
    idx = []
    for qd in range(2):
        for ab in range(2):
            for hs in range(4):
                h = 4 * qd + hs
                idx.extend(range(h * 64 + ab * 32, h * 64 + ab * 32 + 32))
    return np.array(idx)


def _layer_weights(l, w_in, conv_w, gmlp_ln_g, gmlp_ln_b, w_s, b_s, p_a, p_b, p_c, w_o, ln1_g, ln1_b,
                   w_gate, w_up, w_down, ln2_g, ln2_b, suffix):
    perm = _qk_perm()
    wi = np.array(w_in[l], dtype=np.float32, copy=True)
    for g in range(3):
        base = ATT0 + g * 1536
        wi[:, base:base + 512] = w_in[l][:, base + perm]
        wi[:, base + 512:base + 1024] = w_in[l][:, base + 512 + perm]

    def pj(v):
        return np.ascontiguousarray(np.asarray(v, np.float32).reshape(NT, 128).T)
    cw = np.asarray(conv_w[l], np.float32)
    convw = np.ascontiguousarray(cw.reshape(3, NT, 128).transpose(2, 1, 0).reshape(128, NT * 3))
    d = {
        "w_in": wi,
        "convw": convw,
        "glng": np.asarray(gmlp_ln_g[l], np.float32).reshape(1, D),
        "glnb": np.asarray(gmlp_ln_b[l], np.float32).reshape(1, D),
        "wsT": np.ascontiguousarray(np.asarray(w_s[l], np.float32).transpose(0, 2, 1)),
        "bs": np.asarray(b_s[l], np.float32).reshape(1, D),
        "p_a": np.asarray(p_a[l], np.float32),
        "p_b": np.asarray(p_b[l], np.float32),
        "p_c": np.asarray(p_c[l], np.float32),
        "w_o": np.asarray(w_o[l], np.float32),
        "ln1g": pj(ln1_g[l]), "ln1b": pj(ln1_b[l]),
        "w_gate": np.asarray(w_gate[l], np.float32),
        "w_up": np.asarray(w_up[l], np.float32),
        "w_down": np.asarray(w_down[l], np.float32),
        "ln2g": pj(ln2_g[l]), "ln2b": pj(ln2_b[l]),
    }
    return {k + suffix: np.ascontiguousarray(v) for k, v in d.items()}


_NC_CACHE = {}


def kernel(x, positions, w_in, conv_w, gmlp_ln_g, gmlp_ln_b, w_s, b_s, p_a, p_b, p_c, w_o,
           ln1_g, ln1_b, w_gate, w_up, w_down, ln2_g, ln2_b):
    x = np.asarray(x, np.float32)
    positions = np.asarray(positions, np.int32)
    B, Sq, _ = x.shape
    consts = _consts()
    if 2 not in _NC_CACHE:
        plan = build_program(2)
        _NC_CACHE[2] = build_program(2, plan=plan)
    nc = _NC_CACHE[2]
    lw = {}
    for l in range(DEPTH):
        lw.update(_layer_weights(l, w_in, conv_w, gmlp_ln_g, gmlp_ln_b, w_s, b_s, p_a, p_b, p_c, w_o, ln1_g, ln1_b,
                                 w_gate, w_up, w_down, ln2_g, ln2_b, str(l)))
    zx = np.zeros((D, T), np.float32)
    zp = np.zeros((T,), np.int32)

    def xt(b, q):
        return np.ascontiguousarray(x[b, q * T:(q + 1) * T, :].T) if q >= 0 else zx

    def pp(b, q):
        return positions[b, q * T:(q + 1) * T] if q >= 0 else zp

    in_maps = []
    for c in range(8):
        b, qtr = c // 4, c % 4
        pos = np.concatenate([pp(b, qtr - 2), pp(b, qtr - 1), pp(b, qtr)]).reshape(1, 3 * T).astype(np.int32)
        m = {"xT": xt(b, qtr), "xhT": xt(b, qtr - 1), "xh2T": xt(b, qtr - 2), "pos": pos,
             "flag": np.full((128, 1), 1.0 if qtr >= 1 else 0.0, np.float32),
             "flag2": np.full((128, 1), 1.0 if qtr >= 2 else 0.0, np.float32)}
        m.update(consts)
        m.update(lw)
        in_maps.append(m)
    res = run_bass_kernel_spmd(nc, in_maps, core_ids=list(range(8)))
    out = np.empty((B, Sq, D), np.float32)
    for c in range(8):
        out[c // 4, (c % 4) * T:(c % 4 + 1) * T, :] = np.asarray(res.results[c]["out"]).T
    return out
```

```python
import math
from contextlib import ExitStack

import numpy as np
import concourse.bass as bass
import concourse.mybir as mybir
from concourse.bass_utils import run_bass_kernel_spmd

F32 = mybir.dt.float32
BF16 = mybir.dt.bfloat16
I32 = mybir.dt.int32
AF = mybir.ActivationFunctionType
ALU = mybir.AluOpType

D = 1024
NT = 8
T = 2048
SP = 512
NS = T // SP
DFF = 2816
NF = DFF // 128
DEPTH = 2
ALPHA = (2 * DEPTH) ** 0.25
EPS = 1e-5
N_IN = 12800
GA, GB, GC, CB, CC, CH = 0, 1024, 2048, 3072, 4096, 5120
ATT0 = 6144
UO, VO = 10752, 11776
DILS = (1, 4, 16)
MAGIC = 12582912.0
TWO_PI = 2.0 * math.pi
C1 = 6.28125
C2 = TWO_PI - C1


class Res:
    def __init__(self, name=""):
        self.name = name
        self.last_w = None
        self.reads = []


class Sched:
    ENG = ["pe", "act", "dve", "pool", "sp"]

    def __init__(self, nc, ndma=16):
        self.nc = nc
        self.ops = {e: [] for e in self.ENG}
        self.cnt = {e: 0 for e in self.ENG}
        self.known = {e: {} for e in self.ENG}
        self.ndma = ndma
        self.dma_cnt = [0] * ndma
        self.n_sp = 8
        self.rr_sp = 0
        self.rr_pool = 0
        self.opclock = 0
        self.on_op = None
        self.last_use = None

    def _need(self, eng, tok, waits):
        if tok is None:
            return
        kind, key, val = tok
        if kind == 'e' and key == eng and eng == 'pe':
            return
        k = (kind, key)
        if self.known[eng].get(k, 0) >= val:
            return
        self.known[eng][k] = val
        waits[k] = max(waits.get(k, 0), val)

    def _deps(self, eng, reads, writes, waits):
        for r in reads:
            self._need(eng, r.last_w, waits)
        for w in writes:
            self._need(eng, w.last_w, waits)
            for t in w.reads:
                self._need(eng, t, waits)

    def _commit(self, tok, reads, writes):
        for r in reads:
            r.reads.append(tok)
            if len(r.reads) > 64:
                r.reads = r.reads[-48:]
        for w in writes:
            w.last_w = tok
            w.reads = []

    def op(self, eng, fn, reads=(), writes=()):
        self.opclock += 1
        if self.last_use is not None:
            for r in reads:
                w = getattr(r, "widx", None)
                if w is not None:
                    self.last_use[w] = self.opclock
        if self.on_op is not None:
            self.on_op()
        waits = {}
        self._deps(eng, reads, writes, waits)
        self.cnt[eng] += 1
        tok = ('e', eng, self.cnt[eng])
        self.ops[eng].append((list(waits.items()), fn, ('e', eng, 1)))
        self._commit(tok, reads, writes)
        return tok

    def dma(self, fn, reads=(), writes=(), eng="sp"):
        if eng == "sp":
            ch = self.rr_sp
            self.rr_sp = (self.rr_sp + 1) % self.n_sp
        else:
            ch = self.n_sp + self.rr_pool
            self.rr_pool = (self.rr_pool + 1) % (self.ndma - self.n_sp)
        waits = {}
        if self.dma_cnt[ch] > 0:
            self._need(eng, ('d', ch, self.dma_cnt[ch] * 16), waits)
        self._deps(eng, reads, writes, waits)
        self.dma_cnt[ch] += 1
        tok = ('d', ch, self.dma_cnt[ch] * 16)
        self.ops[eng].append((list(waits.items()), fn, ('d', ch, 16)))
        self._commit(tok, reads, writes)
        return tok

    def barrier(self):
        toks = [('e', e2, self.cnt[e2]) for e2 in self.ENG if self.cnt[e2] > 0]
        toks += [('d', i, self.dma_cnt[i] * 16) for i in range(self.ndma) if self.dma_cnt[i] > 0]
        for eng in self.ENG:
            waits = {}
            for t in toks:
                if t[0] == 'e' and t[1] == eng:
                    continue
                self._need(eng, t, waits)
            self.ops[eng].append((list(waits.items()), None, None))

    def finish(self, eng="sp"):
        waits = {}
        for i in range(self.ndma):
            if self.dma_cnt[i] > 0:
                self._need(eng, ('d', i, self.dma_cnt[i] * 16), waits)
        self.ops[eng].append((list(waits.items()), None, None))

    def emit(self, stack):
        nc = self.nc
        esem = {e: stack.enter_context(nc.semaphore("s_" + e)) for e in self.ENG}
        dsem = [stack.enter_context(nc.semaphore("d_%d" % i)) for i in range(self.ndma)]

        def semof(k):
            return esem[k[1]] if k[0] == 'e' else dsem[k[1]]

        block = stack.enter_context(nc.Block())

        def run(engname):
            def body(e):
                for waits, fn, inc in self.ops[engname]:
                    for k, v in waits:
                        e.wait_ge(semof(k), v)
                    if fn is not None:
                        ins = fn(e)
                        ins.then_inc(semof(inc), inc[2])
            return body
        block.tensor(run("pe"))
        block.scalar(run("act"))
        block.vector(run("dve"))
        block.gpsimd(run("pool"))
        block.sync(run("sp"))


class Tile:
    def __init__(self, h, name):
        self.h = h
        self.r = Res(name)

    def __getitem__(self, k):
        return self.h[k]


class Ring:
    def __init__(self, tiles):
        self.tiles = tiles
        self.i = 0

    def next(self):
        t = self.tiles[self.i % len(self.tiles)]
        self.i += 1
        return t


def build_program(n_layers, kstop=99, plan=None):
    nc = bass.Bass("TRN2", target_bir_lowering=False)

    def din(name, shape, dt=F32):
        return nc.dram_tensor(name, list(shape), dt, kind="ExternalInput").ap()

    xT_d = din("xT", [D, T])
    xhT_d = din("xhT", [D, T])
    xh2T_d = din("xh2T", [D, T])
    pos_d = din("pos", [1, 3 * T], I32)
    flag_d = din("flag", [128, 1])
    flag2_d = din("flag2", [128, 1])
    ident_d = din("ident", [128, 128])
    cmask_d = din("cmask", [128, 1024])
    esel_d = din("esel", [128, 16])
    selb_d = din("selb", [4, 256])
    ones_d = din("ones", [128, 128])
    tril_d = din("tril", [128, 128])
    invf_d = din("invf", [128, 1])
    W = []
    for l in range(n_layers):
        W.append(dict(
            w_in=din("w_in%d" % l, [D, N_IN]),
            convw=din("convw%d" % l, [128, NT * 3]),
            glng=din("glng%d" % l, [1, D]),
            glnb=din("glnb%d" % l, [1, D]),
            wsT=din("wsT%d" % l, [8, 128, 128]),
            bs=din("bs%d" % l, [1, D]),
            p_a=din("p_a%d" % l, [D, D]),
            p_b=din("p_b%d" % l, [512, D]),
            p_c=din("p_c%d" % l, [D, D]),
            w_o=din("w_o%d" % l, [D, D]),
            ln1g=din("ln1g%d" % l, [128, NT]),
            ln1b=din("ln1b%d" % l, [128, NT]),
            w_gate=din("w_gate%d" % l, [D, DFF]),
            w_up=din("w_up%d" % l, [D, DFF]),
            w_down=din("w_down%d" % l, [DFF, D]),
            ln2g=din("ln2g%d" % l, [128, NT]),
            ln2b=din("ln2b%d" % l, [128, NT]),
        ))
    out_d = nc.dram_tensor("out", [D, T], F32, kind="ExternalOutput").ap()
    x1_d = nc.dram_tensor("x1_scratch", [D, T], F32).ap()
    x1h_d = nc.dram_tensor("x1h_scratch", [D, T], F32).ap()

    S = Sched(nc, ndma=24)

    with ExitStack() as top:
        def sb(st, name, shape, dt):
            return Tile(st.enter_context(nc.sbuf_tensor("sb_" + name, list(shape), dt)), name)

        def psum(st, name, shape, dt=F32):
            return Tile(st.enter_context(nc.psum_tensor("ps_" + name, list(shape), dt)), name)

        y_b = sb(top, "y_b", [128, 4, T], BF16)
        ident = sb(top, "ident", [128, 128], BF16)
        cmask = sb(top, "cmask", [128, 1024], BF16)
        esel = sb(top, "esel", [128, 16], BF16)
        eselF = sb(top, "eselF", [128, 16], BF16)
        selb = sb(top, "selb", [4, 256], F32)
        ones = sb(top, "ones", [128, 128], BF16)
        tril = sb(top, "tril", [128, 128], F32)
        invf = sb(top, "invf", [128, 1], F32)
        flag = sb(top, "flag", [128, 1], F32)
        flag2 = sb(top, "flag2", [128, 1], F32)
        eselF2 = sb(top, "eselF2", [128, 16], BF16)
        zhist = sb(top, "zhist", [128, NT, 2], F32)
        wring = Ring([sb(top, "wbuf%d" % i, [128, 4096], BF16) for i in range(6)])
        tmpr = Ring([sb(top, "tmp%d" % i, [128, 512], F32) for i in range(8)])

        def dmac(out, in_, writes, reads=()):
            return S.dma(lambda e, o=out, i=in_: e.dma_start(out=o, in_=i), reads=reads, writes=writes, eng="pool")

        def dmas(out, in_, writes=(), reads=()):
            return S.dma(lambda e, o=out, i=in_: e.dma_start(out=o, in_=i), reads=reads, writes=writes, eng="sp")

        NBUF = len(wring.tiles)
        PF = 4
        wst = {"idx": 0, "next": 0}
        WAP = {}
        for Wl_ in W:
            for ap_ in Wl_.values():
                WAP[ap_.tensor.name] = ap_
        if plan is None:
            S.last_use = {}
            dry_seq = []

        wcache = {}
        wcount = {}
        if plan is not None:
            for key_ in plan["seq"]:
                wcount[key_] = wcount.get(key_, 0) + 1

        def _issue(k):
            key = plan["seq"][k]
            (nm, c0, width, K) = key
            kt = K // 128
            t = wring.tiles[k % NBUF]
            flat = t.h[:, 0:kt * width]
            view = flat.rearrange("p (k c) -> p k c", k=kt)
            if wcount[key] < 3:
                dmac(view, WAP[nm][:, c0:c0 + width].rearrange("(k p) c -> p k c", p=128), writes=[t.r])
            elif key not in wcache:
                dmac(view, WAP[nm][:, c0:c0 + width].rearrange("(k p) c -> p k c", p=128), writes=[t.r])
                sc = nc.dram_tensor("wcache_%d" % len(wcache), [128, kt * width], BF16).ap()
                r = Res("wc")
                wcache[key] = (sc, r)
                dmas(sc, flat, writes=[r], reads=[t.r])
            else:
                sc, r = wcache[key]
                dmas(flat, sc, writes=[t.r], reads=[r])

        def _issue_upto(limit):
            limit = min(limit, len(plan["seq"]) - 1)
            while wst["next"] <= limit:
                k = wst["next"]
                if k >= NBUF and S.opclock <= plan["last_use"].get(k - NBUF, 0):
                    break
                _issue(k)
                wst["next"] += 1

        if plan is not None:
            S.on_op = lambda: _issue_upto(wst["idx"] - 1 + PF)

        def wload(w_ap, c0, width, K):
            kt = K // 128
            key = (w_ap.tensor.name, c0, width, K)
            idx = wst["idx"]
            wst["idx"] += 1
            if plan is None:
                dry_seq.append(key)
                t = wring.next()
                view = t.h[:, 0:kt * width].rearrange("p (k c) -> p k c", k=kt)
                r = Res("w%d" % idx)
                r.widx = idx
                return view, r
            assert plan["seq"][idx] == key, (idx, key, plan["seq"][idx])
            _issue_upto(idx + PF)
            assert wst["next"] > idx, "weight ring too small: block %d not issuable" % idx
            t = wring.tiles[idx % NBUF]
            view = t.h[:, 0:kt * width].rearrange("p (k c) -> p k c", k=kt)
            return view, t.r

        def mmgroup(out_ap, pairs, reads, writes, tp=None):
            n = len(pairs)

            def fn(e, out_ap=out_ap, pairs=pairs, tp=tp):
                ins = None
                for i, (l, r) in enumerate(pairs):
                    if tp is None:
                        ins = e.matmul(out_ap, lhsT=l, rhs=r, start=(i == 0), stop=(i == n - 1))
                    else:
                        ins = e.matmul(out_ap, lhsT=l, rhs=r, start=(i == 0), stop=(i == n - 1), tile_position=tp)
                return ins
            return S.op("pe", fn, reads=reads, writes=writes)

        def tt(eng, out, in0, in1, op, reads, writes):
            return S.op(eng, lambda e: e.tensor_tensor(out=out, in0=in0, in1=in1, op=op), reads=reads, writes=writes)

        def tsc(eng, out, in0, s1, s2, op0, op1, reads, writes):
            if op1 is None:
                return S.op(eng, lambda e: e.tensor_scalar(out=out, in0=in0, scalar1=s1, scalar2=None, op0=op0), reads=reads, writes=writes)
            return S.op(eng, lambda e: e.tensor_scalar(out=out, in0=in0, scalar1=s1, scalar2=s2, op0=op0, op1=op1), reads=reads, writes=writes)

        def stt(eng, out, in0, scalar, in1, op0, op1, reads, writes):
            return S.op(eng, lambda e: e.scalar_tensor_tensor(out=out, in0=in0, scalar=scalar, in1=in1, op0=op0, op1=op1), reads=reads, writes=writes)

        def cpy(eng, out, in_, reads, writes):
            return S.op(eng, lambda e: e.tensor_copy(out=out, in_=in_), reads=reads, writes=writes)

        def rcp(out, in_, reads, writes):
            return S.op("dve", lambda e: e.reciprocal(out=out, in_=in_), reads=reads, writes=writes)

        def act(out, in_, func, reads, writes, **kw):
            return S.op("act", lambda e: e.activation(out=out, in_=in_, func=func, **kw), reads=reads, writes=writes)

        dmac(ident[:], ident_d, [ident.r])
        dmac(cmask[:], cmask_d, [cmask.r])
        dmac(esel[:], esel_d, [esel.r])
        dmac(ones[:], ones_d, [ones.r])
        dmas(selb[:], selb_d, [selb.r])
        dmas(tril[:], tril_d, [tril.r])
        dmas(invf[:], invf_d, [invf.r])
        dmas(flag[:], flag_d, [flag.r])
        dmas(flag2[:], flag2_d, [flag2.r])
        S.op("dve", lambda e: e.tensor_scalar(out=eselF2[:], in0=esel[:], scalar1=flag2[:, 0:1], scalar2=None, op0=ALU.mult),
             reads=[esel.r, flag2.r], writes=[eselF2.r])
        S.op("dve", lambda e: e.tensor_scalar(out=eselF[:], in0=esel[:], scalar1=flag[:, 0:1], scalar2=None, op0=ALU.mult),
             reads=[esel.r, flag.r], writes=[eselF.r])

        def rope_tables(p0, n, cos_ap, sin_ap, cos_r, sin_r, posi):
            posf, ta, tb = tmpr.next(), tmpr.next(), tmpr.next()
            dmas(posi[:, 0:n], pos_d[0:1, p0:p0 + n].partition_broadcast(128), writes=[posi.r])
            S.op("dve", lambda e: e.tensor_copy(out=posf[:, 0:n], in_=posi[:, 0:n]), reads=[posi.r], writes=[posf.r])
            S.op("dve", lambda e: e.tensor_scalar(out=posf[:, 0:n], in0=posf[:, 0:n], scalar1=invf[:, 0:1], scalar2=None, op0=ALU.mult),
                 reads=[posf.r, invf.r], writes=[posf.r])
            S.op("dve", lambda e: e.tensor_scalar(out=ta[:, 0:n], in0=posf[:, 0:n], scalar1=1.0 / TWO_PI, scalar2=MAGIC, op0=ALU.mult, op1=ALU.add),
                 reads=[posf.r], writes=[ta.r])
            S.op("dve", lambda e: e.tensor_scalar(out=ta[:, 0:n], in0=ta[:, 0:n], scalar1=MAGIC, scalar2=None, op0=ALU.subtract),
                 reads=[ta.r], writes=[ta.r])
            S.op("dve", lambda e: e.scalar_tensor_tensor(out=posf[:, 0:n], in0=ta[:, 0:n], scalar=-C1, in1=posf[:, 0:n], op0=ALU.mult, op1=ALU.add),
                 reads=[ta.r, posf.r], writes=[posf.r])
            S.op("dve", lambda e: e.scalar_tensor_tensor(out=posf[:, 0:n], in0=ta[:, 0:n], scalar=-C2, in1=posf[:, 0:n], op0=ALU.mult, op1=ALU.add),
                 reads=[ta.r, posf.r], writes=[posf.r])
            S.op("dve", lambda e: e.tensor_scalar(out=posf[:, 0:n], in0=posf[:, 0:n], scalar1=-3.1415925, scalar2=3.1415925, op0=ALU.max, op1=ALU.min),
                 reads=[posf.r], writes=[posf.r])
            act(sin_ap, posf[:, 0:n], AF.Sin, [posf.r], [sin_r])
            S.op("dve", lambda e: e.scalar_tensor_tensor(out=tb[:, 0:n], in0=posf[:, 0:n], scalar=-1.0, in1=posf[:, 0:n], op0=ALU.mult, op1=ALU.max),
                 reads=[posf.r], writes=[tb.r])
            S.op("dve", lambda e: e.tensor_scalar(out=tb[:, 0:n], in0=tb[:, 0:n], scalar1=-1.0, scalar2=math.pi / 2, op0=ALU.mult, op1=ALU.add),
                 reads=[tb.r], writes=[tb.r])
            act(cos_ap, tb[:, 0:n], AF.Sin, [tb.r], [cos_r])

        xin_r = Res("xin")
        x1_r = Res("x1")
        x1h_r = Res("x1h")
        if n_layers == 1:
            passes = [(0, xT_d, xhT_d, T, flag, eselF, out_d, xin_r, xin_r, None)]
        else:
            passes = [
                (0, xhT_d, xh2T_d, 0, flag2, eselF2, x1h_d, xin_r, xin_r, x1h_r),
                (0, xT_d, xhT_d, T, flag, eselF, x1_d, xin_r, xin_r, x1_r),
                (1, x1_d, x1h_d, T, flag, eselF, out_d, x1_r, x1h_r, None),
            ]
        for pi, (l, x_src, xh_src, pos0, flag_t, eselF_t, out_ap, xsrc_r, xhsrc_r, out_r) in enumerate(passes):
            Wl = W[l]
            w_in = Wl["w_in"]
            with ExitStack() as pa:
                bring = Ring([psum(pa, "bankA%d_%d" % (i, pi), [128, 512]) for i in range(7)])
                bank_t = psum(pa, "bank_t_%d" % pi, [128, 1024], BF16)
                cos_o = sb(pa, "cos_o" + "_%d" % pi, [128, T], F32)
                sin_o = sb(pa, "sin_o" + "_%d" % pi, [128, T], F32)
                cos_h = sb(pa, "cos_h" + "_%d" % pi, [128, SP], F32)
                sin_h = sb(pa, "sin_h" + "_%d" % pi, [128, SP], F32)
                posi = sb(pa, "posi" + "_%d" % pi, [128, SP], I32)
                acc = sb(pa, "acc" + "_%d" % pi, [128, 2, T], F32)
                accd = sb(pa, "accd" + "_%d" % pi, [4, T], F32)
                KT = sb(pa, "KT" + "_%d" % pi, [128, 2, 2 * T], BF16)
                VT = sb(pa, "VT" + "_%d" % pi, [128, 2, 2 * T], BF16)
                QT = sb(pa, "QT" + "_%d" % pi, [128, 2, T], BF16)
                vbr = Ring([sb(pa, "vb%d_%d" % (i, pi), [128, 256], BF16) for i in range(6)])
                P_t = Ring([sb(pa, "P%d_%d" % (i, pi), [128, 1024], BF16) for i in range(3)])
                xcr = Ring([sb(pa, "xca%d_%d" % (i, pi), [128, NT, SP], BF16) for i in range(2)])
                for s in range(NS):
                    rope_tables(pos0 + T + s * SP, SP, cos_o[:, s * SP:(s + 1) * SP], sin_o[:, s * SP:(s + 1) * SP], cos_o.r, sin_o.r, posi)

                def chunk_list(dil_):
                    HL_ = 128 * dil_
                    ch_ = []
                    hs0_ = T - HL_
                    n_h_ = min(HL_, SP)
                    for a_ in range(hs0_, T, n_h_):
                        ch_.append((True, a_, n_h_, a_ - hs0_))
                    for s_ in range(NS):
                        ch_.append((False, s_ * SP, SP, HL_ + s_ * SP))
                    return ch_

                def issue_first(chunks_):
                    (is_h_, t0_, n_, c0_) = chunks_[0]
                    xc_ = xcr.next()
                    src_ = xh_src if is_h_ else x_src
                    dmac(xc_[:, :, 0:n_], src_[:, t0_:t0_ + n_].rearrange("(k p) t -> p k t", p=128), [xc_.r], reads=[xhsrc_r if is_h_ else xsrc_r])
                    return xc_
                phases = [(qd_, g_) for qd_ in range(2) for g_ in range(3)]
                pre_first = {0: issue_first(chunk_list(DILS[0]))}
                for qd in range(2):
                    if kstop <= 1:
                        break
                    S.op("pool", lambda e, acc=acc: e.memset(acc[:], 0.0), writes=[acc.r])
                    S.op("pool", lambda e, accd=accd: e.memset(accd[:], 0.0), writes=[accd.r])
                    for g, dil in enumerate(DILS):
                        HL = 128 * dil
                        base = ATT0 + g * 1536
                        wq_v, wq_r = wload(w_in, base + qd * 256, 256, D)
                        wk_v, wk_r = wload(w_in, base + 512 + qd * 256, 256, D)
                        wv_v, wv_r = wload(w_in, base + 1024 + qd * 256, 256, D)
                        chunks = []
                        hs0 = T - HL
                        n_h = min(HL, SP)
                        for a in range(hs0, T, n_h):
                            chunks.append((True, a, n_h, a - hs0))
                        for s in range(NS):
                            chunks.append((False, s * SP, SP, HL + s * SP))
                        loaded = {}

                        def issue_chunk(ci, chunks=chunks, loaded=loaded):
                            (is_h_, t0_, n_, c0_) = chunks[ci]
                            xc_ = xcr.next()
                            src_ = xh_src if is_h_ else x_src
                            dmac(xc_[:, :, 0:n_], src_[:, t0_:t0_ + n_].rearrange("(k p) t -> p k t", p=128), [xc_.r], reads=[xhsrc_r if is_h_ else xsrc_r])
                            loaded[ci] = xc_
                        ph_i = qd * 3 + g
                        loaded[0] = pre_first.pop(ph_i)
                        for ci, (is_h, t0, n, c0) in enumerate(chunks):
                            if ci + 1 < len(chunks):
                                issue_chunk(ci + 1)
                            xc = loaded[ci]
                            if is_h:
                                rope_tables(pos0 + t0, n, cos_h[:, 0:n], sin_h[:, 0:n], cos_h.r, sin_h.r, posi)
                                cs_ap, sn_ap, cs_r, sn_r = cos_h[:, 0:n], sin_h[:, 0:n], cos_h.r, sin_h.r
                            else:
                                cs_ap, sn_ap, cs_r, sn_r = cos_o[:, t0:t0 + n], sin_o[:, t0:t0 + n], cos_o.r, sin_o.r
                            todo = [(wk_v, wk_r, KT, c0)]
                            if not is_h:
                                todo.append((wq_v, wq_r, QT, t0))
                            for (wt, wr, dst, dc0) in todo:
                                pA = bring.next()
                                pB = bring.next()
                                mmgroup(pA[:, 0:n], [(wt[:, k, 0:128], xc[:, k, 0:n]) for k in range(NT)], [wr, xc.r], [pA.r])
                                mmgroup(pB[:, 0:n], [(wt[:, k, 128:256], xc[:, k, 0:n]) for k in range(NT)], [wr, xc.r], [pB.r])
                                t1, t2, t3, t4 = tmpr.next(), tmpr.next(), tmpr.next(), tmpr.next()
                                tt("dve", t1[:, 0:n], pA[:, 0:n], cs_ap, ALU.mult, [pA.r, cs_r], [t1.r])
                                tt("dve", t2[:, 0:n], pB[:, 0:n], sn_ap, ALU.mult, [pB.r, sn_r], [t2.r])
                                tt("pool", dst[:, 0, dc0:dc0 + n], t1[:, 0:n], t2[:, 0:n], ALU.subtract, [t1.r, t2.r], [dst.r])
                                tt("dve", t3[:, 0:n], pB[:, 0:n], cs_ap, ALU.mult, [pB.r, cs_r], [t3.r])
                                tt("dve", t4[:, 0:n], pA[:, 0:n], sn_ap, ALU.mult, [pA.r, sn_r], [t4.r])
                                tt("pool", dst[:, 1, dc0:dc0 + n], t3[:, 0:n], t4[:, 0:n], ALU.add, [t3.r, t4.r], [dst.r])
                            for vt in range(2):
                                pV = bring.next()
                                mmgroup(pV[:, 0:n], [(wv_v[:, k, vt * 128:(vt + 1) * 128], xc[:, k, 0:n]) for k in range(NT)], [wv_r, xc.r], [pV.r])
                                act(VT[:, vt, c0:c0 + n], pV[:, 0:n], AF.Copy, [pV.r], [VT.r])
                        if kstop <= 2:
                            break
                        if ph_i + 1 < len(phases):
                            pre_first[ph_i + 1] = issue_first(chunk_list(DILS[phases[ph_i + 1][1]]))
                        nb = T // (128 * dil)
                        Wd = HL + T

                        def kview(tl, lo, hi, ab, m, r, dil=dil, Wd=Wd):
                            return tl.h[lo:hi, ab, 0:Wd].rearrange("p (m i r) -> p m r i", i=128, r=dil)[:, m, r, :]

                        def qview(tl, lo, hi, ab, m, r, dil=dil):
                            return tl.h[lo:hi, ab, 0:T].rearrange("p (m i r) -> p m r i", i=128, r=dil)[:, m, r, :]

                        pending = [None]
                        for r in range(dil):
                            vprev = None
                            for m in range(nb + 1):
                                vb = vbr.next()

                                v0_ap = kview(VT, 0, 128, 0, m, r)
                                v1_ap = kview(VT, 0, 128, 1, m, r)

                                def tr2(e, v0_ap=v0_ap, v1_ap=v1_ap, o0=bank_t[:, 0:128], o1=bank_t[:, 128:256], idn=ident[:]):
                                    e.transpose(out=o0, in_=v0_ap, identity=idn)
                                    return e.transpose(out=o1, in_=v1_ap, identity=idn)
                                S.op("pe", tr2, reads=[VT.r, ident.r], writes=[bank_t.r])
                                if m == 0:
                                    act(vb[:], bank_t[:, 0:256], AF.Copy, [bank_t.r, flag_t.r], [vb.r], scale=flag_t[:, 0:1])
                                    vprev = vb
                                    continue
                                act(vb[:], bank_t[:, 0:256], AF.Copy, [bank_t.r], [vb.r])
                                n_q = m - 1
                                sbks = [bring.next() for _ in range(4)]
                                for hs in range(4):
                                    sbk = sbks[hs]
                                    lo, hi = 32 * hs, 32 * hs + 32
                                    for kt in range(2):
                                        o0 = kt * 128
                                        mmgroup(sbk[:, o0:o0 + 128],
                                                [(kview(KT, lo, hi, 0, n_q + kt, r), qview(QT, lo, hi, 0, n_q, r)),
                                                 (kview(KT, lo, hi, 1, n_q + kt, r), qview(QT, lo, hi, 1, n_q, r))],
                                                [KT.r, QT.r], [sbk.r], tp=(32 * hs, 0))
                                P = P_t.next()
                                for hs in range(4):
                                    act(P[:, hs * 256:(hs + 1) * 256], sbks[hs][:, 0:256], AF.Exp, [sbks[hs].r], [P.r], scale=0.125)
                                tt("pool", P[:], P[:], cmask[:], ALU.mult, [P.r, cmask.r], [P.r])
                                def pv_stage(vprev=vprev, vb=vb, P=P, m=m, n_q=n_q, r=r, dil=dil, acc=acc, accd=accd):
                                    nd = bring.next()
                                    vbs = (vprev, vb)
                                    for pr in range(2):
                                        for hh in range(2):
                                            hs = 2 * pr + hh
                                            mmgroup(nd[64 * hh:64 * hh + 64, pr * 128:(pr + 1) * 128],
                                                    [(vbs[kt][:, hs * 64:(hs + 1) * 64], P[:, hs * 256 + kt * 128: hs * 256 + kt * 128 + 128]) for kt in range(2)],
                                                    [vprev.r, vb.r, P.r], [nd.r], tp=(0, 64 * hh))
                                    es_prev = eselF_t if m == 1 else esel
                                    pairs = []
                                    for hs in range(4):
                                        pairs.append((es_prev[:, hs * 4:(hs + 1) * 4], P[:, hs * 256: hs * 256 + 128]))
                                        pairs.append((esel[:, hs * 4:(hs + 1) * 4], P[:, hs * 256 + 128: hs * 256 + 256]))
                                    mmgroup(nd[0:4, 256:384], pairs, [P.r, esel.r, eselF_t.r], [nd.r])
                                    accv = acc.h[:, :, :].rearrange("p a (m i r) -> p a m r i", i=128, r=dil)[:, :, n_q, r, :]
                                    tt("dve", accv, accv, nd[:, 0:256].rearrange("p (a i) -> p a i", a=2), ALU.add, [nd.r, acc.r], [acc.r])
                                    adv = accd.h[:, :].rearrange("p (m i r) -> p m r i", i=128, r=dil)[:, n_q, r, :]
                                    tt("dve", adv, adv, nd[0:4, 256:384], ALU.add, [nd.r, accd.r], [accd.r])
                                if pending[0] is not None:
                                    pending[0]()
                                pending[0] = pv_stage
                                vprev = vb
                        if pending[0] is not None:
                            pending[0]()
                            pending[0] = None
                    if kstop <= 3:
                        break
                    rcp(accd[:], accd[:], [accd.r], [accd.r])
                    if kstop <= 4:
                        break
                    for pr in range(2):
                        for s in range(NS):
                            bc = bring.next()
                            mmgroup(bc[:, :], [(selb[:, pr * 128:(pr + 1) * 128], accd[:, s * SP:(s + 1) * SP])], [selb.r, accd.r], [bc.r])
                            tt("dve", y_b[:, 2 * qd + pr, s * SP:(s + 1) * SP], acc[:, pr, s * SP:(s + 1) * SP], bc[:, :], ALU.mult,
                               [bc.r, acc.r], [y_b.r])

                S.barrier()
            if kstop <= 5:
                break
            with ExitStack() as pb:
                bring = Ring([psum(pb, "bankB%d_%d" % (i, pi), [128, 512]) for i in range(6)])
                stat_banks = [psum(pb, "bankS%d_%d" % (i, pi), [128, 512]) for i in range(2)]
                xs = sb(pb, "xs" + "_%d" % pi, [128, NT, SP], F32)
                xc = sb(pb, "xcb" + "_%d" % pi, [128, NT, SP], BF16)
                vfm = sb(pb, "vfm" + "_%d" % pi, [128, 4 * D], F32)
                u_sb = sb(pb, "u_sb" + "_%d" % pi, [128, NT, SP], BF16)
                y_a = sb(pb, "y_a" + "_%d" % pi, [128, NT, SP], BF16)
                y_c = sb(pb, "y_c" + "_%d" % pi, [128, NT, SP], BF16)
                vbf = [sb(pb, "vbf%d_%d" % (i, pi), [128, D], BF16) for i in range(4)]
                m_sb = u_sb
                h_sb = sb(pb, "h_sb" + "_%d" % pi, [128, NF, SP], BF16)
                zbuf = sb(pb, "zbuf" + "_%d" % pi, [128, SP + 2], F32)
                convw = sb(pb, "convw" + "_%d" % pi, [128, NT * 3], F32)
                glng = sb(pb, "glng" + "_%d" % pi, [128, D], F32)
                glnb = sb(pb, "glnb" + "_%d" % pi, [128, D], F32)
                bsb = sb(pb, "bsb" + "_%d" % pi, [128, D], F32)
                wsf = sb(pb, "wsf" + "_%d" % pi, [128, 8, 128], F32)
                wsm = sb(pb, "wsm" + "_%d" % pi, [128, 8, 128], BF16)
                ln1g = sb(pb, "ln1g" + "_%d" % pi, [128, NT], F32)
                ln1b = sb(pb, "ln1b" + "_%d" % pi, [128, NT], F32)
                ln2g = sb(pb, "ln2g" + "_%d" % pi, [128, NT], F32)
                ln2b = sb(pb, "ln2b" + "_%d" % pi, [128, NT], F32)
                st6 = sb(pb, "st6" + "_%d" % pi, [128, 2, 6], F32)
                mv = sb(pb, "mv" + "_%d" % pi, [128, 2], F32)
                rstd1 = sb(pb, "rstd1" + "_%d" % pi, [128, 1], F32)
                xh16 = sb(pb, "xh16" + "_%d" % pi, [128, NT, 16], BF16)
                mean_t = sb(pb, "mean_t" + "_%d" % pi, [128, SP], F32)
                rstd_t = sb(pb, "rstd_t" + "_%d" % pi, [128, SP], F32)

                dmas(convw[:], Wl["convw"], [convw.r])
                dmas(glng[:], Wl["glng"].partition_broadcast(128), [glng.r])
                dmas(glnb[:], Wl["glnb"].partition_broadcast(128), [glnb.r])
                dmas(bsb[:], Wl["bs"].partition_broadcast(128), [bsb.r])
                dmas(wsf[:], Wl["wsT"].rearrange("g j i -> j g i"), [wsf.r])
                dmas(ln1g[:], Wl["ln1g"], [ln1g.r])
                dmas(ln1b[:], Wl["ln1b"], [ln1b.r])
                dmas(ln2g[:], Wl["ln2g"], [ln2g.r])
                dmas(ln2b[:], Wl["ln2b"], [ln2b.r])
                tt("dve", wsm[:], wsf[:], tril[:].unsqueeze(1).to_broadcast([128, 8, 128]), ALU.mult, [wsf.r, tril.r], [wsm.r])
                dmac(xh16[:], xh_src[:, T - 16:T].rearrange("(k p) t -> p k t", p=128), [xh16.r], reads=[xhsrc_r])
                for jb in range(2):
                    wC, wCr = wload(w_in, CC + jb * 512, 512, D)
                    wH, wHr = wload(w_in, CH + jb * 512, 512, D)
                    for jj in range(4):
                        j = jb * 4 + jj
                        pC = bring.next()
                        pH = bring.next()
                        mmgroup(pC[:, 0:2], [(wC[:, k, jj * 128:(jj + 1) * 128], xh16[:, k, 14:16]) for k in range(NT)], [wCr, xh16.r], [pC.r])
                        mmgroup(pH[:, 0:2], [(wH[:, k, jj * 128:(jj + 1) * 128], xh16[:, k, 14:16]) for k in range(NT)], [wHr, xh16.r], [pH.r])
                        tz = tmpr.next()
                        act(tz[:, 0:2], pC[:, 0:2], AF.Copy, [pC.r, flag_t.r], [tz.r], scale=flag_t[:, 0:1])
                        tt("dve", zhist[:, j, :], tz[:, 0:2], pH[:, 0:2], ALU.mult, [tz.r, pH.r], [zhist.r])

                class LNStats:
                    def __init__(self):
                        self.rb, self.rsq = y_a, y_c
                        self.pm = stat_banks[0]
                        self.pq = stat_banks[1]
                        self.lag = None

                    def _mm(self, j):
                        rb, rsq, pm, pq = self.rb, self.rsq, self.pm, self.pq
                        S.op("pe", lambda e: e.matmul(pm[:, :], lhsT=ones[:], rhs=rb[:, j, :], start=(j == 0), stop=(j == NT - 1)),
                             reads=[ones.r, rb.r], writes=[pm.r])
                        S.op("pe", lambda e: e.matmul(pq[:, :], lhsT=ones[:], rhs=rsq[:, j, :], start=(j == 0), stop=(j == NT - 1)),
                             reads=[ones.r, rsq.r], writes=[pq.r])

                    def tile_done(self, j):
                        if self.lag is not None:
                            self._mm(self.lag)
                        act(self.rb[:, j, :], xs[:, j, :], AF.Identity, [xs.r], [self.rb.r])
                        tt("dve", self.rsq[:, j, :], xs[:, j, :], xs[:, j, :], ALU.mult, [xs.r], [self.rsq.r])
                        self.lag = j

                    def finish(self, g_t, b_t, write_xc=True):
                        self._mm(self.lag)
                        pm, pq = self.pm, self.pq
                        msq = tmpr.next()
                        tsc("dve", mean_t[:], pm[:, :], 1.0 / D, None, ALU.mult, None, [pm.r], [mean_t.r])
                        tt("dve", msq[:], mean_t[:], mean_t[:], ALU.mult, [mean_t.r], [msq.r])
                        stt("dve", msq[:], pq[:, :], 1.0 / D, msq[:], ALU.mult, ALU.subtract, [pq.r, msq.r], [msq.r])
                        tsc("dve", msq[:], msq[:], EPS, None, ALU.add, None, [msq.r], [msq.r])
                        act(msq[:], msq[:], AF.Sqrt, [msq.r], [msq.r])
                        rcp(rstd_t[:], msq[:], [msq.r], [rstd_t.r])
                        ts_ = []
                        for j in range(NT):
                            t = tmpr.next()
                            eng = "dve" if j % 2 == 0 else "pool"
                            tt(eng, t[:], xs[:, j, :], mean_t[:], ALU.subtract, [xs.r, mean_t.r], [t.r])
                            tt(eng, t[:], t[:], rstd_t[:], ALU.mult, [t.r, rstd_t.r], [t.r])
                            if write_xc:
                                act(xc[:, j, :], t[:], AF.Identity, [t.r, g_t.r, b_t.r], [xc.r], scale=g_t[:, j:j + 1], bias=b_t[:, j:j + 1])
                            ts_.append(t)
                        for j in range(NT):
                            t = ts_[j]
                            act(xs[:, j, :], t[:], AF.Identity, [t.r, g_t.r, b_t.r], [xs.r], scale=g_t[:, j:j + 1], bias=b_t[:, j:j + 1])

                mf = lambda j: vfm[:, j * SP:(j + 1) * SP]
                vf = lambda t_: vfm[:, t_ * D:(t_ + 1) * D]

                for s in range(NS):
                    c0 = s * SP
                    vfm3 = vfm.h[:, :].rearrange("p (k t) -> p k t", k=NT)
                    if s == 0:
                        dmac(xs[:, :, :], x_src[:, c0:c0 + SP].rearrange("(k p) t -> p k t", p=128), [xs.r], reads=[xsrc_r])
                        act(xc[:, :, :], xs[:, :, :], AF.Copy, [xs.r], [xc.r])
                    else:
                        cpy("pool", xs[:, :, :], vfm3, [vfm.r], [xs.r])
                    for jb in range(2):
                        wB, wBr = wload(w_in, CB + jb * 512, 512, D)
                        wC, wCr = wload(w_in, CC + jb * 512, 512, D)
                        wH, wHr = wload(w_in, CH + jb * 512, 512, D)
                        for jj in range(4):
                            j = jb * 4 + jj
                            pB_, pC, pH = bring.next(), bring.next(), bring.next()
                            mmgroup(pC[:, :], [(wC[:, k, jj * 128:(jj + 1) * 128], xc[:, k, :]) for k in range(NT)], [wCr, xc.r], [pC.r])
                            mmgroup(pH[:, :], [(wH[:, k, jj * 128:(jj + 1) * 128], xc[:, k, :]) for k in range(NT)], [wHr, xc.r], [pH.r])
                            mmgroup(pB_[:, :], [(wB[:, k, jj * 128:(jj + 1) * 128], xc[:, k, :]) for k in range(NT)], [wBr, xc.r], [pB_.r])
                            tc_ = tmpr.next()
                            ta_ = tmpr.next()
                            act(tc_[:], pC[:, :], AF.Copy, [pC.r], [tc_.r])
                            act(zbuf[:, 0:2], zhist[:, j, :], AF.Copy, [zhist.r], [zbuf.r])
                            tt("dve", zbuf[:, 2:SP + 2], tc_[:], pH[:, :], ALU.mult, [tc_.r, pH.r], [zbuf.r])
                            act(zhist[:, j, :], zbuf[:, SP:SP + 2], AF.Copy, [zbuf.r], [zhist.r])
                            tsc("dve", ta_[:], zbuf[:, 0:SP], convw[:, 3 * j:3 * j + 1], None, ALU.mult, None, [zbuf.r, convw.r], [ta_.r])
                            stt("dve", ta_[:], zbuf[:, 1:SP + 1], convw[:, 3 * j + 1:3 * j + 2], ta_[:], ALU.mult, ALU.add, [zbuf.r, convw.r, ta_.r], [ta_.r])
                            stt("dve", ta_[:], zbuf[:, 2:SP + 2], convw[:, 3 * j + 2:3 * j + 3], ta_[:], ALU.mult, ALU.add, [zbuf.r, convw.r, ta_.r], [ta_.r])
                            tt("dve", y_a[:, j, :], ta_[:], pB_[:, :], ALU.mult, [ta_.r, pB_.r], [y_a.r])
                    for jb in range(2):
                        wU, wUr = wload(w_in, UO + jb * 512, 512, D)
                        for jj in range(4):
                            j = jb * 4 + jj
                            pU = bring.next()
                            mmgroup(pU[:, :], [(wU[:, k, jj * 128:(jj + 1) * 128], xc[:, k, :]) for k in range(NT)], [wUr, xc.r], [pU.r])
                            act(u_sb[:, j, :], pU[:, :], AF.Gelu, [pU.r], [u_sb.r])
                    for half in range(2):
                        wV, wVr = wload(w_in, VO + half * 512, 512, D)
                        for t_ in range(4):
                            pV = bring.next()
                            mmgroup(pV[:, :], [(xc[:, k, t_ * 128:(t_ + 1) * 128], wV[:, k, :]) for k in range(NT)], [wVr, xc.r], [pV.r])
                            act(vfm[:, t_ * D + half * 512: t_ * D + (half + 1) * 512], pV[:, :], AF.Gelu, [pV.r], [vfm.r])
                    for t_ in range(4):
                        v = vf(t_)
                        S.op("dve", lambda e, o_=st6[:, 0, :], i_=v[:, 0:512]: e.bn_stats(out=o_, in_=i_), reads=[vfm.r], writes=[st6.r])
                        S.op("dve", lambda e, o_=st6[:, 1, :], i_=v[:, 512:1024]: e.bn_stats(out=o_, in_=i_), reads=[vfm.r, st6.r], writes=[st6.r])
                        S.op("dve", lambda e, o_=mv[:], i_=st6[:].rearrange("p a b -> p (a b)"): e.bn_aggr(out=o_, in_=i_), reads=[st6.r], writes=[mv.r])
                        tsc("dve", rstd1[:], mv[:, 1:2], EPS, None, ALU.add, None, [mv.r], [rstd1.r])
                        act(rstd1[:], rstd1[:], AF.Sqrt, [rstd1.r], [rstd1.r])
                        rcp(rstd1[:], rstd1[:], [rstd1.r], [rstd1.r])
                        tsc("dve", v, v, mv[:, 0:1], rstd1[:, 0:1], ALU.subtract, ALU.mult, [vfm.r, mv.r, rstd1.r], [vfm.r])
                        tt("dve", v, v, glng[:], ALU.mult, [vfm.r, glng.r], [vfm.r])
                        tt("dve", vbf[t_][:], v, glnb[:], ALU.add, [vfm.r, glnb.r], [vbf[t_].r])
                    def spatial_stage():
                        for gg in range(8):
                            pS = bring.next()
                            for t_ in range(4):
                                mmgroup(pS[:, t_ * 128:(t_ + 1) * 128], [(vbf[t_][:, gg * 128:(gg + 1) * 128], wsm[:, gg, :])], [vbf[t_].r, wsm.r], [pS.r])
                            tq = tmpr.next()
                            tt("dve", tq[:].rearrange("p (c i) -> p c i", c=4), pS[:, :].rearrange("p (c i) -> p c i", c=4),
                               bsb[:, gg * 128:(gg + 1) * 128].unsqueeze(1).to_broadcast([128, 4, 128]), ALU.add, [pS.r, bsb.r], [tq.r])
                            tt("dve", y_c[:, gg, :], tq[:], u_sb[:, gg, :], ALU.mult, [tq.r, u_sb.r], [y_c.r])
                    for bi, (gcol, p_ap, K, y_t) in enumerate(((GA, Wl["p_a"], D, y_a), (GB, Wl["p_b"], 512, y_b), (GC, Wl["p_c"], D, y_c))):
                        kt = K // 128
                        if bi == 2:
                            spatial_stage()
                        for jb in range(2):
                            wG_, wGr_ = wload(w_in, gcol + jb * 512, 512, D)
                            wP_, wPr_ = wload(p_ap, jb * 512, 512, K)
                            for jj in range(4):
                                j = jb * 4 + jj
                                pg, py = bring.next(), bring.next()
                                mmgroup(pg[:, :], [(wG_[:, k, jj * 128:(jj + 1) * 128], xc[:, k, :]) for k in range(NT)], [wGr_, xc.r], [pg.r])
                                if bi == 1:
                                    prs = [(wP_[:, k, jj * 128:(jj + 1) * 128], y_b[:, k, c0:c0 + SP]) for k in range(kt)]
                                else:
                                    prs = [(wP_[:, k, jj * 128:(jj + 1) * 128], y_t[:, k, :]) for k in range(kt)]
                                mmgroup(py[:, :], prs, [wPr_, y_t.r], [py.r])
                                sg = tmpr.next()
                                act(sg[:], pg[:, :], AF.Sigmoid, [pg.r], [sg.r])
                                if bi == 0:
                                    tt("dve", mf(j), sg[:], py[:, :], ALU.mult, [sg.r, py.r], [vfm.r])
                                elif bi == 1:
                                    tt("dve", sg[:], sg[:], py[:, :], ALU.mult, [sg.r, py.r], [sg.r])
                                    tt("dve", mf(j), mf(j), sg[:], ALU.add, [sg.r, vfm.r], [vfm.r])
                                else:
                                    tt("dve", sg[:], sg[:], py[:, :], ALU.mult, [sg.r, py.r], [sg.r])
                                    tt("dve", m_sb[:, j, :], mf(j), sg[:], ALU.add, [sg.r, vfm.r], [m_sb.r])
                    if s + 1 < NS:
                        dmac(vfm3, x_src[:, c0 + SP:c0 + 2 * SP].rearrange("(k p) t -> p k t", p=128), [vfm.r], reads=[xsrc_r])
                    lns = LNStats()
                    for jb in range(2):
                        wO, wOr = wload(Wl["w_o"], jb * 512, 512, D)
                        for jj in range(4):
                            j = jb * 4 + jj
                            po = bring.next()
                            mmgroup(po[:, :], [(wO[:, k, jj * 128:(jj + 1) * 128], m_sb[:, k, :]) for k in range(NT)], [wOr, m_sb.r], [po.r])
                            stt("dve", xs[:, j, :], xs[:, j, :], ALPHA, po[:, :], ALU.mult, ALU.add, [po.r, xs.r], [xs.r])
                            lns.tile_done(j)
                    lns.finish(ln1g, ln1b)
                    for fb in range(NF // 2):
                        wG_, wGr_ = wload(Wl["w_gate"], fb * 256, 256, D)
                        wU_, wUr_ = wload(Wl["w_up"], fb * 256, 256, D)
                        for ff in range(2):
                            f = fb * 2 + ff
                            pg, pu = bring.next(), bring.next()
                            mmgroup(pg[:, :], [(wG_[:, k, ff * 128:(ff + 1) * 128], xc[:, k, :]) for k in range(NT)], [wGr_, xc.r], [pg.r])
                            mmgroup(pu[:, :], [(wU_[:, k, ff * 128:(ff + 1) * 128], xc[:, k, :]) for k in range(NT)], [wUr_, xc.r], [pu.r])
                            sg = tmpr.next()
                            act(sg[:], pg[:, :], AF.Silu, [pg.r], [sg.r])
                            tt("dve", h_sb[:, f, :], sg[:], pu[:, :], ALU.mult, [sg.r, pu.r], [h_sb.r])
                    if s + 1 < NS:
                        act(xc[:, :, :], vfm3, AF.Identity, [vfm.r], [xc.r])
                    lns = LNStats()
                    for j in range(NT):
                        wD, wDr = wload(Wl["w_down"], j * 128, 128, DFF)
                        pd = bring.next()
                        mmgroup(pd[:, :], [(wD[:, k, :], h_sb[:, k, :]) for k in range(NF)], [wDr, h_sb.r], [pd.r])
                        stt("dve", xs[:, j, :], xs[:, j, :], ALPHA, pd[:, :], ALU.mult, ALU.add, [pd.r, xs.r], [xs.r])
                        lns.tile_done(j)
                    lns.finish(ln2g, ln2b, write_xc=False)
                    dmac(out_ap[:, c0:c0 + SP].rearrange("(k p) t -> p k t", p=128), xs[:, :, :], reads=[xs.r], writes=([out_r] if out_r is not None else []))
                S.barrier()
        if plan is None:
            return {"seq": dry_seq, "last_use": dict(S.last_use)}
        S.finish("sp")
        S.emit(top)
    return nc


def _consts():
    ident = np.eye(128, dtype=np.float32)
    k = np.arange(128)[:, None]
    q = np.arange(128)[None, :]
    prev = (k >= q).astype(np.float32)
    cur = (k <= q).astype(np.float32)
    cm = np.concatenate([prev, cur], axis=1)
    cmask = np.tile(cm, (1, 4))
    esel = np.zeros((128, 16), np.float32)
    for hs in range(4):
        esel[:, hs * 4 + hs] = 1.0
    selb = np.zeros((4, 256), np.float32)
    for pr in range(2):
        for col in range(128):
            selb[2 * pr + col // 64, pr * 128 + col] = 1.0
    ones = np.ones((128, 128), np.float32)
    tril = cur.copy()
    half = 32
    inv_freq = (np.float32(10000.0) ** (-np.arange(half, dtype=np.float32) / np.float32(half))).astype(np.float32)
    invf = np.tile(inv_freq, 4).reshape(128, 1).astype(np.float32)
    return dict(ident=ident, cmask=cmask, esel=esel, selb=selb, ones=ones, tril=tril, invf=invf)


def _qk_perm():
    idx = []
    for qd in range(2):
        for ab in range(2):
            for hs in range(4):
                h = 4 * qd + hs
                idx.extend(range(h * 64 + ab * 32, h * 64 + ab * 32 + 32))
    return np.array(idx)


def _layer_weights(l, w_in, conv_w, gmlp_ln_g, gmlp_ln_b, w_s, b_s, p_a, p_b, p_c, w_o, ln1_g, ln1_b,
                   w_gate, w_up, w_down, ln2_g, ln2_b, suffix):
    perm = _qk_perm()
    wi = np.array(w_in[l], dtype=np.float32, copy=True)
    for g in range(3):
        base = ATT0 + g * 1536
        wi[:, base:base + 512] = w_in[l][:, base + perm]
        wi[:, base + 512:base + 1024] = w_in[l][:, base + 512 + perm]

    def pj(v):
        return np.ascontiguousarray(np.asarray(v, np.float32).reshape(NT, 128).T)
    cw = np.asarray(conv_w[l], np.float32)
    convw = np.ascontiguousarray(cw.reshape(3, NT, 128).transpose(2, 1, 0).reshape(128, NT * 3))
    d = {
        "w_in": wi,
        "convw": convw,
        "glng": np.asarray(gmlp_ln_g[l], np.float32).reshape(1, D),
        "glnb": np.asarray(gmlp_ln_b[l], np.float32).reshape(1, D),
        "wsT": np.ascontiguousarray(np.asarray(w_s[l], np.float32).transpose(0, 2, 1)),
        "bs": np.asarray(b_s[l], np.float32).reshape(1, D),
        "p_a": np.asarray(p_a[l], np.float32),
        "p_b": np.asarray(p_b[l], np.float32),
        "p_c": np.asarray(p_c[l], np.float32),
        "w_o": np.asarray(w_o[l], np.float32),
        "ln1g": pj(ln1_g[l]), "ln1b": pj(ln1_b[l]),
        "w_gate": np.asarray(w_gate[l], np.float32),
        "w_up": np.asarray(w_up[l], np.float32),
        "w_down": np.asarray(w_down[l], np.float32),
        "ln2g": pj(ln2_g[l]), "ln2b": pj(ln2_b[l]),
    }
    return {k + suffix: np.ascontiguousarray(v) for k, v in d.items()}


_NC_CACHE = {}


def kernel(x, positions, w_in, conv_w, gmlp_ln_g, gmlp_ln_b, w_s, b_s, p_a, p_b, p_c, w_o,
           ln1_g, ln1_b, w_gate, w_up, w_down, ln2_g, ln2_b):
    x = np.asarray(x, np.float32)
    positions = np.asarray(positions, np.int32)
    B, Sq, _ = x.shape
    consts = _consts()
    if 2 not in _NC_CACHE:
        plan = build_program(2)
        _NC_CACHE[2] = build_program(2, plan=plan)
    nc = _NC_CACHE[2]
    lw = {}
    for l in range(DEPTH):
        lw.update(_layer_weights(l, w_in, conv_w, gmlp_ln_g, gmlp_ln_b, w_s, b_s, p_a, p_b, p_c, w_o, ln1_g, ln1_b,
                                 w_gate, w_up, w_down, ln2_g, ln2_b, str(l)))
    zx = np.zeros((D, T), np.float32)
    zp = np.zeros((T,), np.int32)

    def xt(b, q):
        return np.ascontiguousarray(x[b, q * T:(q + 1) * T, :].T) if q >= 0 else zx

    def pp(b, q):
        return positions[b, q * T:(q + 1) * T] if q >= 0 else zp

    in_maps = []
    for c in range(8):
        b, qtr = c // 4, c % 4
        pos = np.concatenate([pp(b, qtr - 2), pp(b, qtr - 1), pp(b, qtr)]).reshape(1, 3 * T).astype(np.int32)
        m = {"xT": xt(b, qtr), "xhT": xt(b, qtr - 1), "xh2T": xt(b, qtr - 2), "pos": pos,
             "flag": np.full((128, 1), 1.0 if qtr >= 1 else 0.0, np.float32),
             "flag2": np.full((128, 1), 1.0 if qtr >= 2 else 0.0, np.float32)}
        m.update(consts)
        m.update(lw)
        in_maps.append(m)
    res = run_bass_kernel_spmd(nc, in_maps, core_ids=list(range(8)))
    out = np.empty((B, Sq, D), np.float32)
    for c in range(8):
        out[c // 4, (c % 4) * T:(c % 4 + 1) * T, :] = np.asarray(res.results[c]["out"]).T
    return out
```

```python
import math
from contextlib import ExitStack

import numpy as np
import concourse.bass as bass
import concourse.mybir as mybir
from concourse.bass_utils import run_bass_kernel_spmd

F32 = mybir.dt.float32
BF16 = mybir.dt.bfloat16
I32 = mybir.dt.int32
AF = mybir.ActivationFunctionType
ALU = mybir.AluOpType

D = 1024
NT = 8
T = 2048
SP = 512
NS = T // SP
DFF = 2816
NF = DFF // 128
DEPTH = 2
ALPHA = (2 * DEPTH) ** 0.25
EPS = 1e-5
N_IN = 12800
GA, GB, GC, CB, CC, CH = 0, 1024, 2048, 3072, 4096, 5120
ATT0 = 6144
UO, VO = 10752, 11776
DILS = (1, 4, 16)
MAGIC = 12582912.0
TWO_PI = 2.0 * math.pi
C1 = 6.28125
C2 = TWO_PI - C1


class Res:
    def __init__(self, name=""):
        self.name = name
        self.last_w = None
        self.reads = []


class Sched:
    ENG = ["pe", "act", "dve", "pool", "sp"]

    def __init__(self, nc, ndma=16):
        self.nc = nc
        self.ops = {e: [] for e in self.ENG}
        self.cnt = {e: 0 for e in self.ENG}
        self.known = {e: {} for e in self.ENG}
        self.ndma = ndma
        self.dma_cnt = [0] * ndma
        self.n_sp = 8
        self.rr_sp = 0
        self.rr_pool = 0
        self.opclock = 0
        self.on_op = None
        self.last_use = None

    def _need(self, eng, tok, waits):
        if tok is None:
            return
        kind, key, val = tok
        if kind == 'e' and key == eng and eng == 'pe':
            return
        k = (kind, key)
        if self.known[eng].get(k, 0) >= val:
            return
        self.known[eng][k] = val
        waits[k] = max(waits.get(k, 0), val)

    def _deps(self, eng, reads, writes, waits):
        for r in reads:
            self._need(eng, r.last_w, waits)
        for w in writes:
            self._need(eng, w.last_w, waits)
            for t in w.reads:
                self._need(eng, t, waits)

    def _commit(self, tok, reads, writes):
        for r in reads:
            r.reads.append(tok)
            if len(r.reads) > 64:
                r.reads = r.reads[-48:]
        for w in writes:
            w.last_w = tok
            w.reads = []

    def op(self, eng, fn, reads=(), writes=()):
        self.opclock += 1
        if self.last_use is not None:
            for r in reads:
                w = getattr(r, "widx", None)
                if w is not None:
                    self.last_use[w] = self.opclock
        if self.on_op is not None:
            self.on_op()
        waits = {}
        self._deps(eng, reads, writes, waits)
        self.cnt[eng] += 1
        tok = ('e', eng, self.cnt[eng])
        self.ops[eng].append((list(waits.items()), fn, ('e', eng, 1)))
        self._commit(tok, reads, writes)
        return tok

    def dma(self, fn, reads=(), writes=(), eng="sp"):
        if eng == "sp":
            ch = self.rr_sp
            self.rr_sp = (self.rr_sp + 1) % self.n_sp
        else:
            ch = self.n_sp + self.rr_pool
            self.rr_pool = (self.rr_pool + 1) % (self.ndma - self.n_sp)
        waits = {}
        if self.dma_cnt[ch] > 0:
            self._need(eng, ('d', ch, self.dma_cnt[ch] * 16), waits)
        self._deps(eng, reads, writes, waits)
        self.dma_cnt[ch] += 1
        tok = ('d', ch, self.dma_cnt[ch] * 16)
        self.ops[eng].append((list(waits.items()), fn, ('d', ch, 16)))
        self._commit(tok, reads, writes)
        return tok

    def barrier(self):
        toks = [('e', e2, self.cnt[e2]) for e2 in self.ENG if self.cnt[e2] > 0]
        toks += [('d', i, self.dma_cnt[i] * 16) for i in range(self.ndma) if self.dma_cnt[i] > 0]
        for eng in self.ENG:
            waits = {}
            for t in toks:
                if t[0] == 'e' and t[1] == eng:
                    continue
                self._need(eng, t, waits)
            self.ops[eng].append((list(waits.items()), None, None))

    def finish(self, eng="sp"):
        waits = {}
        for i in range(self.ndma):
            if self.dma_cnt[i] > 0:
                self._need(eng, ('d', i, self.dma_cnt[i] * 16), waits)
        self.ops[eng].append((list(waits.items()), None, None))

    def emit(self, stack):
        nc = self.nc
        esem = {e: stack.enter_context(nc.semaphore("s_" + e)) for e in self.ENG}
        dsem = [stack.enter_context(nc.semaphore("d_%d" % i)) for i in range(self.ndma)]

        def semof(k):
            return esem[k[1]] if k[0] == 'e' else dsem[k[1]]

        block = stack.enter_context(nc.Block())

        def run(engname):
            def body(e):
                for waits, fn, inc in self.ops[engname]:
                    for k, v in waits:
                        e.wait_ge(semof(k), v)
                    if fn is not None:
                        ins = fn(e)
                        ins.then_inc(semof(inc), inc[2])
            return body
        block.tensor(run("pe"))
        block.scalar(run("act"))
        block.vector(run("dve"))
        block.gpsimd(run("pool"))
        block.sync(run("sp"))


class Tile:
    def __init__(self, h, name):
        self.h = h
        self.r = Res(name)

    def __getitem__(self, k):
        return self.h[k]


class Ring:
    def __init__(self, tiles):
        self.tiles = tiles
        self.i = 0

    def next(self):
        t = self.tiles[self.i % len(self.tiles)]
        self.i += 1
        return t


def build_program(n_layers, kstop=99, plan=None):
    nc = bass.Bass("TRN2", target_bir_lowering=False)

    def din(name, shape, dt=F32):
        return nc.dram_tensor(name, list(shape), dt, kind="ExternalInput").ap()

    xT_d = din("xT", [D, T])
    xhT_d = din("xhT", [D, T])
    xh2T_d = din("xh2T", [D, T])
    pos_d = din("pos", [1, 3 * T], I32)
    flag_d = din("flag", [128, 1])
    flag2_d = din("flag2", [128, 1])
    ident_d = din("ident", [128, 128])
    cmask_d = din("cmask", [128, 1024])
    esel_d = din("esel", [128, 16])
    selb_d = din("selb", [4, 256])
    ones_d = din("ones", [128, 128])
    tril_d = din("tril", [128, 128])
    invf_d = din("invf", [128, 1])
    W = []
    for l in range(n_layers):
        W.append(dict(
            w_in=din("w_in%d" % l, [D, N_IN]),
            convw=din("convw%d" % l, [128, NT * 3]),
            glng=din("glng%d" % l, [1, D]),
            glnb=din("glnb%d" % l, [1, D]),
            wsT=din("wsT%d" % l, [8, 128, 128]),
            bs=din("bs%d" % l, [1, D]),
            p_a=din("p_a%d" % l, [D, D]),
            p_b=din("p_b%d" % l, [512, D]),
            p_c=din("p_c%d" % l, [D, D]),
            w_o=din("w_o%d" % l, [D, D]),
            ln1g=din("ln1g%d" % l, [128, NT]),
            ln1b=din("ln1b%d" % l, [128, NT]),
            w_gate=din("w_gate%d" % l, [D, DFF]),
            w_up=din("w_up%d" % l, [D, DFF]),
            w_down=din("w_down%d" % l, [DFF, D]),
            ln2g=din("ln2g%d" % l, [128, NT]),
            ln2b=din("ln2b%d" % l, [128, NT]),
        ))
    out_d = nc.dram_tensor("out", [D, T], F32, kind="ExternalOutput").ap()
    x1_d = nc.dram_tensor("x1_scratch", [D, T], F32).ap()
    x1h_d = nc.dram_tensor("x1h_scratch", [D, T], F32).ap()

    S = Sched(nc, ndma=24)

    with ExitStack() as top:
        def sb(st, name, shape, dt):
            return Tile(st.enter_context(nc.sbuf_tensor("sb_" + name, list(shape), dt)), name)

        def psum(st, name, shape, dt=F32):
            return Tile(st.enter_context(nc.psum_tensor("ps_" + name, list(shape), dt)), name)

        y_b = sb(top, "y_b", [128, 4, T], BF16)
        ident = sb(top, "ident", [128, 128], BF16)
        cmask = sb(top, "cmask", [128, 1024], BF16)
        esel = sb(top, "esel", [128, 16], BF16)
        eselF = sb(top, "eselF", [128, 16], BF16)
        selb = sb(top, "selb", [4, 256], F32)
        ones = sb(top, "ones", [128, 128], BF16)
        tril = sb(top, "tril", [128, 128], F32)
        invf = sb(top, "invf", [128, 1], F32)
        flag = sb(top, "flag", [128, 1], F32)
        flag2 = sb(top, "flag2", [128, 1], F32)
        eselF2 = sb(top, "eselF2", [128, 16], BF16)
        zhist = sb(top, "zhist", [128, NT, 2], F32)
        wring = Ring([sb(top, "wbuf%d" % i, [128, 4096], BF16) for i in range(6)])
        tmpr = Ring([sb(top, "tmp%d" % i, [128, 512], F32) for i in range(8)])

        def dmac(out, in_, writes, reads=()):
            return S.dma(lambda e, o=out, i=in_: e.dma_start(out=o, in_=i), reads=reads, writes=writes, eng="pool")

        def dmas(out, in_, writes=(), reads=()):
            return S.dma(lambda e, o=out, i=in_: e.dma_start(out=o, in_=i), reads=reads, writes=writes, eng="sp")

        NBUF = len(wring.tiles)
        PF = 4
        wst = {"idx": 0, "next": 0}
        WAP = {}
        for Wl_ in W:
            for ap_ in Wl_.values():
                WAP[ap_.tensor.name] = ap_
        if plan is None:
            S.last_use = {}
            dry_seq = []

        wcache = {}
        wcount = {}
        if plan is not None:
            for key_ in plan["seq"]:
                wcount[key_] = wcount.get(key_, 0) + 1

        def _issue(k):
            key = plan["seq"][k]
            (nm, c0, width, K) = key
            kt = K // 128
            t = wring.tiles[k % NBUF]
            flat = t.h[:, 0:kt * width]
            view = flat.rearrange("p (k c) -> p k c", k=kt)
            if wcount[key] < 3:
                dmac(view, WAP[nm][:, c0:c0 + width].rearrange("(k p) c -> p k c", p=128), writes=[t.r])
            elif key not in wcache:
                dmac(view, WAP[nm][:, c0:c0 + width].rearrange("(k p) c -> p k c", p=128), writes=[t.r])
                sc = nc.dram_tensor("wcache_%d" % len(wcache), [128, kt * width], BF16).ap()
                r = Res("wc")
                wcache[key] = (sc, r)
                dmas(sc, flat, writes=[r], reads=[t.r])
            else:
                sc, r = wcache[key]
                dmas(flat, sc, writes=[t.r], reads=[r])

        def _issue_upto(limit):
            limit = min(limit, len(plan["seq"]) - 1)
            while wst["next"] <= limit:
                k = wst["next"]
                if k >= NBUF and S.opclock <= plan["last_use"].get(k - NBUF, 0):
                    break
                _issue(k)
                wst["next"] += 1

        if plan is not None:
            S.on_op = lambda: _issue_upto(wst["idx"] - 1 + PF)

        def wload(w_ap, c0, width, K):
            kt = K // 128
            key = (w_ap.tensor.name, c0, width, K)
            idx = wst["idx"]
            wst["idx"] += 1
            if plan is None:
                dry_seq.append(key)
                t = wring.next()
                view = t.h[:, 0:kt * width].rearrange("p (k c) -> p k c", k=kt)
                r = Res("w%d" % idx)
                r.widx = idx
                return view, r
            assert plan["seq"][idx] == key, (idx, key, plan["seq"][idx])
            _issue_upto(idx + PF)
            assert wst["next"] > idx, "weight ring too small: block %d not issuable" % idx
            t = wring.tiles[idx % NBUF]
            view = t.h[:, 0:kt * width].rearrange("p (k c) -> p k c", k=kt)
            return view, t.r

        def mmgroup(out_ap, pairs, reads, writes, tp=None):
            n = len(pairs)

            def fn(e, out_ap=out_ap, pairs=pairs, tp=tp):
                ins = None
                for i, (l, r) in enumerate(pairs):
                    if tp is None:
                        ins = e.matmul(out_ap, lhsT=l, rhs=r, start=(i == 0), stop=(i == n - 1))
                    else:
                        ins = e.matmul(out_ap, lhsT=l, rhs=r, start=(i == 0), stop=(i == n - 1), tile_position=tp)
                return ins
            return S.op("pe", fn, reads=reads, writes=writes)

        def tt(eng, out, in0, in1, op, reads, writes):
            return S.op(eng, lambda e: e.tensor_tensor(out=out, in0=in0, in1=in1, op=op), reads=reads, writes=writes)

        def tsc(eng, out, in0, s1, s2, op0, op1, reads, writes):
            if op1 is None:
                return S.op(eng, lambda e: e.tensor_scalar(out=out, in0=in0, scalar1=s1, scalar2=None, op0=op0), reads=reads, writes=writes)
            return S.op(eng, lambda e: e.tensor_scalar(out=out, in0=in0, scalar1=s1, scalar2=s2, op0=op0, op1=op1), reads=reads, writes=writes)

        def stt(eng, out, in0, scalar, in1, op0, op1, reads, writes):
            return S.op(eng, lambda e: e.scalar_tensor_tensor(out=out, in0=in0, scalar=scalar, in1=in1, op0=op0, op1=op1), reads=reads, writes=writes)

        def cpy(eng, out, in_, reads, writes):
            return S.op(eng, lambda e: e.tensor_copy(out=out, in_=in_), reads=reads, writes=writes)

        def rcp(out, in_, reads, writes):
            return S.op("dve", lambda e: e.reciprocal(out=out, in_=in_), reads=reads, writes=writes)

        def act(out, in_, func, reads, writes, **kw):
            return S.op("act", lambda e: e.activation(out=out, in_=in_, func=func, **kw), reads=reads, writes=writes)

        dmac(ident[:], ident_d, [ident.r])
        dmac(cmask[:], cmask_d, [cmask.r])
        dmac(esel[:], esel_d, [esel.r])
        dmac(ones[:], ones_d, [ones.r])
        dmas(selb[:], selb_d, [selb.r])
        dmas(tril[:], tril_d, [tril.r])
        dmas(invf[:], invf_d, [invf.r])
        dmas(flag[:], flag_d, [flag.r])
        dmas(flag2[:], flag2_d, [flag2.r])
        S.op("dve", lambda e: e.tensor_scalar(out=eselF2[:], in0=esel[:], scalar1=flag2[:, 0:1], scalar2=None, op0=ALU.mult),
             reads=[esel.r, flag2.r], writes=[eselF2.r])
        S.op("dve", lambda e: e.tensor_scalar(out=eselF[:], in0=esel[:], scalar1=flag[:, 0:1], scalar2=None, op0=ALU.mult),
             reads=[esel.r, flag.r], writes=[eselF.r])

        def rope_tables(p0, n, cos_ap, sin_ap, cos_r, sin_r, posi):
            posf, ta, tb = tmpr.next(), tmpr.next(), tmpr.next()
            dmas(posi[:, 0:n], pos_d[0:1, p0:p0 + n].partition_broadcast(128), writes=[posi.r])
            S.op("dve", lambda e: e.tensor_copy(out=posf[:, 0:n], in_=posi[:, 0:n]), reads=[posi.r], writes=[posf.r])
            S.op("dve", lambda e: e.tensor_scalar(out=posf[:, 0:n], in0=posf[:, 0:n], scalar1=invf[:, 0:1], scalar2=None, op0=ALU.mult),
                 reads=[posf.r, invf.r], writes=[posf.r])
            S.op("dve", lambda e: e.tensor_scalar(out=ta[:, 0:n], in0=posf[:, 0:n], scalar1=1.0 / TWO_PI, scalar2=MAGIC, op0=ALU.mult, op1=ALU.add),
                 reads=[posf.r], writes=[ta.r])
            S.op("dve", lambda e: e.tensor_scalar(out=ta[:, 0:n], in0=ta[:, 0:n], scalar1=MAGIC, scalar2=None, op0=ALU.subtract),
                 reads=[ta.r], writes=[ta.r])
            S.op("dve", lambda e: e.scalar_tensor_tensor(out=posf[:, 0:n], in0=ta[:, 0:n], scalar=-C1, in1=posf[:, 0:n], op0=ALU.mult, op1=ALU.add),
                 reads=[ta.r, posf.r], writes=[posf.r])
            S.op("dve", lambda e: e.scalar_tensor_tensor(out=posf[:, 0:n], in0=ta[:, 0:n], scalar=-C2, in1=posf[:, 0:n], op0=ALU.mult, op1=ALU.add),
                 reads=[ta.r, posf.r], writes=[posf.r])
            S.op("dve", lambda e: e.tensor_scalar(out=posf[:, 0:n], in0=posf[:, 0:n], scalar1=-3.1415925, scalar2=3.1415925, op0=ALU.max, op1=ALU.min),
                 reads=[posf.r], writes=[posf.r])
            act(sin_ap, posf[:, 0:n], AF.Sin, [posf.r], [sin_r])
            S.op("dve", lambda e: e.scalar_tensor_tensor(out=tb[:, 0:n], in0=posf[:, 0:n], scalar=-1.0, in1=posf[:, 0:n], op0=ALU.mult, op1=ALU.max),
                 reads=[posf.r], writes=[tb.r])
            S.op("dve", lambda e: e.tensor_scalar(out=tb[:, 0:n], in0=tb[:, 0:n], scalar1=-1.0, scalar2=math.pi / 2, op0=ALU.mult, op1=ALU.add),
                 reads=[tb.r], writes=[tb.r])
            act(cos_ap, tb[:, 0:n], AF.Sin, [tb.r], [cos_r])

        xin_r = Res("xin")
        x1_r = Res("x1")
        x1h_r = Res("x1h")
        if n_layers == 1:
            passes = [(0, xT_d, xhT_d, T, flag, eselF, out_d, xin_r, xin_r, None)]
        else:
            passes = [
                (0, xhT_d, xh2T_d, 0, flag2, eselF2, x1h_d, xin_r, xin_r, x1h_r),
                (0, xT_d, xhT_d, T, flag, eselF, x1_d, xin_r, xin_r, x1_r),
                (1, x1_d, x1h_d, T, flag, eselF, out_d, x1_r, x1h_r, None),
            ]
        for pi, (l, x_src, xh_src, pos0, flag_t, eselF_t, out_ap, xsrc_r, xhsrc_r, out_r) in enumerate(passes):
            Wl = W[l]
            w_in = Wl["w_in"]
            with ExitStack() as pa:
                bring = Ring([psum(pa, "bankA%d_%d" % (i, pi), [128, 512]) for i in range(7)])
                bank_t = psum(pa, "bank_t_%d" % pi, [128, 1024], BF16)
                cos_o = sb(pa, "cos_o" + "_%d" % pi, [128, T], F32)
                sin_o = sb(pa, "sin_o" + "_%d" % pi, [128, T], F32)
                cos_h = sb(pa, "cos_h" + "_%d" % pi, [128, SP], F32)
                sin_h = sb(pa, "sin_h" + "_%d" % pi, [128, SP], F32)
                posi = sb(pa, "posi" + "_%d" % pi, [128, SP], I32)
                acc = sb(pa, "acc" + "_%d" % pi, [128, 2, T], F32)
                accd = sb(pa, "accd" + "_%d" % pi, [4, T], F32)
                KT = sb(pa, "KT" + "_%d" % pi, [128, 2, 2 * T], BF16)
                VT = sb(pa, "VT" + "_%d" % pi, [128, 2, 2 * T], BF16)
                QT = sb(pa, "QT" + "_%d" % pi, [128, 2, T], BF16)
                vbr = Ring([sb(pa, "vb%d_%d" % (i, pi), [128, 256], BF16) for i in range(6)])
                P_t = Ring([sb(pa, "P%d_%d" % (i, pi), [128, 1024], BF16) for i in range(3)])
                xcr = Ring([sb(pa, "xca%d_%d" % (i, pi), [128, NT, SP], BF16) for i in range(2)])
                for s in range(NS):
                    rope_tables(pos0 + T + s * SP, SP, cos_o[:, s * SP:(s + 1) * SP], sin_o[:, s * SP:(s + 1) * SP], cos_o.r, sin_o.r, posi)

                def chunk_list(dil_):
                    HL_ = 128 * dil_
                    ch_ = []
                    hs0_ = T - HL_
                    n_h_ = min(HL_, SP)
                    for a_ in range(hs0_, T, n_h_):
                        ch_.append((True, a_, n_h_, a_ - hs0_))
                    for s_ in range(NS):
                        ch_.append((False, s_ * SP, SP, HL_ + s_ * SP))
                    return ch_

                def issue_first(chunks_):
                    (is_h_, t0_, n_, c0_) = chunks_[0]
                    xc_ = xcr.next()
                    src_ = xh_src if is_h_ else x_src
                    dmac(xc_[:, :, 0:n_], src_[:, t0_:t0_ + n_].rearrange("(k p) t -> p k t", p=128), [xc_.r], reads=[xhsrc_r if is_h_ else xsrc_r])
                    return xc_
                phases = [(qd_, g_) for qd_ in range(2) for g_ in range(3)]
                pre_first = {0: issue_first(chunk_list(DILS[0]))}
                for qd in range(2):
                    if kstop <= 1:
                        break
                    S.op("pool", lambda e, acc=acc: e.memset(acc[:], 0.0), writes=[acc.r])
                    S.op("pool", lambda e, accd=accd: e.memset(accd[:], 0.0), writes=[accd.r])
                    for g, dil in enumerate(DILS):
                        HL = 128 * dil
                        base = ATT0 + g * 1536
                        wq_v, wq_r = wload(w_in, base + qd * 256, 256, D)
                        wk_v, wk_r = wload(w_in, base + 512 + qd * 256, 256, D)
                        wv_v, wv_r = wload(w_in, base + 1024 + qd * 256, 256, D)
                        chunks = []
                        hs0 = T - HL
                        n_h = min(HL, SP)
                        for a in range(hs0, T, n_h):
                            chunks.append((True, a, n_h, a - hs0))
                        for s in range(NS):
                            chunks.append((False, s * SP, SP, HL + s * SP))
                        loaded = {}

                        def issue_chunk(ci, chunks=chunks, loaded=loaded):
                            (is_h_, t0_, n_, c0_) = chunks[ci]
                            xc_ = xcr.next()
                            src_ = xh_src if is_h_ else x_src
                            dmac(xc_[:, :, 0:n_], src_[:, t0_:t0_ + n_].rearrange("(k p) t -> p k t", p=128), [xc_.r], reads=[xhsrc_r if is_h_ else xsrc_r])
                            loaded[ci] = xc_
                        ph_i = qd * 3 + g
                        loaded[0] = pre_first.pop(ph_i)
                        for ci, (is_h, t0, n, c0) in enumerate(chunks):
                            if ci + 1 < len(chunks):
                                issue_chunk(ci + 1)
                            xc = loaded[ci]
                            if is_h:
                                rope_tables(pos0 + t0, n, cos_h[:, 0:n], sin_h[:, 0:n], cos_h.r, sin_h.r, posi)
                                cs_ap, sn_ap, cs_r, sn_r = cos_h[:, 0:n], sin_h[:, 0:n], cos_h.r, sin_h.r
                            else:
                                cs_ap, sn_ap, cs_r, sn_r = cos_o[:, t0:t0 + n], sin_o[:, t0:t0 + n], cos_o.r, sin_o.r
                            todo = [(wk_v, wk_r, KT, c0)]
                            if not is_h:
                                todo.append((wq_v, wq_r, QT, t0))
                            for (wt, wr, dst, dc0) in todo:
                                pA = bring.next()
                                pB = bring.next()
                                mmgroup(pA[:, 0:n], [(wt[:, k, 0:128], xc[:, k, 0:n]) for k in range(NT)], [wr, xc.r], [pA.r])
                                mmgroup(pB[:, 0:n], [(wt[:, k, 128:256], xc[:, k, 0:n]) for k in range(NT)], [wr, xc.r], [pB.r])
                                t1, t2, t3, t4 = tmpr.next(), tmpr.next(), tmpr.next(), tmpr.next()
                                tt("dve", t1[:, 0:n], pA[:, 0:n], cs_ap, ALU.mult, [pA.r, cs_r], [t1.r])
                                tt("dve", t2[:, 0:n], pB[:, 0:n], sn_ap, ALU.mult, [pB.r, sn_r], [t2.r])
                                tt("pool", dst[:, 0, dc0:dc0 + n], t1[:, 0:n], t2[:, 0:n], ALU.subtract, [t1.r, t2.r], [dst.r])
                                tt("dve", t3[:, 0:n], pB[:, 0:n], cs_ap, ALU.mult, [pB.r, cs_r], [t3.r])
                                tt("dve", t4[:, 0:n], pA[:, 0:n], sn_ap, ALU.mult, [pA.r, sn_r], [t4.r])
                                tt("pool", dst[:, 1, dc0:dc0 + n], t3[:, 0:n], t4[:, 0:n], ALU.add, [t3.r, t4.r], [dst.r])
                            for vt in range(2):
                                pV = bring.next()
                                mmgroup(pV[:, 0:n], [(wv_v[:, k, vt * 128:(vt + 1) * 128], xc[:, k, 0:n]) for k in range(NT)], [wv_r, xc.r], [pV.r])
                                act(VT[:, vt, c0:c0 + n], pV[:, 0:n], AF.Copy, [pV.r], [VT.r])
                        if kstop <= 2:
                            break
                        if ph_i + 1 < len(phases):
                            pre_first[ph_i + 1] = issue_first(chunk_list(DILS[phases[ph_i + 1][1]]))
                        nb = T // (128 * dil)
                        Wd = HL + T

                        def kview(tl, lo, hi, ab, m, r, dil=dil, Wd=Wd):
                            return tl.h[lo:hi, ab, 0:Wd].rearrange("p (m i r) -> p m r i", i=128, r=dil)[:, m, r, :]

                        def qview(tl, lo, hi, ab, m, r, dil=dil):
                            return tl.h[lo:hi, ab, 0:T].rearrange("p (m i r) -> p m r i", i=128, r=dil)[:, m, r, :]

                        pending = [None]
                        for r in range(dil):
                            vprev = None
                            for m in range(nb + 1):
                                vb = vbr.next()

                                v0_ap = kview(VT, 0, 128, 0, m, r)
                                v1_ap = kview(VT, 0, 128, 1, m, r)

                                def tr2(e, v0_ap=v0_ap, v1_ap=v1_ap, o0=bank_t[:, 0:128], o1=bank_t[:, 128:256], idn=ident[:]):
                                    e.transpose(out=o0, in_=v0_ap, identity=idn)
                                    return e.transpose(out=o1, in_=v1_ap, identity=idn)
                                S.op("pe", tr2, reads=[VT.r, ident.r], writes=[bank_t.r])
                                if m == 0:
                                    act(vb[:], bank_t[:, 0:256], AF.Copy, [bank_t.r, flag_t.r], [vb.r], scale=flag_t[:, 0:1])
                                    vprev = vb
                                    continue
                                act(vb[:], bank_t[:, 0:256], AF.Copy, [bank_t.r], [vb.r])
                                n_q = m - 1
                                sbks = [bring.next() for _ in range(4)]
                                for hs in range(4):
                                    sbk = sbks[hs]
                                    lo, hi = 32 * hs, 32 * hs + 32
                                    for kt in range(2):
                                        o0 = kt * 128
                                        mmgroup(sbk[:, o0:o0 + 128],
                                                [(kview(KT, lo, hi, 0, n_q + kt, r), qview(QT, lo, hi, 0, n_q, r)),
                                                 (kview(KT, lo, hi, 1, n_q + kt, r), qview(QT, lo, hi, 1, n_q, r))],
                                                [KT.r, QT.r], [sbk.r], tp=(32 * hs, 0))
                                P = P_t.next()
                                for hs in range(4):
                                    act(P[:, hs * 256:(hs + 1) * 256], sbks[hs][:, 0:256], AF.Exp, [sbks[hs].r], [P.r], scale=0.125)
                                tt("dve", P[:], P[:], cmask[:], ALU.mult, [P.r, cmask.r], [P.r])
                                def pv_stage(vprev=vprev, vb=vb, P=P, m=m, n_q=n_q, r=r, dil=dil, acc=acc, accd=accd):
                                    nd = bring.next()
                                    vbs = (vprev, vb)
                                    for pr in range(2):
                                        for hh in range(2):
                                            hs = 2 * pr + hh
                                            mmgroup(nd[64 * hh:64 * hh + 64, pr * 128:(pr + 1) * 128],
                                                    [(vbs[kt][:, hs * 64:(hs + 1) * 64], P[:, hs * 256 + kt * 128: hs * 256 + kt * 128 + 128]) for kt in range(2)],
                                                    [vprev.r, vb.r, P.r], [nd.r], tp=(0, 64 * hh))
                                    es_prev = eselF_t if m == 1 else esel
                                    pairs = []
                                    for hs in range(4):
                                        pairs.append((es_prev[:, hs * 4:(hs + 1) * 4], P[:, hs * 256: hs * 256 + 128]))
                                        pairs.append((esel[:, hs * 4:(hs + 1) * 4], P[:, hs * 256 + 128: hs * 256 + 256]))
                                    mmgroup(nd[0:4, 256:384], pairs, [P.r, esel.r, eselF_t.r], [nd.r])
                                    accv = acc.h[:, :, :].rearrange("p a (m i r) -> p a m r i", i=128, r=dil)[:, :, n_q, r, :]
                                    tt("dve", accv, accv, nd[:, 0:256].rearrange("p (a i) -> p a i", a=2), ALU.add, [nd.r, acc.r], [acc.r])
                                    adv = accd.h[:, :].rearrange("p (m i r) -> p m r i", i=128, r=dil)[:, n_q, r, :]
                                    tt("dve", adv, adv, nd[0:4, 256:384], ALU.add, [nd.r, accd.r], [accd.r])
                                if pending[0] is not None:
                                    pending[0]()
                                pending[0] = pv_stage
                                vprev = vb
                        if pending[0] is not None:
                            pending[0]()
                            pending[0] = None
                    if kstop <= 3:
                        break
                    rcp(accd[:], accd[:], [accd.r], [accd.r])
                    if kstop <= 4:
                        break
                    for pr in range(2):
                        for s in range(NS):
                            bc = bring.next()
                            mmgroup(bc[:, :], [(selb[:, pr * 128:(pr + 1) * 128], accd[:, s * SP:(s + 1) * SP])], [selb.r, accd.r], [bc.r])
                            tt("dve", y_b[:, 2 * qd + pr, s * SP:(s + 1) * SP], acc[:, pr, s * SP:(s + 1) * SP], bc[:, :], ALU.mult,
                               [bc.r, acc.r], [y_b.r])

                S.barrier()
            if kstop <= 5:
                break
            with ExitStack() as pb:
                bring = Ring([psum(pb, "bankB%d_%d" % (i, pi), [128, 512]) for i in range(6)])
                stat_banks = [psum(pb, "bankS%d_%d" % (i, pi), [128, 512]) for i in range(2)]
                xs = sb(pb, "xs" + "_%d" % pi, [128, NT, SP], F32)
                xc = sb(pb, "xcb" + "_%d" % pi, [128, NT, SP], BF16)
                vfm = sb(pb, "vfm" + "_%d" % pi, [128, 4 * D], F32)
                u_sb = sb(pb, "u_sb" + "_%d" % pi, [128, NT, SP], BF16)
                y_a = sb(pb, "y_a" + "_%d" % pi, [128, NT, SP], BF16)
                y_c = sb(pb, "y_c" + "_%d" % pi, [128, NT, SP], BF16)
                vbf = [sb(pb, "vbf%d_%d" % (i, pi), [128, D], BF16) for i in range(4)]
                m_sb = u_sb
                h_sb = sb(pb, "h_sb" + "_%d" % pi, [128, NF, SP], BF16)
                zbuf = sb(pb, "zbuf" + "_%d" % pi, [128, SP + 2], F32)
                convw = sb(pb, "convw" + "_%d" % pi, [128, NT * 3], F32)
                glng = sb(pb, "glng" + "_%d" % pi, [128, D], F32)
                glnb = sb(pb, "glnb" + "_%d" % pi, [128, D], F32)
                bsb = sb(pb, "bsb" + "_%d" % pi, [128, D], F32)
                wsf = sb(pb, "wsf" + "_%d" % pi, [128, 8, 128], F32)
                wsm = sb(pb, "wsm" + "_%d" % pi, [128, 8, 128], BF16)
                ln1g = sb(pb, "ln1g" + "_%d" % pi, [128, NT], F32)
                ln1b = sb(pb, "ln1b" + "_%d" % pi, [128, NT], F32)
                ln2g = sb(pb, "ln2g" + "_%d" % pi, [128, NT], F32)
                ln2b = sb(pb, "ln2b" + "_%d" % pi, [128, NT], F32)
                st6 = sb(pb, "st6" + "_%d" % pi, [128, 2, 6], F32)
                mv = sb(pb, "mv" + "_%d" % pi, [128, 2], F32)
                rstd1 = sb(pb, "rstd1" + "_%d" % pi, [128, 1], F32)
                xh16 = sb(pb, "xh16" + "_%d" % pi, [128, NT, 16], BF16)
                mean_t = sb(pb, "mean_t" + "_%d" % pi, [128, SP], F32)
                rstd_t = sb(pb, "rstd_t" + "_%d" % pi, [128, SP], F32)

                dmas(convw[:], Wl["convw"], [convw.r])
                dmas(glng[:], Wl["glng"].partition_broadcast(128), [glng.r])
                dmas(glnb[:], Wl["glnb"].partition_broadcast(128), [glnb.r])
                dmas(bsb[:], Wl["bs"].partition_broadcast(128), [bsb.r])
                dmas(wsf[:], Wl["wsT"].rearrange("g j i -> j g i"), [wsf.r])
                dmas(ln1g[:], Wl["ln1g"], [ln1g.r])
                dmas(ln1b[:], Wl["ln1b"], [ln1b.r])
                dmas(ln2g[:], Wl["ln2g"], [ln2g.r])
                dmas(ln2b[:], Wl["ln2b"], [ln2b.r])
                tt("dve", wsm[:], wsf[:], tril[:].unsqueeze(1).to_broadcast([128, 8, 128]), ALU.mult, [wsf.r, tril.r], [wsm.r])
                dmac(xh16[:], xh_src[:, T - 16:T].rearrange("(k p) t -> p k t", p=128), [xh16.r], reads=[xhsrc_r])
                for jb in range(2):
                    wC, wCr = wload(w_in, CC + jb * 512, 512, D)
                    wH, wHr = wload(w_in, CH + jb * 512, 512, D)
                    for jj in range(4):
                        j = jb * 4 + jj
                        pC = bring.next()
                        pH = bring.next()
                        mmgroup(pC[:, 0:2], [(wC[:, k, jj * 128:(jj + 1) * 128], xh16[:, k, 14:16]) for k in range(NT)], [wCr, xh16.r], [pC.r])
                        mmgroup(pH[:, 0:2], [(wH[:, k, jj * 128:(jj + 1) * 128], xh16[:, k, 14:16]) for k in range(NT)], [wHr, xh16.r], [pH.r])
                        tz = tmpr.next()
                        act(tz[:, 0:2], pC[:, 0:2], AF.Copy, [pC.r, flag_t.r], [tz.r], scale=flag_t[:, 0:1])
                        tt("dve", zhist[:, j, :], tz[:, 0:2], pH[:, 0:2], ALU.mult, [tz.r, pH.r], [zhist.r])

                class LNStats:
                    def __init__(self):
                        self.rb, self.rsq = y_a, y_c
                        self.pm = stat_banks[0]
                        self.pq = stat_banks[1]
                        self.lag = None

                    def _mm(self, j):
                        rb, rsq, pm, pq = self.rb, self.rsq, self.pm, self.pq
                        S.op("pe", lambda e: e.matmul(pm[:, :], lhsT=ones[:], rhs=rb[:, j, :], start=(j == 0), stop=(j == NT - 1)),
                             reads=[ones.r, rb.r], writes=[pm.r])
                        S.op("pe", lambda e: e.matmul(pq[:, :], lhsT=ones[:], rhs=rsq[:, j, :], start=(j == 0), stop=(j == NT - 1)),
                             reads=[ones.r, rsq.r], writes=[pq.r])

                    def tile_done(self, j):
                        if self.lag is not None:
                            self._mm(self.lag)
                        act(self.rb[:, j, :], xs[:, j, :], AF.Identity, [xs.r], [self.rb.r])
                        tt("dve", self.rsq[:, j, :], xs[:, j, :], xs[:, j, :], ALU.mult, [xs.r], [self.rsq.r])
                        self.lag = j

                    def finish_stats(self):
                        if self.lag is not None:
                            self._mm(self.lag)
                            self.lag = None

                    def finish(self, g_t, b_t, write_xc=True):
                        self.finish_stats()
                        pm, pq = self.pm, self.pq
                        msq = tmpr.next()
                        tsc("dve", mean_t[:], pm[:, :], 1.0 / D, None, ALU.mult, None, [pm.r], [mean_t.r])
                        tt("dve", msq[:], mean_t[:], mean_t[:], ALU.mult, [mean_t.r], [msq.r])
                        stt("dve", msq[:], pq[:, :], 1.0 / D, msq[:], ALU.mult, ALU.subtract, [pq.r, msq.r], [msq.r])
                        tsc("dve", msq[:], msq[:], EPS, None, ALU.add, None, [msq.r], [msq.r])
                        act(msq[:], msq[:], AF.Sqrt, [msq.r], [msq.r])
                        rcp(rstd_t[:], msq[:], [msq.r], [rstd_t.r])
                        ts_ = []
                        for j in range(NT):
                            t = tmpr.next()
                            eng = "dve" if j % 2 == 0 else "pool"
                            tt(eng, t[:], xs[:, j, :], mean_t[:], ALU.subtract, [xs.r, mean_t.r], [t.r])
                            tt(eng, t[:], t[:], rstd_t[:], ALU.mult, [t.r, rstd_t.r], [t.r])
                            if write_xc:
                                act(xc[:, j, :], t[:], AF.Identity, [t.r, g_t.r, b_t.r], [xc.r], scale=g_t[:, j:j + 1], bias=b_t[:, j:j + 1])
                            ts_.append(t)
                        for j in range(NT):
                            t = ts_[j]
                            act(xs[:, j, :], t[:], AF.Identity, [t.r, g_t.r, b_t.r], [xs.r], scale=g_t[:, j:j + 1], bias=b_t[:, j:j + 1])

                mf = lambda j: vfm[:, j * SP:(j + 1) * SP]
                vf = lambda t_: vfm[:, t_ * D:(t_ + 1) * D]

                deferred_tail = [None]
                for s in range(NS):
                    c0 = s * SP
                    vfm3 = vfm.h[:, :].rearrange("p (k t) -> p k t", k=NT)
                    if s == 0:
                        dmac(xs[:, :, :], x_src[:, c0:c0 + SP].rearrange("(k p) t -> p k t", p=128), [xs.r], reads=[xsrc_r])
                        act(xc[:, :, :], xs[:, :, :], AF.Copy, [xs.r], [xc.r])
                    for jb in range(2):
                        wB, wBr = wload(w_in, CB + jb * 512, 512, D)
                        wC, wCr = wload(w_in, CC + jb * 512, 512, D)
                        wH, wHr = wload(w_in, CH + jb * 512, 512, D)
                        for jj in range(4):
                            j = jb * 4 + jj
                            pB_, pC, pH = bring.next(), bring.next(), bring.next()
                            mmgroup(pC[:, :], [(wC[:, k, jj * 128:(jj + 1) * 128], xc[:, k, :]) for k in range(NT)], [wCr, xc.r], [pC.r])
                            mmgroup(pH[:, :], [(wH[:, k, jj * 128:(jj + 1) * 128], xc[:, k, :]) for k in range(NT)], [wHr, xc.r], [pH.r])
                            mmgroup(pB_[:, :], [(wB[:, k, jj * 128:(jj + 1) * 128], xc[:, k, :]) for k in range(NT)], [wBr, xc.r], [pB_.r])
                            tc_ = tmpr.next()
                            ta_ = tmpr.next()
                            act(tc_[:], pC[:, :], AF.Copy, [pC.r], [tc_.r])
                            act(zbuf[:, 0:2], zhist[:, j, :], AF.Copy, [zhist.r], [zbuf.r])
                            tt("dve", zbuf[:, 2:SP + 2], tc_[:], pH[:, :], ALU.mult, [tc_.r, pH.r], [zbuf.r])
                            act(zhist[:, j, :], zbuf[:, SP:SP + 2], AF.Copy, [zbuf.r], [zhist.r])
                            tsc("dve", ta_[:], zbuf[:, 0:SP], convw[:, 3 * j:3 * j + 1], None, ALU.mult, None, [zbuf.r, convw.r], [ta_.r])
                            stt("dve", ta_[:], zbuf[:, 1:SP + 1], convw[:, 3 * j + 1:3 * j + 2], ta_[:], ALU.mult, ALU.add, [zbuf.r, convw.r, ta_.r], [ta_.r])
                            stt("dve", ta_[:], zbuf[:, 2:SP + 2], convw[:, 3 * j + 2:3 * j + 3], ta_[:], ALU.mult, ALU.add, [zbuf.r, convw.r, ta_.r], [ta_.r])
                            tt("dve", y_a[:, j, :], ta_[:], pB_[:, :], ALU.mult, [ta_.r, pB_.r], [y_a.r])
                    if s > 0:
                        deferred_tail[0]()
                        deferred_tail[0] = None
                        cpy("pool", xs[:, :, :], vfm3, [vfm.r], [xs.r])
                    for jb in range(2):
                        wU, wUr = wload(w_in, UO + jb * 512, 512, D)
                        for jj in range(4):
                            j = jb * 4 + jj
                            pU = bring.next()
                            mmgroup(pU[:, :], [(wU[:, k, jj * 128:(jj + 1) * 128], xc[:, k, :]) for k in range(NT)], [wUr, xc.r], [pU.r])
                            act(u_sb[:, j, :], pU[:, :], AF.Gelu, [pU.r], [u_sb.r])
                    for half in range(2):
                        wV, wVr = wload(w_in, VO + half * 512, 512, D)
                        for t_ in range(4):
                            pV = bring.next()
                            mmgroup(pV[:, :], [(xc[:, k, t_ * 128:(t_ + 1) * 128], wV[:, k, :]) for k in range(NT)], [wVr, xc.r], [pV.r])
                            act(vfm[:, t_ * D + half * 512: t_ * D + (half + 1) * 512], pV[:, :], AF.Gelu, [pV.r], [vfm.r])
                    for t_ in range(4):
                        v = vf(t_)
                        S.op("dve", lambda e, o_=st6[:, 0, :], i_=v[:, 0:512]: e.bn_stats(out=o_, in_=i_), reads=[vfm.r], writes=[st6.r])
                        S.op("dve", lambda e, o_=st6[:, 1, :], i_=v[:, 512:1024]: e.bn_stats(out=o_, in_=i_), reads=[vfm.r, st6.r], writes=[st6.r])
                        S.op("dve", lambda e, o_=mv[:], i_=st6[:].rearrange("p a b -> p (a b)"): e.bn_aggr(out=o_, in_=i_), reads=[st6.r], writes=[mv.r])
                        tsc("dve", rstd1[:], mv[:, 1:2], EPS, None, ALU.add, None, [mv.r], [rstd1.r])
                        act(rstd1[:], rstd1[:], AF.Sqrt, [rstd1.r], [rstd1.r])
                        rcp(rstd1[:], rstd1[:], [rstd1.r], [rstd1.r])
                        tsc("dve", v, v, mv[:, 0:1], rstd1[:, 0:1], ALU.subtract, ALU.mult, [vfm.r, mv.r, rstd1.r], [vfm.r])
                        tt("dve", v, v, glng[:], ALU.mult, [vfm.r, glng.r], [vfm.r])
                        tt("dve", vbf[t_][:], v, glnb[:], ALU.add, [vfm.r, glnb.r], [vbf[t_].r])
                    def spatial_stage():
                        for gg in range(8):
                            pS = bring.next()
                            for t_ in range(4):
                                mmgroup(pS[:, t_ * 128:(t_ + 1) * 128], [(vbf[t_][:, gg * 128:(gg + 1) * 128], wsm[:, gg, :])], [vbf[t_].r, wsm.r], [pS.r])
                            tq = tmpr.next()
                            tt("dve", tq[:].rearrange("p (c i) -> p c i", c=4), pS[:, :].rearrange("p (c i) -> p c i", c=4),
                               bsb[:, gg * 128:(gg + 1) * 128].unsqueeze(1).to_broadcast([128, 4, 128]), ALU.add, [pS.r, bsb.r], [tq.r])
                            tt("dve", y_c[:, gg, :], tq[:], u_sb[:, gg, :], ALU.mult, [tq.r, u_sb.r], [y_c.r])
                    for bi, (gcol, p_ap, K, y_t) in enumerate(((GA, Wl["p_a"], D, y_a), (GB, Wl["p_b"], 512, y_b), (GC, Wl["p_c"], D, y_c))):
                        kt = K // 128
                        if bi == 2:
                            spatial_stage()
                        for jb in range(2):
                            wG_, wGr_ = wload(w_in, gcol + jb * 512, 512, D)
                            wP_, wPr_ = wload(p_ap, jb * 512, 512, K)
                            for jj in range(4):
                                j = jb * 4 + jj
                                pg, py = bring.next(), bring.next()
                                mmgroup(pg[:, :], [(wG_[:, k, jj * 128:(jj + 1) * 128], xc[:, k, :]) for k in range(NT)], [wGr_, xc.r], [pg.r])
                                if bi == 1:
                                    prs = [(wP_[:, k, jj * 128:(jj + 1) * 128], y_b[:, k, c0:c0 + SP]) for k in range(kt)]
                                else:
                                    prs = [(wP_[:, k, jj * 128:(jj + 1) * 128], y_t[:, k, :]) for k in range(kt)]
                                mmgroup(py[:, :], prs, [wPr_, y_t.r], [py.r])
                                sg = tmpr.next()
                                act(sg[:], pg[:, :], AF.Sigmoid, [pg.r], [sg.r])
                                if bi == 0:
                                    tt("dve", mf(j), sg[:], py[:, :], ALU.mult, [sg.r, py.r], [vfm.r])
                                elif bi == 1:
                                    tt("dve", sg[:], sg[:], py[:, :], ALU.mult, [sg.r, py.r], [sg.r])
                                    tt("dve", mf(j), mf(j), sg[:], ALU.add, [sg.r, vfm.r], [vfm.r])
                                else:
                                    tt("dve", sg[:], sg[:], py[:, :], ALU.mult, [sg.r, py.r], [sg.r])
                                    tt("dve", m_sb[:, j, :], mf(j), sg[:], ALU.add, [sg.r, vfm.r], [m_sb.r])
                    if s + 1 < NS:
                        dmac(vfm3, x_src[:, c0 + SP:c0 + 2 * SP].rearrange("(k p) t -> p k t", p=128), [vfm.r], reads=[xsrc_r])
                    lns = LNStats()
                    for jb in range(2):
                        wO, wOr = wload(Wl["w_o"], jb * 512, 512, D)
                        for jj in range(4):
                            j = jb * 4 + jj
                            po = bring.next()
                            mmgroup(po[:, :], [(wO[:, k, jj * 128:(jj + 1) * 128], m_sb[:, k, :]) for k in range(NT)], [wOr, m_sb.r], [po.r])
                            stt("dve", xs[:, j, :], xs[:, j, :], ALPHA, po[:, :], ALU.mult, ALU.add, [po.r, xs.r], [xs.r])
                            lns.tile_done(j)
                    lns.finish(ln1g, ln1b)
                    for fb in range(NF // 2):
                        wG_, wGr_ = wload(Wl["w_gate"], fb * 256, 256, D)
                        wU_, wUr_ = wload(Wl["w_up"], fb * 256, 256, D)
                        for ff in range(2):
                            f = fb * 2 + ff
                            pg, pu = bring.next(), bring.next()
                            mmgroup(pg[:, :], [(wG_[:, k, ff * 128:(ff + 1) * 128], xc[:, k, :]) for k in range(NT)], [wGr_, xc.r], [pg.r])
                            mmgroup(pu[:, :], [(wU_[:, k, ff * 128:(ff + 1) * 128], xc[:, k, :]) for k in range(NT)], [wUr_, xc.r], [pu.r])
                            sg = tmpr.next()
                            act(sg[:], pg[:, :], AF.Silu, [pg.r], [sg.r])
                            tt("dve", h_sb[:, f, :], sg[:], pu[:, :], ALU.mult, [sg.r, pu.r], [h_sb.r])
                    if s + 1 < NS:
                        act(xc[:, :, :], vfm3, AF.Identity, [vfm.r], [xc.r])
                    lns = LNStats()
                    for j in range(NT):
                        wD, wDr = wload(Wl["w_down"], j * 128, 128, DFF)
                        pd = bring.next()
                        mmgroup(pd[:, :], [(wD[:, k, :], h_sb[:, k, :]) for k in range(NF)], [wDr, h_sb.r], [pd.r])
                        stt("dve", xs[:, j, :], xs[:, j, :], ALPHA, pd[:, :], ALU.mult, ALU.add, [pd.r, xs.r], [xs.r])
                        lns.tile_done(j)
                    lns.finish_stats()

                    def ln2_tail(lns=lns, c0=c0):
                        lns.finish(ln2g, ln2b, write_xc=False)
                        dmac(out_ap[:, c0:c0 + SP].rearrange("(k p) t -> p k t", p=128), xs[:, :, :], reads=[xs.r], writes=([out_r] if out_r is not None else []))
                    if s + 1 < NS:
                        deferred_tail[0] = ln2_tail
                    else:
                        ln2_tail()
                S.barrier()
        if plan is None:
            return {"seq": dry_seq, "last_use": dict(S.last_use)}
        S.finish("sp")
        S.emit(top)
    return nc


def _consts():
    ident = np.eye(128, dtype=np.float32)
    k = np.arange(128)[:, None]
    q = np.arange(128)[None, :]
    prev = (k >= q).astype(np.float32)
    cur = (k <= q).astype(np.float32)
    cm = np.concatenate([prev, cur], axis=1)
    cmask = np.tile(cm, (1, 4))
    esel = np.zeros((128, 16), np.float32)
    for hs in range(4):
        esel[:, hs * 4 + hs] = 1.0
    selb = np.zeros((4, 256), np.float32)
    for pr in range(2):
        for col in range(128):
            selb[2 * pr + col // 64, pr * 128 + col] = 1.0
    ones = np.ones((128, 128), np.float32)
    tril = cur.copy()
    half = 32
    inv_freq = (np.float32(10000.0) ** (-np.arange(half, dtype=np.float32) / np.float32(half))).astype(np.float32)
    invf = np.tile(inv_freq, 4).reshape(128, 1).astype(np.float32)
    return dict(ident=ident, cmask=cmask, esel=esel, selb=selb, ones=ones, tril=tril, invf=invf)


def _qk_perm():
    idx = []
    for qd in range(2):
        for ab in range(2):
            for hs in range(4):
                h = 4 * qd + hs
                idx.extend(range(h * 64 + ab * 32, h * 64 + ab * 32 + 32))
    return np.array(idx)


def _layer_weights(l, w_in, conv_w, gmlp_ln_g, gmlp_ln_b, w_s, b_s, p_a, p_b, p_c, w_o, ln1_g, ln1_b,
                   w_gate, w_up, w_down, ln2_g, ln2_b, suffix):
    perm = _qk_perm()
    wi = np.array(w_in[l], dtype=np.float32, copy=True)
    for g in range(3):
        base = ATT0 + g * 1536
        wi[:, base:base + 512] = w_in[l][:, base + perm]
        wi[:, base + 512:base + 1024] = w_in[l][:, base + 512 + perm]

    def pj(v):
        return np.ascontiguousarray(np.asarray(v, np.float32).reshape(NT, 128).T)
    cw = np.asarray(conv_w[l], np.float32)
    convw = np.ascontiguousarray(cw.reshape(3, NT, 128).transpose(2, 1, 0).reshape(128, NT * 3))
    d = {
        "w_in": wi,
        "convw": convw,
        "glng": np.asarray(gmlp_ln_g[l], np.float32).reshape(1, D),
        "glnb": np.asarray(gmlp_ln_b[l], np.float32).reshape(1, D),
        "wsT": np.ascontiguousarray(np.asarray(w_s[l], np.float32).transpose(0, 2, 1)),
        "bs": np.asarray(b_s[l], np.float32).reshape(1, D),
        "p_a": np.asarray(p_a[l], np.float32),
        "p_b": np.asarray(p_b[l], np.float32),
        "p_c": np.asarray(p_c[l], np.float32),
        "w_o": np.asarray(w_o[l], np.float32),
        "ln1g": pj(ln1_g[l]), "ln1b": pj(ln1_b[l]),
        "w_gate": np.asarray(w_gate[l], np.float32),
        "w_up": np.asarray(w_up[l], np.float32),
        "w_down": np.asarray(w_down[l], np.float32),
        "ln2g": pj(ln2_g[l]), "ln2b": pj(ln2_b[l]),
    }
    return {k + suffix: np.ascontiguousarray(v) for k, v in d.items()}


_NC_CACHE = {}


def kernel(x, positions, w_in, conv_w, gmlp_ln_g, gmlp_ln_b, w_s, b_s, p_a, p_b, p_c, w_o,
           ln1_g, ln1_b, w_gate, w_up, w_down, ln2_g, ln2_b):
    x = np.asarray(x, np.float32)
    positions = np.asarray(positions, np.int32)
    B, Sq, _ = x.shape
    consts = _consts()
    if 2 not in _NC_CACHE:
        plan = build_program(2)
        _NC_CACHE[2] = build_program(2, plan=plan)
    nc = _NC_CACHE[2]
    lw = {}
    for l in range(DEPTH):
        lw.update(_layer_weights(l, w_in, conv_w, gmlp_ln_g, gmlp_ln_b, w_s, b_s, p_a, p_b, p_c, w_o, ln1_g, ln1_b,
                                 w_gate, w_up, w_down, ln2_g, ln2_b, str(l)))
    zx = np.zeros((D, T), np.float32)
    zp = np.zeros((T,), np.int32)

    def xt(b, q):
        return np.ascontiguousarray(x[b, q * T:(q + 1) * T, :].T) if q >= 0 else zx

    def pp(b, q):
        return positions[b, q * T:(q + 1) * T] if q >= 0 else zp

    in_maps = []
    for c in range(8):
        b, qtr = c // 4, c % 4
        pos = np.concatenate([pp(b, qtr - 2), pp(b, qtr - 1), pp(b, qtr)]).reshape(1, 3 * T).astype(np.int32)
        m = {"xT": xt(b, qtr), "xhT": xt(b, qtr - 1), "xh2T": xt(b, qtr - 2), "pos": pos,
             "flag": np.full((128, 1), 1.0 if qtr >= 1 else 0.0, np.float32),
             "flag2": np.full((128, 1), 1.0 if qtr >= 2 else 0.0, np.float32)}
        m.update(consts)
        m.update(lw)
        in_maps.append(m)
    res = run_bass_kernel_spmd(nc, in_maps, core_ids=list(range(8)))
    out = np.empty((B, Sq, D), np.float32)
    for c in range(8):
        out[c // 4, (c % 4) * T:(c % 4 + 1) * T, :] = np.asarray(res.results[c]["out"]).T
    return out
```
